# Optimizing a Trainium2 kernel written in Bass

```python
import jax, jax.numpy as jnp
from jax import lax
import numpy as np

D_MODEL = 1024
BATCH = 16
SEQ = 4096
DEPTH = 2
DEC_BATCH = 16
DEC_SEQ = 64
PAST_LEN = 2048

CHUNK = 64
ATT_QBLOCK = 128
N_A_LAYERS = (DEPTH + 1) // 2
N_C_LAYERS = DEPTH // 2
ML_HEADS = 4
ML_DH = D_MODEL // (2 * ML_HEADS)
ML_W = ML_HEADS * ML_DH
RET_HEADS = 4
RET_DK = D_MODEL // (4 * RET_HEADS)
RET_DV = D_MODEL // (2 * RET_HEADS)
RET_KW = RET_HEADS * RET_DK
RET_VW = RET_HEADS * RET_DV
SPLIT_A = (ML_W, ML_W, ML_W, ML_W, 2 * ML_HEADS, RET_KW, RET_KW, RET_VW, RET_VW)
PROJ_A = 4 * ML_W + 2 * ML_HEADS + 2 * RET_KW + 2 * RET_VW
MLA_HEADS = 8
QK_NOPE = 128
QK_ROPE = 64
V_HEAD = 128
Q_LORA = 512
KV_LORA = 256
PROJ_C = Q_LORA + KV_LORA + QK_ROPE
MLA_SCALE = (QK_NOPE + QK_ROPE) ** -0.5
D_FF = -(-8 * D_MODEL // (3 * 256)) * 256
ALPHA = (2 * DEPTH) ** 0.25
BETA = (8 * DEPTH) ** -0.25
ROPE_BASE = 10000.0
LN_EPS = 1e-5
RMS_EPS = 1e-6

kernel_name = 'hybrid_mlstm_retention_mla_stream_step'


def _split(a, sizes):
    idx = np.cumsum(sizes)[:-1].tolist()
    return jnp.split(a, idx, axis=-1)


def layer_norm(x, g, b):
    xf = x.astype(jnp.float32)
    mu = jnp.mean(xf, axis=-1, keepdims=True)
    var = jnp.mean(jnp.square(xf - mu), axis=-1, keepdims=True)
    return ((xf - mu) * lax.rsqrt(var + LN_EPS) * g.astype(jnp.float32) + b.astype(jnp.float32)).astype(x.dtype)


def rms_norm(x, g):
    xf = x.astype(jnp.float32)
    y = xf * lax.rsqrt(jnp.mean(jnp.square(xf), axis=-1, keepdims=True) + RMS_EPS)
    return (y * g.astype(jnp.float32)).astype(x.dtype)


def head_norm(h, g):
    mu = jnp.mean(h, axis=-1, keepdims=True)
    var = jnp.mean(jnp.square(h - mu), axis=-1, keepdims=True)
    return (h - mu) * lax.rsqrt(var + LN_EPS) * g.astype(jnp.float32).reshape(h.shape[-2:])


def rope(x, pos):
    half = x.shape[-1] // 2
    inv = ROPE_BASE ** (-jnp.arange(half, dtype=jnp.float32) / half)
    ang = pos.astype(jnp.float32)[:, None] * inv[None, :]
    cos = jnp.cos(ang)[None, :, None, :]
    sin = jnp.sin(ang)[None, :, None, :]
    xf = x.astype(jnp.float32)
    x1, x2 = xf[..., :half], xf[..., half:]
    return jnp.concatenate([x1 * cos - x2 * sin, x2 * cos + x1 * sin], axis=-1).astype(x.dtype)


def _to_chunks(t, L):
    B, T, H = t.shape[:3]
    t = t.reshape((B, T // L, L, H) + t.shape[3:])
    return jnp.moveaxis(t, (1, 3), (0, 2))


def _from_chunks(t):
    t = jnp.moveaxis(t, (0, 2), (1, 3))
    B, n, L, H, d = t.shape
    return t.reshape(B, n * L, H, d)


def mlstm_chunkwise(q, k, v, ig, lf, C0, n0, m0):
    T = q.shape[1]
    L = min(CHUNK, T)
    causal = jnp.tril(jnp.ones((L, L), dtype=bool))

    def step(carry, inp):
        C, n, m = carry
        qc, kc, vc, igc, lfc = inp
        b = jnp.cumsum(lfc, axis=-1)
        d_mat = jnp.where(causal, b[..., :, None] - b[..., None, :] + igc[..., None, :], -jnp.inf)
        m_inter = b + m[..., None]
        m_t = jnp.maximum(m_inter, jnp.max(d_mat, axis=-1))
        w_inter = jnp.exp(m_inter - m_t)
        s = jnp.einsum('bhtd,bhsd->bhts', qc, kc) * jnp.exp(d_mat - m_t[..., None])
        num = w_inter[..., None] * jnp.einsum('bhvd,bhtd->bhtv', C, qc) + jnp.einsum('bhts,bhsv->bhtv', s, vc)
        den = w_inter * jnp.einsum('bhd,bhtd->bht', n, qc) + jnp.sum(s, axis=-1)
        h = num / jnp.maximum(jnp.abs(den), jnp.exp(-m_t))[..., None]
        m_new = m_t[..., -1]
        w_k = jnp.exp(b[..., -1:] - b + igc - m_new[..., None])
        decay = jnp.exp(b[..., -1] + m - m_new)
        C_new = decay[..., None, None] * C + jnp.einsum('bhsv,bhsd->bhvd', vc * w_k[..., None], kc)
        n_new = decay[..., None] * n + jnp.einsum('bhs,bhsd->bhd', w_k, kc)
        return (C_new, n_new, m_new), h

    xs = tuple(_to_chunks(t, L) for t in (q, k, v, ig, lf))
    (C, n, m), h = lax.scan(step, (C0, n0, m0), xs)
    return _from_chunks(h), C, n, m


def retention_chunkwise(q, k, v, S0):
    T = q.shape[1]
    L = min(CHUNK, T)
    log_g = jnp.log(1.0 - 2.0 ** (-5.0 - jnp.arange(RET_HEADS, dtype=jnp.float32)))
    idx = jnp.arange(L, dtype=jnp.float32)
    rel = idx[:, None] - idx[None, :]
    intra = jnp.where(rel >= 0, jnp.exp(log_g[:, None, None] * jnp.maximum(rel, 0.0)), 0.0)
    q_dec = jnp.exp(log_g[:, None] * (idx + 1.0))
    k_dec = jnp.exp(log_g[:, None] * (L - 1.0 - idx))
    s_dec = jnp.exp(log_g * L)

    def step(S, inp):
        qc, kc, vc = inp
        a = jnp.einsum('bhtd,bhsd->bhts', qc, kc) * intra
        o = jnp.einsum('bhts,bhsv->bhtv', a, vc) + q_dec[..., None] * jnp.einsum('bhtd,bhdv->bhtv', qc, S)
        S_new = s_dec[:, None, None] * S + jnp.einsum('bhsd,bhsv->bhdv', kc * k_dec[..., None], vc)
        return S_new, o

    xs = tuple(_to_chunks(t, L) for t in (q, k, v))
    S, o = lax.scan(step, S0, xs)
    return _from_chunks(o), S


def mixer_ab(x, pos, C0, n0, m0, S0, w_in, b_if, g_ml, g_ret, w_out):
    B, T, _ = x.shape
    f32 = jnp.float32
    mq, mk, mv, mo, mif, rq, rk, rv, rg = _split(x @ w_in, SPLIT_A)
    heads = lambda t, nh: t.astype(f32).reshape(B, T, nh, -1)
    gates = mif.astype(f32) + b_if.astype(f32)
    ig = gates[..., :ML_HEADS]
    lf = jax.nn.log_sigmoid(gates[..., ML_HEADS:])
    h, C, n, m = mlstm_chunkwise(heads(mq, ML_HEADS), heads(mk, ML_HEADS) * ML_DH ** -0.5,
                                 heads(mv, ML_HEADS), ig, lf,
                                 C0.astype(f32), n0.astype(f32), m0.astype(f32))
    h = head_norm(h, g_ml) * jax.nn.sigmoid(heads(mo, ML_HEADS))
    q_r = rope(heads(rq, RET_HEADS), pos)
    k_r = rope(heads(rk, RET_HEADS), pos) * RET_DK ** -0.5
    o, S = retention_chunkwise(q_r, k_r, heads(rv, RET_HEADS), S0.astype(f32))
    o = head_norm(o, g_ret) * jax.nn.silu(heads(rg, RET_HEADS))
    mixed = jnp.concatenate([h.reshape(B, T, ML_W), o.reshape(B, T, RET_VW)], axis=-1).astype(x.dtype)
    return mixed @ w_out, C, n, m, S


def mixer_c(x, pos, ckv_past, krope_past, w_in, g_q, g_kv, w_uq, w_ukv, w_out):
    B, T, _ = x.shape
    cq, ckv, kr = _split(x @ w_in, (Q_LORA, KV_LORA, QK_ROPE))
    cq = rms_norm(cq, g_q)
    ckv = rms_norm(ckv, g_kv)
    q = (cq @ w_uq).reshape(B, T, MLA_HEADS, QK_NOPE + QK_ROPE)
    q_nope = q[..., :QK_NOPE]
    q_rope = rope(q[..., QK_NOPE:], pos)
    k_rope = rope(kr[:, :, None, :], pos)[:, :, 0, :]
    if ckv_past is None:
        ckv_all, kr_all, k_pos = ckv, k_rope, pos
    else:
        P = ckv_past.shape[1]
        ckv_all = jnp.concatenate([ckv_past.astype(ckv.dtype), ckv], axis=1)
        kr_all = jnp.concatenate([krope_past.astype(k_rope.dtype), k_rope], axis=1)
        k_pos = jnp.arange(P + T, dtype=jnp.int32)
    S = ckv_all.shape[1]
    kv = (ckv_all @ w_ukv).reshape(B, S, MLA_HEADS, QK_NOPE + V_HEAD)
    k_nope, v = kv[..., :QK_NOPE], kv[..., QK_NOPE:]
    k_chunk = k_pos // CHUNK
    qb = min(ATT_QBLOCK, T)
    nb = T // qb

    def block(args):
        qn, qr, qp = args
        s = jnp.einsum('bqhd,bkhd->bhqk', qn, k_nope) + jnp.einsum('bqhr,bkr->bhqk', qr, kr_all)
        s = s.astype(jnp.float32) * MLA_SCALE
        vis = k_chunk[None, :] <= (qp // CHUNK)[:, None]
        p = jax.nn.softmax(jnp.where(vis, s, -jnp.inf), axis=-1).astype(v.dtype)
        return jnp.einsum('bhqk,bkhv->bqhv', p, v)

    to_blocks = lambda t: jnp.swapaxes(t.reshape((B, nb, qb) + t.shape[2:]), 0, 1)
    o = lax.map(block, (to_blocks(q_nope), to_blocks(q_rope), pos.reshape(nb, qb)))
    o = jnp.swapaxes(o, 0, 1).reshape(B, T, MLA_HEADS * V_HEAD)
    return o @ w_out, ckv, k_rope


def swiglu(x, w_gu, w_down):
    g, u = jnp.split(x @ w_gu, 2, axis=-1)
    return (jax.nn.silu(g) * u) @ w_down


def _nrm(key, shape, scale):
    return jax.random.normal(key, shape, jnp.float32) * scale


def setup_inputs(seed: int = 0) -> dict:
    key = jax.random.key(seed)
    ks = jax.random.split(key, 32)
    NA, NC = N_A_LAYERS, N_C_LAYERS
    d = {}
    d['x_prompt'] = _nrm(ks[0], (BATCH, SEQ, D_MODEL), 1.0)
    d['x_sample'] = _nrm(ks[1], (DEC_BATCH, DEC_SEQ, D_MODEL), 1.0)
    d['state_mlstm_C'] = _nrm(ks[2], (NA, DEC_BATCH, ML_HEADS, ML_DH, ML_DH), ML_DH ** -0.5)
    d['state_mlstm_n'] = _nrm(ks[3], (NA, DEC_BATCH, ML_HEADS, ML_DH), ML_DH ** -0.5)
    d['state_mlstm_m'] = _nrm(ks[4], (NA, DEC_BATCH, ML_HEADS), 1.0)
    d['state_ret_S'] = _nrm(ks[5], (NA, DEC_BATCH, RET_HEADS, RET_DK, RET_DV), 1.0)
    d['cache_ckv'] = _nrm(ks[6], (NC, DEC_BATCH, PAST_LEN, KV_LORA), 1.0)
    d['cache_krope'] = _nrm(ks[7], (NC, DEC_BATCH, PAST_LEN, QK_ROPE), 1.0)
    d['w_in_a'] = _nrm(ks[8], (NA, D_MODEL, PROJ_A), D_MODEL ** -0.5)
    d['b_if_a'] = jnp.concatenate([
        _nrm(ks[9], (NA, ML_HEADS), 0.1),
        jnp.linspace(3.0, 6.0, ML_HEADS, dtype=jnp.float32)[None, :] + _nrm(ks[10], (NA, ML_HEADS), 0.1)], axis=-1)
    d['g_ml'] = 1.0 + _nrm(ks[11], (NA, ML_W), 0.02)
    d['g_ret'] = 1.0 + _nrm(ks[12], (NA, RET_VW), 0.02)
    d['w_out_a'] = _nrm(ks[13], (NA, ML_W + RET_VW, D_MODEL), BETA * (ML_W + RET_VW) ** -0.5)
    d['w_in_c'] = _nrm(ks[14], (NC, D_MODEL, PROJ_C), D_MODEL ** -0.5)
    d['g_q'] = 1.0 + _nrm(ks[15], (NC, Q_LORA), 0.02)
    d['g_kv'] = 1.0 + _nrm(ks[16], (NC, KV_LORA), 0.02)
    d['w_uq'] = _nrm(ks[17], (NC, Q_LORA, MLA_HEADS * (QK_NOPE + QK_ROPE)), Q_LORA ** -0.5)
    d['w_ukv'] = _nrm(ks[18], (NC, KV_LORA, MLA_HEADS * (QK_NOPE + V_HEAD)), KV_LORA ** -0.5)
    d['w_out_c'] = _nrm(ks[19], (NC, MLA_HEADS * V_HEAD, D_MODEL), BETA * (MLA_HEADS * V_HEAD) ** -0.5)
    d['ln_mix_g'] = 1.0 + _nrm(ks[20], (DEPTH, D_MODEL), 0.02)
    d['ln_mix_b'] = _nrm(ks[21], (DEPTH, D_MODEL), 0.02)
    d['ln_ffn_g'] = 1.0 + _nrm(ks[22], (DEPTH, D_MODEL), 0.02)
    d['ln_ffn_b'] = _nrm(ks[23], (DEPTH, D_MODEL), 0.02)
    d['w_gu'] = _nrm(ks[24], (DEPTH, D_MODEL, 2 * D_FF), D_MODEL ** -0.5)
    d['w_down'] = _nrm(ks[25], (DEPTH, D_FF, D_MODEL), BETA * D_FF ** -0.5)
    return d


def reference(x_prompt, x_sample, state_mlstm_C, state_mlstm_n, state_mlstm_m, state_ret_S,
              cache_ckv, cache_krope, w_in_a, b_if_a, g_ml, g_ret, w_out_a,
              w_in_c, g_q, g_kv, w_uq, w_ukv, w_out_c,
              ln_mix_g, ln_mix_b, ln_ffn_g, ln_ffn_b, w_gu, w_down):
    f32 = jnp.float32
    Bp, Tp, _ = x_prompt.shape
    Bs, Ts, _ = x_sample.shape
    past = cache_ckv.shape[2]
    pos_p = jnp.arange(Tp, dtype=jnp.int32)
    pos_s = past + jnp.arange(Ts, dtype=jnp.int32)
    xp, xs = x_prompt, x_sample
    pC, pn, pm, pS, pckv, pkr = [], [], [], [], [], []
    sC, sn, sm, sS, sckv, skr = [], [], [], [], [], []
    for layer in range(DEPTH):
        li = layer // 2
        if layer % 2 == 0:
            mp, C, n, m, S = mixer_ab(
                xp, pos_p,
                jnp.zeros((Bp, ML_HEADS, ML_DH, ML_DH), f32), jnp.zeros((Bp, ML_HEADS, ML_DH), f32),
                jnp.zeros((Bp, ML_HEADS), f32), jnp.zeros((Bp, RET_HEADS, RET_DK, RET_DV), f32),
                w_in_a[li], b_if_a[li], g_ml[li], g_ret[li], w_out_a[li])
            pC.append(C); pn.append(n); pm.append(m); pS.append(S)
            ms, C, n, m, S = mixer_ab(
                xs, pos_s, state_mlstm_C[li], state_mlstm_n[li], state_mlstm_m[li], state_ret_S[li],
                w_in_a[li], b_if_a[li], g_ml[li], g_ret[li], w_out_a[li])
            sC.append(C); sn.append(n); sm.append(m); sS.append(S)
        else:
            mp, ckv, kr = mixer_c(xp, pos_p, None, None, w_in_c[li], g_q[li], g_kv[li],
                                  w_uq[li], w_ukv[li], w_out_c[li])
            pckv.append(ckv); pkr.append(kr)
            ms, ckv, kr = mixer_c(xs, pos_s, cache_ckv[li], cache_krope[li], w_in_c[li], g_q[li], g_kv[li],
                                  w_uq[li], w_ukv[li], w_out_c[li])
            sckv.append(ckv); skr.append(kr)
        xp = layer_norm(ALPHA * xp + mp, ln_mix_g[layer], ln_mix_b[layer])
        xs = layer_norm(ALPHA * xs + ms, ln_mix_g[layer], ln_mix_b[layer])
        xp = layer_norm(ALPHA * xp + swiglu(xp, w_gu[layer], w_down[layer]), ln_ffn_g[layer], ln_ffn_b[layer])
        xs = layer_norm(ALPHA * xs + swiglu(xs, w_gu[layer], w_down[layer]), ln_ffn_g[layer], ln_ffn_b[layer])
    return (xp, xs,
            jnp.stack(pC), jnp.stack(pn), jnp.stack(pm), jnp.stack(pS), jnp.stack(pckv), jnp.stack(pkr),
            jnp.stack(sC), jnp.stack(sn), jnp.stack(sm), jnp.stack(sS), jnp.stack(sckv), jnp.stack(skr))
```

```python
import math
import numpy as np
import concourse.bass as bass
import concourse.mybir as mybir
from concourse.bass_utils import run_bass_kernel_spmd

F32 = mybir.dt.float32
BF16 = mybir.dt.bfloat16
AF = mybir.ActivationFunctionType
ALU = mybir.AluOpType
AX = mybir.AxisListType

NCORES = 8
D = 1024
SEQ = 4096
DEC_SEQ = 64
PAST = 2048
DFF = 2816
PROJ_A = 3592
ALPHA = 4.0 ** 0.25
LN_EPS = 1e-5
RMS_EPS = 1e-6
MLA_SCALE = 192.0 ** -0.5
NPOS = SEQ + DEC_SEQ
NEG = -1.0e30
DBG = {}


def _freeze(fn):
    if getattr(fn, "__closure__", None) is None:
        return fn
    import types
    cells = []
    for c in fn.__closure__:
        try:
            cells.append(types.CellType(c.cell_contents))
        except ValueError:
            cells.append(c)
    return types.FunctionType(fn.__code__, fn.__globals__, fn.__name__, fn.__defaults__, tuple(cells))


class Res:
    __slots__ = ("w", "r")

    def __init__(self):
        self.w = None
        self.r = {}


class Obj:
    def __init__(self, n=1):
        self.res = [Res() for _ in range(n)]


class Buf:
    def __init__(self, ap, res):
        self.ap = ap
        self.res = res

    def __getitem__(self, idx):
        return self.ap[idx]


class Eng:
    def __init__(self, name, key):
        self.name = name
        self.key = key
        self.cnt = 0
        self.known = {}
        self.ops = []


class Prog:
    BLK = 256

    def __init__(self, nc, arena_bytes):
        self.nc = nc
        self.engs = {n: Eng(n, i) for i, n in enumerate(["pe", "act", "dve", "pool", "sp"])}
        self.nsem = 5
        self.dma_cnt = {}
        self.dma_last = {}
        self.arena_bytes = arena_bytes
        self.arena = nc.alloc_sbuf_tensor("arena", [128, arena_bytes // 2], BF16)
        self.ares = [Res() for _ in range(arena_bytes // self.BLK)]
        self.top = 0
        self.psum = []
        for i in range(8):
            t = nc.alloc_psum_tensor("ps%d" % i, [128, 512], F32)
            o = Obj()
            o.t = t
            o.excl = True
            self.psum.append(o)
        self.store_sems = []
        self.store_rr = 0
        self.store_eng = "pool"

    def new_dma_sem(self):
        k = self.nsem
        self.nsem += 1
        self.dma_cnt[k] = 0
        return k

    def alloc(self, shape, dtype, at=None):
        esz = 4 if dtype == F32 else 2
        free = 1
        for s in shape[1:]:
            free *= s
        nbytes = free * esz
        nbytes_r = (nbytes + self.BLK - 1) // self.BLK * self.BLK
        if at is not None:
            off = at
            assert off % self.BLK == 0 and off + nbytes_r <= self.arena_bytes
        else:
            off = self.top
            self.top += nbytes_r
            self.peak = max(getattr(self, "peak", 0), self.top)
            assert self.top <= self.arena_bytes, "arena overflow %d" % self.top
        ap = self.arena[0:shape[0], off // 2:(off + nbytes) // 2]
        if dtype == F32:
            ap = ap.bitcast(F32)
        if len(shape) == 3:
            ap = ap.rearrange("p (a b) -> p a b", a=shape[1])
        elif len(shape) == 4:
            ap = ap.rearrange("p (a b c) -> p a b c", a=shape[1], b=shape[2])
        res = self.ares[off // self.BLK:(off + nbytes_r) // self.BLK]
        return Buf(ap, res)

    def mark(self):
        return self.top

    def release(self, m):
        self.top = m

    def emit(self, ename, fn, R=(), W=(), dma_sem=None, chain=False):
        eng = self.engs[ename]
        need = {}
        if any(getattr(b, "excl", False) for b in R):
            W = tuple(W) + tuple(b for b in R if getattr(b, "excl", False))
            R = tuple(b for b in R if not getattr(b, "excl", False))

        def req(s, v):
            if ename == "pe" and s == eng.key:
                return
            if eng.known.get(s, 0) >= v:
                return
            if need.get(s, 0) < v:
                need[s] = v

        for b in R:
            for r in b.res:
                if r.w is not None:
                    req(*r.w)
        for b in W:
            for r in b.res:
                if r.w is not None:
                    req(*r.w)
                for s, v in r.r.items():
                    req(s, v)
        if dma_sem is not None:
            lt = self.dma_last.get(dma_sem)
            if lt is not None and not chain:
                req(*lt)
            self.dma_cnt[dma_sem] += 16
            tok = (dma_sem, self.dma_cnt[dma_sem])
            self.dma_last[dma_sem] = tok
        else:
            eng.cnt += 1
            tok = (eng.key, eng.cnt)
        for s, v in need.items():
            eng.known[s] = v
        for b in R:
            for r in b.res:
                if r.r.get(tok[0], 0) < tok[1]:
                    r.r[tok[0]] = tok[1]
        for b in W:
            for r in b.res:
                r.w = tok
                r.r = {}
        eng.ops.append((_freeze(fn), sorted(need.items()), tok, dma_sem is not None))
        return tok

    def pe(self, fn, R=(), W=()):
        return self.emit("pe", fn, R, W)

    def act(self, fn, R=(), W=()):
        return self.emit("act", fn, R, W)

    def dve(self, fn, R=(), W=()):
        return self.emit("dve", fn, R, W)

    def pool(self, fn, R=(), W=()):
        return self.emit("pool", fn, R, W)

    def load(self, out_ap, in_ap, sem, R=(), W=(), **kw):
        return self.emit("sp", lambda h: h.dma_start(out=out_ap, in_=in_ap, **kw), R, W, dma_sem=sem)

    def store(self, out_ap, in_ap, R=(), W=(), **kw):
        sem = self.store_sems[self.store_rr % len(self.store_sems)]
        self.store_rr += 1
        return self.emit(self.store_eng, lambda h: h.dma_start(out=out_ap, in_=in_ap, **kw), R, W, dma_sem=sem)

    def finish(self, stack):
        nc = self.nc
        pool = self.engs["pool"]
        fin = []
        for s, v in self.dma_cnt.items():
            if v > 0 and pool.known.get(s, 0) < v:
                fin.append((s, v))
        sems = [stack.enter_context(nc.semaphore("s%d" % i)) for i in range(self.nsem)]
        block = stack.enter_context(nc.Block())

        def replay(ename, h, extra=()):
            for fn, waits, tok, is_dma in self.engs[ename].ops:
                for s, v in waits:
                    h.wait_ge(sems[s], v)
                fn(h).then_inc(sems[tok[0]], 16 if is_dma else 1)
            for s, v in extra:
                h.wait_ge(sems[s], v)

        @block.tensor
        def _(h):
            replay("pe", h)

        @block.scalar
        def _(h):
            replay("act", h)

        @block.vector
        def _(h):
            replay("dve", h)

        @block.gpsimd
        def _(h):
            replay("pool", h, fin)

        @block.sync
        def _(h):
            replay("sp", h)


def _wspec():
    S = {}
    a = []
    a.append([(0, 512)])
    a.append([(512, 512)])
    a.append([(1536, 512)])
    rq0, rk0 = 2056, 2312

    def swp(base):
        p = []
        for h in range(4):
            p.append((base + 64 * h + 32, 32))
            p.append((base + 64 * h, 32))
        return p
    a.append([(rq0, 256)])
    a.append([(rk0, 256)])
    a.append([(3080, 512)])
    a.append([(1024, 512)])
    a.append([(2568, 512)])
    a.append([(2048, 8)])
    S["A"] = ("w_in_a", 0, 1024, a)
    S["OA"] = ("w_out_a", 0, 1024, [[(0, 512)], [(512, 512)]])
    for l in range(2):
        g = []
        for grp in range(11):
            m0 = 2 * grp
            g.append([(m0 * 128, 256), (DFF + m0 * 128, 256)])
        S["GU%d" % l] = ("w_gu", l, 1024, g)
        S["DN%d" % l] = ("w_down", l, DFF, [[(m * 128, 128)] for m in range(8)])
    S["C"] = ("w_in_c", 0, 1024, [[(0, 512)], [(512, 320)]])
    uq0 = [(h * 192, 128) for h in range(8)]
    uq1 = [(h * 192 + 128, 64) for h in range(8)]
    S["UQ"] = ("w_uq", 0, 512, [uq0, uq1])
    S["UKV"] = ("w_ukv", 0, 256, [[(h * 256, 128) for h in range(8)], [(h * 256 + 128, 128) for h in range(8)]])
    S["OC"] = ("w_out_c", 0, 1024, [[(0, 512)], [(512, 512)]])
    return S


def _merge(pieces):
    out = []
    for c0, n in pieces:
        if out and out[-1][0] + out[-1][1] == c0:
            out[-1] = (out[-1][0], out[-1][1] + n)
        else:
            out.append((c0, n))
    return out


def build(n_ptiles=8, n_pseq=2, n_sseq=2, stages=("mix0", "ffn0", "mix1", "ffn1")):
    from contextlib import ExitStack
    nc = bass.Bass("TRN2", target_bir_lowering=False)
    stack = ExitStack()

    def din(name, shape, dt=F32):
        return nc.dram_tensor(name, list(shape), dt, kind="ExternalInput").ap()

    def dout(name, shape, dt=F32):
        return nc.dram_tensor(name, list(shape), dt, kind="ExternalOutput").ap()

    def dscr(name, shape, dt=BF16):
        return nc.dram_tensor(name, list(shape), dt, kind="Internal").ap()

    I = {}
    I["xp"] = din("xp", [2, SEQ, D])
    I["xs"] = din("xs", [2, DEC_SEQ, D])
    I["stC"] = din("stC", [2, 4, 128, 128])
    I["stn"] = din("stn", [2, 4, 128])
    I["stm"] = din("stm", [2, 4])
    I["stS"] = din("stS", [2, 4, 64, 128])
    I["cckv"] = din("cckv", [2, PAST, 256])
    I["ckr"] = din("ckr", [2, PAST, 64])
    I["w_in_a"] = din("w_in_a", [1, D, PROJ_A])
    I["b_if"] = din("b_if", [8])
    I["g_ml"] = din("g_ml", [512])
    I["g_ret"] = din("g_ret", [512])
    I["w_out_a"] = din("w_out_a", [1, D, D])
    I["w_in_c"] = din("w_in_c", [1, D, 832])
    I["g_q"] = din("g_q", [512])
    I["g_kv"] = din("g_kv", [256])
    I["w_uq"] = din("w_uq", [1, 512, 1536])
    I["w_ukv"] = din("w_ukv", [1, 256, 2048])
    I["w_out_c"] = din("w_out_c", [1, D, D])
    I["ln"] = din("ln", [8, D])
    I["w_gu"] = din("w_gu", [2, D, 2 * DFF])
    I["w_down"] = din("w_down", [2, DFF, D])
    I["ident"] = din("ident", [128, 128])
    I["cosF"] = din("cosF", [128, NPOS])
    I["sinF"] = din("sinF", [128, NPOS])
    I["cosT"] = din("cosT", [NPOS, 32])
    I["sinT"] = din("sinT", [NPOS, 32])
    I["mask_ml"] = din("mask_ml", [128, 64])
    I["mret"] = din("mret", [128, 4, 64])
    I["kdec"] = din("kdec", [128, 2, 128])
    I["qdec"] = din("qdec", [128, 2, 64])
    I["sel"] = din("sel", [4, 4, 128])
    I["dmask"] = din("dmask", [128, 128])

    O = {}
    O["yp"] = dout("yp", [2, SEQ, D])
    O["ys"] = dout("ys", [2, DEC_SEQ, D])
    O["pC"] = dout("pC", [2, 4, 128, 128])
    O["pn"] = dout("pn", [2, 4, 128])
    O["pm"] = dout("pm", [2, 4])
    O["pS"] = dout("pS", [2, 4, 64, 128])
    O["pckv"] = dout("pckv", [2, SEQ, 256])
    O["pkr"] = dout("pkr", [2, SEQ, 64])
    O["sC"] = dout("sC", [2, 4, 128, 128])
    O["sn"] = dout("sn", [2, 4, 128])
    O["sm"] = dout("sm", [2, 4])
    O["sS"] = dout("sS", [2, 4, 64, 128])
    O["sckv"] = dout("sckv", [2, DEC_SEQ, 256])
    O["skr"] = dout("skr", [2, DEC_SEQ, 64])

    P = Prog(nc, arena_bytes=207 * 1024)
    P.store_sems = [P.new_dma_sem() for _ in range(8)]
    ps = P.psum

    def PS(i, parts=128, cols=512, c0=0, p0=0):
        return ps[i].t.ap()[p0:p0 + parts, c0:c0 + cols]

    def PSB(i):
        return ps[i].t.ap().bitcast(BF16)

    spec = _wspec()
    WS = {}
    need = set()
    if "mix0" in stages:
        need |= {"A", "OA"}
    if "ffn0" in stages:
        need |= {"GU0", "DN0"}
    if "mix1" in stages:
        need |= {"C", "UQ", "UKV", "OC"}
    if "ffn1" in stages:
        need |= {"GU1", "DN1"}
    for name in ("A", "OA", "GU0", "DN0", "C", "UKV", "UQ", "OC", "GU1", "DN1"):
        (src, li, K, groups) = spec[name]
        if name not in need:
            continue
        M = I[src].shape[2]
        scr = dscr("ws_%s" % name, [K, M])
        o = Obj()
        for c0 in range(0, M, 2048):
            c1 = min(M, c0 + 2048)
            P.store(scr[:, c0:c1], I[src][li, :, c0:c1], R=(), W=(o,))
        WS[name] = (scr, o, K // 128, [_merge(g) for g in groups])

    sem_c = P.new_dma_sem()
    ident = P.alloc([128, 128], F32)
    P.load(ident.ap, I["ident"], sem_c, W=(ident,))
    identb = P.alloc([128, 128], BF16)
    P.dve(lambda h: h.tensor_copy(out=identb.ap, in_=ident.ap), R=(ident,), W=(identb,))
    onesb = P.alloc([128, 128], BF16)
    P.pool(lambda h: h.memset(onesb.ap, 1.0), W=(onesb,))
    lnp = P.alloc([128, 8, 8], F32)
    lnrow = P.alloc([8, 1024], F32, at=P.arena_bytes - 4096)
    P.load(lnrow.ap, I["ln"], sem_c, W=(lnrow,))
    for m in range(8):
        P.pe(lambda h, m=m: h.transpose(out=ps[0].t.ap()[:, m * 8:(m + 1) * 8], in_=lnrow.ap[0:8, m * 128:(m + 1) * 128],
                                        identity=ident.ap[0:8, 0:8]), R=(lnrow, ident), W=(ps[0],))
    P.dve(lambda h: h.tensor_copy(out=lnp.ap.rearrange("p w m -> p m w"), in_=ps[0].t.ap()[:, 0:64].rearrange("p (m w) -> p m w", m=8)),
          R=(ps[0],), W=(lnp,))

    NW = 3
    WSLOT = 4096
    wring = [P.alloc([128, WSLOT], BF16) for _ in range(NW)]
    wsem = [P.new_dma_sem() for _ in range(NW)]
    wrr = [0]

    def wload(name, gi):
        scr, o, KC, groups = WS[name]
        pieces = groups[gi]
        i = wrr[0] % NW
        wrr[0] += 1
        slot = wring[i]
        n = pieces[0][1]
        cnt = len(pieces)
        ncols = cnt * n
        assert KC * ncols <= WSLOT
        if cnt == 1:
            c0 = pieces[0][0]
            srcap = scr[:, c0:c0 + n].rearrange("(kc p) c -> p kc c", p=128)
            dstap = slot.ap[:, 0:KC * n].rearrange("p (k c) -> p k c", k=KC)
            v = slot.ap[:, 0:KC * n].rearrange("p (k c) -> p k c", k=KC)
            P.load(dstap, srcap, wsem[i], R=(o,), W=(slot,))
            return slot, v
        assert all(p_[1] == n for p_ in pieces)
        v = slot.ap[:, 0:KC * ncols].rearrange("p (k c) -> p k c", k=KC)
        for j, (c0, _) in enumerate(pieces):
            srcap = scr[:, c0:c0 + n].rearrange("(kc p) c -> p kc c", p=128)
            dstap = v[:, :, j * n:(j + 1) * n]
            if j == 0:
                P.load(dstap, srcap, wsem[i], R=(o,), W=(slot,))
            else:
                P.emit("sp", lambda h: h.dma_start(out=dstap, in_=srcap), (), (), dma_sem=wsem[i], chain=True)
        ftok = (wsem[i], P.dma_cnt[wsem[i]])
        for r in slot.res:
            r.w = ftok
        return slot, v

    def wswap(slot, w, KC, ncols):
        wsw = slot.ap[:, KC * ncols:2 * KC * ncols].rearrange("p (k c) -> p k c", k=KC)
        src = w.rearrange("p k (h t j) -> p k h t j", t=2, j=32)
        dst = wsw.rearrange("p k (h t j) -> p k h t j", t=2, j=32)
        P.dve(lambda h: h.tensor_copy(out=dst[:, :, :, 0, :], in_=src[:, :, :, 1, :]), R=(slot,), W=(slot,))
        P.act(lambda h: h.activation(out=dst[:, :, :, 1, :], in_=src[:, :, :, 0, :], func=AF.Copy), R=(slot,), W=(slot,))
        return wsw

    xT = P.alloc([128, 8, 512], F32)
    xTb = P.alloc([128, 8, 512], BF16)
    def subs(buf, n):
        k = len(buf.res) // n
        assert k * n == len(buf.res)
        return [Buf(buf.ap[:, i], buf.res[i * k:(i + 1) * k]) for i in range(n)]
    xTc = subs(xT, 8)
    xTbc = subs(xTb, 8)
    T = [P.alloc([128, 512], F32) for _ in range(4)]
    TB = [P.alloc([128, 512], BF16) for _ in range(4)]
    xsem = [P.new_dma_sem() for _ in range(2)]
    base_mark = P.mark()

    psrr = [0]

    def nextps(lo=0, hi=4):
        i = lo + psrr[0] % (hi - lo)
        psrr[0] += 1
        return i

    def load_x(src_rows, N):
        nb = (N + 127) // 128
        m0 = P.mark()
        for b in range(nb):
            bs = min(128, N - b * 128)
            xin = P.alloc([128, 1024], F32)
            P.load(xin.ap[0:bs, :], src_rows[b * 128:b * 128 + bs, :], xsem[b % 2], W=(xin,))
            for half in range(2):
                pi = nextps()
                for j in range(4):
                    m = half * 4 + j
                    P.pe(lambda h, m=m, j=j, pi=pi, xin=xin, bs=bs: h.transpose(
                        out=PS(pi, 128, 128, j * 128), in_=xin.ap[:, m * 128:(m + 1) * 128],
                        identity=ident.ap), R=(xin, ident), W=(ps[pi],))
                src = ps[pi].t.ap().rearrange("p (j c) -> p j c", j=4)[:, :, 0:bs]
                P.act(lambda h, src=src, half=half, b=b, bs=bs: h.activation(
                    out=xT.ap[:, half * 4:half * 4 + 4, b * 128:b * 128 + bs], in_=src, func=AF.Copy),
                    R=(ps[pi],), W=(xT,))
                P.dve(lambda h, src=src, half=half, b=b, bs=bs: h.tensor_copy(
                    out=xTb.ap[:, half * 4:half * 4 + 4, b * 128:b * 128 + bs], in_=src),
                    R=(ps[pi],), W=(xTb,))
            P.release(P.mark())
        if not DBG.get("norel"):
            P.release(m0)

    def store_y(dst_rows, N):
        nb = (N + 127) // 128
        m0 = P.mark()
        for b in range(nb):
            bs = min(128, N - b * 128)
            yo = P.alloc([128, 1024], F32)
            for half in range(2):
                pi = nextps()
                for j in range(4):
                    m = half * 4 + j
                    P.pe(lambda h, m=m, j=j, pi=pi, b=b, bs=bs: h.transpose(
                        out=PS(pi, 128, 128, j * 128), in_=xT.ap[:, m, b * 128:(b + 1) * 128],
                        identity=ident.ap), R=(xT, ident), W=(ps[pi],))
                if half == 0:
                    P.act(lambda h, pi=pi, yo=yo, bs=bs: h.activation(
                        out=yo.ap[0:bs, 0:512], in_=PS(pi, bs, 512), func=AF.Copy), R=(ps[pi],), W=(yo,))
                else:
                    P.dve(lambda h, pi=pi, yo=yo, bs=bs: h.tensor_copy(
                        out=yo.ap[0:bs, 512:1024], in_=PS(pi, bs, 512)), R=(ps[pi],), W=(yo,))
            P.store(dst_rows[b * 128:b * 128 + bs, :], yo.ap[0:bs, :], R=(yo,))
        if not DBG.get("norel"):
            P.release(m0)

    def fm_norm(chunks, N, F, eps, center):
        n = len(chunks)
        p2 = nextps()
        p1 = nextps() if center else None
        for i, (b, ap) in enumerate(chunks):
            sq = TB[i % 2]
            P.act(lambda h: h.activation(out=sq.ap[:, 0:N], in_=ap, func=AF.Square), R=(b,), W=(sq,))
            P.pe(lambda h: h.matmul(PS(p2, 128, N), lhsT=onesb.ap, rhs=sq.ap[:, 0:N], start=(i == 0), stop=(i == n - 1)),
                 R=(sq, onesb), W=(ps[p2],))
            if center:
                zb = TB[2 + i % 2]
                P.dve(lambda h: h.tensor_copy(out=zb.ap[:, 0:N], in_=ap), R=(b,), W=(zb,))
                P.pe(lambda h: h.matmul(PS(p1, 128, N), lhsT=onesb.ap, rhs=zb.ap[:, 0:N], start=(i == 0), stop=(i == n - 1)),
                     R=(zb, onesb), W=(ps[p1],))
        mean, var = T[3], T[2]
        if center:
            P.act(lambda h: h.activation(out=mean.ap[:, 0:N], in_=PS(p1, 128, N), func=AF.Identity, scale=1.0 / F),
                  R=(ps[p1],), W=(mean,))
            P.act(lambda h: h.activation(out=var.ap[:, 0:N], in_=mean.ap[:, 0:N], func=AF.Square), R=(mean,), W=(var,))
            P.dve(lambda h: h.scalar_tensor_tensor(out=var.ap[:, 0:N], in0=PS(p2, 128, N), scalar=1.0 / F, in1=var.ap[:, 0:N],
                                                   op0=ALU.mult, op1=ALU.subtract), R=(ps[p2], var), W=(var,))
            P.dve(lambda h: h.tensor_scalar(out=var.ap[:, 0:N], in0=var.ap[:, 0:N], scalar1=0.0, scalar2=float(eps),
                                            op0=ALU.max, op1=ALU.add), R=(var,), W=(var,))
        else:
            P.dve(lambda h: h.tensor_scalar(out=var.ap[:, 0:N], in0=PS(p2, 128, N), scalar1=1.0 / F, scalar2=float(eps),
                                            op0=ALU.mult, op1=ALU.add), R=(ps[p2],), W=(var,))
        P.act(lambda h: h.activation(out=var.ap[:, 0:N], in_=var.ap[:, 0:N], func=AF.Ln), R=(var,), W=(var,))
        P.act(lambda h: h.activation(out=var.ap[:, 0:N], in_=var.ap[:, 0:N], func=AF.Exp, scale=-0.5), R=(var,), W=(var,))
        for i, (b, ap) in enumerate(chunks):
            if center:
                P.dve(lambda h: h.tensor_tensor(out=ap, in0=ap, in1=mean.ap[:, 0:N], op=ALU.subtract), R=(b, mean), W=(b,))
            P.dve(lambda h: h.tensor_tensor(out=ap, in0=ap, in1=var.ap[:, 0:N], op=ALU.mult), R=(b, var), W=(b,))

    epsb = {}
    for e in (LN_EPS, RMS_EPS):
        eb = P.alloc([128, 1], F32)
        P.pool(lambda h, eb=eb, e=e: h.memset(eb.ap, e), W=(eb,))
        epsb[e] = eb
    base_mark = P.mark()

    def ln_affine(N, gi, bi):
        for m in range(8):
            P.act(lambda h: h.activation(out=xTb.ap[:, m, 0:N], in_=xT.ap[:, m, 0:N], func=AF.Identity,
                                         scale=lnp.ap[:, gi, m:m + 1], bias=lnp.ap[:, bi, m:m + 1]),
                  R=(xTc[m], lnp), W=(xTbc[m],))
            P.act(lambda h: h.activation(out=xT.ap[:, m, 0:N], in_=xT.ap[:, m, 0:N], func=AF.Identity,
                                         scale=lnp.ap[:, gi, m:m + 1], bias=lnp.ap[:, bi, m:m + 1]),
                  R=(xTc[m], lnp), W=(xTc[m],))

    ln_pending = []
    LN1, LN2 = 4, 5

    def resid_chunk(m, pi, N):
        while ln_pending:
            ln_pending.pop(0)()
        zb, sq = TB[2 + m % 2], TB[m % 2]
        P.dve(lambda h: h.scalar_tensor_tensor(out=xT.ap[:, m, 0:N], in0=xT.ap[:, m, 0:N], scalar=ALPHA, in1=PS(pi, 128, N),
                                               op0=ALU.mult, op1=ALU.add), R=(xTc[m], ps[pi]), W=(xTc[m],))
        P.dve(lambda h: h.tensor_copy(out=zb.ap[:, 0:N], in_=xT.ap[:, m, 0:N]), R=(xTc[m],), W=(zb,))
        P.act(lambda h: h.activation(out=sq.ap[:, 0:N], in_=xT.ap[:, m, 0:N], func=AF.Square), R=(xTc[m],), W=(sq,))
        def stat_mm(m=m, zb=zb, sq=sq, N=N):
            P.pe(lambda h: h.matmul(PS(LN1, 128, N), lhsT=onesb.ap, rhs=zb.ap[:, 0:N], start=(m == 0), stop=(m == 7)),
                 R=(zb, onesb), W=(ps[LN1],))
            P.pe(lambda h: h.matmul(PS(LN2, 128, N), lhsT=onesb.ap, rhs=sq.ap[:, 0:N], start=(m == 0), stop=(m == 7)),
                 R=(sq, onesb), W=(ps[LN2],))
        ln_pending.append(stat_mm)

    def residual_ln(N, lay, which):
        while ln_pending:
            ln_pending.pop(0)()
        F = float(D)
        gi = (0 if which == 0 else 4) + lay
        bi = gi + 2
        mean, var = T[3], T[2]
        P.act(lambda h: h.activation(out=mean.ap[:, 0:N], in_=PS(LN1, 128, N), func=AF.Identity, scale=1.0 / F),
              R=(ps[LN1],), W=(mean,))
        P.act(lambda h: h.activation(out=var.ap[:, 0:N], in_=mean.ap[:, 0:N], func=AF.Square), R=(mean,), W=(var,))
        P.dve(lambda h: h.scalar_tensor_tensor(out=var.ap[:, 0:N], in0=PS(LN2, 128, N), scalar=1.0 / F, in1=var.ap[:, 0:N],
                                               op0=ALU.mult, op1=ALU.subtract), R=(ps[LN2], var), W=(var,))
        P.dve(lambda h: h.tensor_scalar(out=var.ap[:, 0:N], in0=var.ap[:, 0:N], scalar1=0.0, scalar2=float(LN_EPS),
                                        op0=ALU.max, op1=ALU.add), R=(var,), W=(var,))
        P.act(lambda h: h.activation(out=var.ap[:, 0:N], in_=var.ap[:, 0:N], func=AF.Ln), R=(var,), W=(var,))
        P.act(lambda h: h.activation(out=var.ap[:, 0:N], in_=var.ap[:, 0:N], func=AF.Exp, scale=-0.5), R=(var,), W=(var,))
        for m in range(8):
            P.dve(lambda h: h.tensor_tensor(out=xT.ap[:, m, 0:N], in0=xT.ap[:, m, 0:N], in1=mean.ap[:, 0:N], op=ALU.subtract),
                  R=(xTc[m], mean), W=(xTc[m],))
            P.dve(lambda h: h.tensor_tensor(out=xT.ap[:, m, 0:N], in0=xT.ap[:, m, 0:N], in1=var.ap[:, 0:N], op=ALU.mult),
                  R=(xTc[m], var), W=(xTc[m],))
            P.act(lambda h: h.activation(out=xTb.ap[:, m, 0:N], in_=xT.ap[:, m, 0:N], func=AF.Identity,
                                         scale=lnp.ap[:, gi, m:m + 1], bias=lnp.ap[:, bi, m:m + 1]),
                  R=(xTc[m], lnp), W=(xTbc[m],))
        for m in range(8):
            P.act(lambda h: h.activation(out=xT.ap[:, m, 0:N], in_=xT.ap[:, m, 0:N], func=AF.Identity,
                                         scale=lnp.ap[:, gi, m:m + 1], bias=lnp.ap[:, bi, m:m + 1]),
                  R=(xTc[m], lnp), W=(xTc[m],))

    def ffn(N, lay):
        m0 = P.mark()
        aT = P.alloc([128, 22, 512], BF16)
        gu = "GU%d" % lay
        for grp in range(11):
            slot, w = wload(gu, grp)
            for j in range(2):
                m = 2 * grp + j
                pg = nextps()
                pu = nextps()
                for kc in range(8):
                    P.pe(lambda h, kc=kc, j=j, w=w, pg=pg: h.matmul(
                        PS(pg, 128, N), lhsT=w[:, kc, j * 128:(j + 1) * 128], rhs=xTb.ap[:, kc, 0:N],
                        start=(kc == 0), stop=(kc == 7)), R=(slot, xTb), W=(ps[pg],))
                for kc in range(8):
                    P.pe(lambda h, kc=kc, j=j, w=w, pu=pu: h.matmul(
                        PS(pu, 128, N), lhsT=w[:, kc, 256 + j * 128:256 + (j + 1) * 128], rhs=xTb.ap[:, kc, 0:N],
                        start=(kc == 0), stop=(kc == 7)), R=(slot, xTb), W=(ps[pu],))
                sg = T[m % 2]
                P.act(lambda h, sg=sg, pg=pg: h.activation(out=sg.ap[:, 0:N], in_=PS(pg, 128, N), func=AF.Silu),
                      R=(ps[pg],), W=(sg,))
                P.dve(lambda h, sg=sg, pu=pu, m=m: h.tensor_tensor(
                    out=aT.ap[:, m, 0:N], in0=PS(pu, 128, N), in1=sg.ap[:, 0:N], op=ALU.mult),
                    R=(ps[pu], sg), W=(aT,))
        dn = "DN%d" % lay
        for m in range(8):
            slot, w = wload(dn, m)
            pi = nextps()
            for kc in range(22):
                P.pe(lambda h, kc=kc, w=w, pi=pi: h.matmul(
                    PS(pi, 128, N), lhsT=w[:, kc, :], rhs=aT.ap[:, kc, 0:N],
                    start=(kc == 0), stop=(kc == 21)), R=(slot, aT), W=(ps[pi],))
            resid_chunk(m, pi, N)
        residual_ln(N, lay, 1)
        P.release(m0)

    vrow = P.alloc([2, 1024], F32, at=P.arena_bytes - 8192)
    P.load(vrow.ap[0:1, 0:512], I["g_ml"].rearrange("(o n) -> o n", o=1), sem_c, W=(vrow,))
    P.load(vrow.ap[0:1, 512:1024], I["g_ret"].rearrange("(o n) -> o n", o=1), sem_c, W=(vrow,))
    P.load(vrow.ap[1:2, 0:512], I["g_q"].rearrange("(o n) -> o n", o=1), sem_c, W=(vrow,))
    P.load(vrow.ap[1:2, 512:1024], I["g_q"].rearrange("(o n) -> o n", o=1), sem_c, W=(vrow,))
    vecp = P.alloc([128, 2, 8], F32)
    for m in range(8):
        P.pe(lambda h, m=m: h.transpose(out=ps[1].t.ap()[:, m * 2:(m + 1) * 2], in_=vrow.ap[0:2, m * 128:(m + 1) * 128],
                                        identity=ident.ap[0:2, 0:2]), R=(vrow, ident), W=(ps[1],))
    P.dve(lambda h: h.tensor_copy(out=vecp.ap.rearrange("p w m -> p m w"),
                                  in_=ps[1].t.ap()[:, 0:16].rearrange("p (m w) -> p m w", m=8)), R=(ps[1],), W=(vecp,))
    bif = P.alloc([4, 2], F32)
    P.load(bif.ap, I["b_if"].rearrange("(t h) -> h t", t=2), sem_c, W=(bif,), allow_slow_non_contiguous=True)
    nbf = P.alloc([4, 1], F32)
    P.dve(lambda h: h.tensor_scalar(out=nbf.ap, in0=bif.ap[:, 1:2], scalar1=-1.0, scalar2=None, op0=ALU.mult),
          R=(bif,), W=(nbf,))
    ones4 = P.alloc([4, 512], F32)
    P.pool(lambda h: h.memset(ones4.ap, 1.0), W=(ones4,))
    sel = P.alloc([4, 512], F32)
    P.load(sel.ap, I["sel"].rearrange("k h c -> k (h c)"), sem_c, W=(sel,))
    maskml = P.alloc([128, 64], F32)
    P.load(maskml.ap, I["mask_ml"], sem_c, W=(maskml,))
    mret = P.alloc([128, 4, 64], F32)
    P.load(mret.ap, I["mret"], sem_c, W=(mret,))
    kdec = P.alloc([128, 2, 128], F32)
    P.load(kdec.ap, I["kdec"], sem_c, W=(kdec,))
    qdec = P.alloc([128, 2, 64], F32)
    P.load(qdec.ap, I["qdec"], sem_c, W=(qdec,))
    gkvb = P.alloc([128, 256], F32)
    P.load(gkvb.ap, I["g_kv"].partition_broadcast(128), sem_c, W=(gkvb,))
    dmask = P.alloc([128, 128], F32)
    P.load(dmask.ap, I["dmask"], sem_c, W=(dmask,))
    CaugH = [[P.alloc([128, 256], F32) for _ in range(2)] for _ in range(4)]
    SH = [[P.alloc([128, 128], F32) for _ in range(2)] for _ in range(4)]
    ccur = [0, 0, 0, 0]
    scur = [0, 0, 0, 0]
    Bc = P.alloc([4, 1], F32)
    Gc = P.alloc([4, 1], F32)
    cosb = P.alloc([128, 512], F32)
    sinb = P.alloc([128, 512], F32)
    tsem = P.new_dma_sem()
    GAM = [1.0 - 2.0 ** (-5.0 - h) for h in range(4)]

    def state_zero():
        for hh in range(4):
            ccur[hh] = 0
            scur[hh] = 0
            P.pool(lambda h: h.memset(CaugH[hh][0].ap, 0.0), W=(CaugH[hh][0],))
            P.pool(lambda h: h.memset(SH[hh][0].ap, 0.0), W=(SH[hh][0],))
        P.pool(lambda h: h.memset(Bc.ap, 0.0), W=(Bc,))
        P.pool(lambda h: h.memset(Gc.ap, 0.0), W=(Gc,))

    def state_load(s):
        m0 = P.mark()
        cin = P.alloc([128, 4, 128], F32)
        P.load(cin.ap, I["stC"][s].rearrange("h v d -> v h d"), tsem, W=(cin,))
        nrow = P.alloc([4, 128], F32)
        P.load(nrow.ap, I["stn"][s], tsem, W=(nrow,))
        pi = nextps()
        for hh in range(4):
            P.pe(lambda h: h.transpose(out=PS(pi, 128, 128, hh * 128), in_=cin.ap[:, hh, :], identity=ident.ap),
                 R=(cin, ident), W=(ps[pi],))
        pj = nextps()
        P.pe(lambda h: h.transpose(out=PS(pj, 128, 4), in_=nrow.ap[0:4, :], identity=ident.ap[0:4, 0:4]),
             R=(nrow, ident), W=(ps[pj],))
        ncol = P.alloc([128, 4], F32)
        P.act(lambda h: h.activation(out=ncol.ap, in_=PS(pj, 128, 4), func=AF.Copy), R=(ps[pj],), W=(ncol,))
        for hh in range(4):
            ccur[hh] = 0
            scur[hh] = 0
            ho = (hh % 2) * 64
            P.act(lambda h: h.activation(out=CaugH[hh][0].ap[:, 0:128], in_=PS(pi, 128, 128, hh * 128), func=AF.Copy),
                  R=(ps[pi],), W=(CaugH[hh][0],))
            P.dve(lambda h: h.tensor_copy(out=CaugH[hh][0].ap[:, 128:256], in_=ncol.ap[:, hh:hh + 1].broadcast_to([128, 128])),
                  R=(ncol,), W=(CaugH[hh][0],))
            P.load(SH[hh][0].ap[ho:ho + 64, :], I["stS"][s, hh], tsem, W=(SH[hh][0],))
        P.load(Gc.ap, I["stm"][s].rearrange("(h o) -> h o", o=1), tsem, W=(Gc,))
        P.pool(lambda h: h.memset(Bc.ap, 0.0), W=(Bc,))
        P.release(m0)

    def state_store(kC, kn, km, kS, s):
        m0 = P.mark()
        co = P.alloc([128, 4, 128], F32)
        no = P.alloc([128, 4], F32)
        pi = nextps()
        for hh in range(4):
            cb_ = CaugH[hh][ccur[hh]]
            P.pe(lambda h: h.transpose(out=PS(pi, 128, 128, hh * 128), in_=cb_.ap[:, 0:128], identity=ident.ap),
                 R=(cb_, ident), W=(ps[pi],))
            P.dve(lambda h: h.tensor_copy(out=no.ap[:, hh:hh + 1], in_=cb_.ap[:, 128:129]), R=(cb_,), W=(no,))
        P.act(lambda h: h.activation(out=co.ap, in_=ps[pi].t.ap().rearrange("p (h c) -> p h c", h=4), func=AF.Copy),
              R=(ps[pi],), W=(co,))
        P.store(O[kC][s].rearrange("h v d -> v h d"), co.ap, R=(co,))
        P.store(O[kn][s].rearrange("h d -> d h"), no.ap, R=(no,), allow_slow_non_contiguous=True)
        mo_ = P.alloc([4, 1], F32)
        P.dve(lambda h: h.tensor_tensor(out=mo_.ap, in0=Bc.ap, in1=Gc.ap, op=ALU.add), R=(Bc, Gc), W=(mo_,))
        P.store(O[km][s].rearrange("(h o) -> h o", o=1), mo_.ap, R=(mo_,))
        for hh in range(4):
            ho = (hh % 2) * 64
            sb_ = SH[hh][scur[hh]]
            P.store(O[kS][s, hh], sb_.ap[ho:ho + 64, :], R=(sb_,))
        P.release(m0)

    def mix0(N, pos0):
        nb = (N + 127) // 128
        nch = N // 64
        m0 = P.mark()
        qTm = P.alloc([128, 4, 512], BF16)
        kTm = P.alloc([128, 4, 512], BF16)
        sigo = P.alloc([128, 4, 512], F32)
        silg = P.alloc([128, 4, 512], F32)
        rqT = P.alloc([128, 2, 512], BF16)
        rqd = P.alloc([128, 2, 512], BF16)
        rkT = P.alloc([128, 2, 512], BF16)
        mv_tm = P.alloc([128, 4, 512], BF16)
        rv_tm = P.alloc([128, 4, 512], BF16)
        rk_tm = P.alloc([128, 4, 256], BF16)
        mixed = P.alloc([128, 8, 512], BF16)
        GR = P.alloc([4, 6, 512], F32)
        wk_tm = P.alloc([128, 4, 4], F32)
        dec_b = P.alloc([128, 4, 8], F32)
        kp_tmL = [P.alloc([128, 4, 128], BF16) for _ in range(2)]
        MWL = [P.alloc([128, 4, 64], F32) for _ in range(2)]
        PTL = [P.alloc([128, 4, 64], BF16) for _ in range(4)]
        CbL = [[P.alloc([128, 256], BF16) for _ in range(3)] for _ in range(2)]
        SbL = [[P.alloc([128, 128], BF16) for _ in range(3)] for _ in range(4)]
        for hh_ in range(4):
            for b_ in SbL[hh_]:
                P.pool(lambda h: h.memset(b_.ap, 0.0), W=(b_,))
        hT = [P.alloc([128, 512], F32) for _ in range(2)]
        P.load(cosb.ap[:, 0:N], I["cosF"][:, pos0:pos0 + N], tsem, W=(cosb,))
        P.load(sinb.ap[:, 0:N], I["sinF"][:, pos0:pos0 + N], tsem, W=(sinb,))

        def fm_chain(w, c0, pi, slot):
            for kc in range(8):
                P.pe(lambda h, kc=kc: h.matmul(PS(pi, 128, N), lhsT=w[:, kc, c0:c0 + 128], rhs=xTb.ap[:, kc, 0:N],
                                               start=(kc == 0), stop=(kc == 7)), R=(slot, xTb), W=(ps[pi],))

        for grp, dst, fn, sc in ((0, qTm, AF.Identity, 1.0), (1, kTm, AF.Identity, 128.0 ** -0.5),
                                 (2, sigo, AF.Sigmoid, 1.0), (5, silg, AF.Silu, 1.0)):
            slot, w = wload("A", grp)
            for hh in range(4):
                pi = nextps()
                fm_chain(w, hh * 128, pi, slot)
                P.act(lambda h, hh=hh, pi=pi, dst=dst, fn=fn, sc=sc: h.activation(
                    out=dst.ap[:, hh, 0:N], in_=PS(pi, 128, N), func=fn, scale=sc), R=(ps[pi],), W=(dst,))
        for grp, dst in ((3, rqT), (4, rkT)):
            slot, w = wload("A", grp)
            wsw = wswap(slot, w, 8, 256)
            for pr in range(2):
                pa = nextps()
                fm_chain(w, pr * 128, pa, slot)
                pb = nextps()
                fm_chain(wsw, pr * 128, pb, slot)
                P.dve(lambda h, pa=pa: h.tensor_tensor(out=T[0].ap[:, 0:N], in0=PS(pa, 128, N), in1=cosb.ap[:, 0:N], op=ALU.mult),
                      R=(ps[pa], cosb), W=(T[0],))
                P.dve(lambda h, pb=pb: h.tensor_tensor(out=T[1].ap[:, 0:N], in0=PS(pb, 128, N), in1=sinb.ap[:, 0:N], op=ALU.mult),
                      R=(ps[pb], sinb), W=(T[1],))
                P.pool(lambda h: h.tensor_tensor(out=T[0].ap[:, 0:N], in0=T[0].ap[:, 0:N], in1=T[1].ap[:, 0:N], op=ALU.add),
                       R=(T[0], T[1]), W=(T[0],))
                P.act(lambda h, dst=dst, pr=pr: h.activation(out=dst.ap[:, pr, 0:N], in_=T[0].ap[:, 0:N], func=AF.Copy),
                      R=(T[0],), W=(dst,))
                if grp == 3:
                    P.dve(lambda h, pr=pr: h.tensor_tensor(
                        out=rqd.ap[:, pr, 0:N].rearrange("p (c t) -> p c t", t=64),
                        in0=T[0].ap[:, 0:N].rearrange("p (c t) -> p c t", t=64),
                        in1=qdec.ap[:, pr:pr + 1, :].broadcast_to([128, nch, 64]), op=ALU.mult),
                        R=(T[0], qdec), W=(rqd,))
        if DBG.get('stop') == 1:
            P.release(m0)
            return
        DBG.get('phase_hook', lambda n, p: None)('m0_vproj', P)
        for grp, dst in ((6, mv_tm), (7, rv_tm)):
            slot, w = wload("A", grp)
            for b in range(nb):
                bs = min(128, N - b * 128)
                pi = nextps()
                for kc in range(8):
                    P.pe(lambda h, kc=kc, b=b, bs=bs, pi=pi, w=w: h.matmul(
                        PS(pi, bs, 512), lhsT=xTb.ap[:, kc, b * 128:b * 128 + bs], rhs=w[:, kc, 0:512],
                        start=(kc == 0), stop=(kc == 7)), R=(slot, xTb), W=(ps[pi],))
                if b % 2 == 0:
                    P.act(lambda h, b=b, bs=bs, pi=pi, dst=dst: h.activation(out=dst.ap[0:bs, b, :], in_=PS(pi, bs, 512), func=AF.Copy),
                          R=(ps[pi],), W=(dst,))
                else:
                    P.dve(lambda h, b=b, bs=bs, pi=pi, dst=dst: h.tensor_copy(out=dst.ap[0:bs, b, :], in_=PS(pi, bs, 512)),
                          R=(ps[pi],), W=(dst,))
        if DBG.get('stop') == 2:
            P.release(m0)
            return
        DBG.get('phase_hook', lambda n, p: None)('m0_gates', P)
        slot, w = wload("A", 8)
        pig = nextps()
        pfg = nextps()
        for kc in range(8):
            P.pe(lambda h, kc=kc: h.matmul(PS(pig, 4, N), lhsT=w[:, kc, 0:4], rhs=xTb.ap[:, kc, 0:N],
                                           start=(kc == 0), stop=(kc == 7)), R=(slot, xTb), W=(ps[pig],))
        for kc in range(8):
            P.pe(lambda h, kc=kc: h.matmul(PS(pfg, 4, N), lhsT=w[:, kc, 4:8], rhs=xTb.ap[:, kc, 0:N],
                                           start=(kc == 0), stop=(kc == 7)), R=(slot, xTb), W=(ps[pfg],))
        gL, gB, gA, gG, gX, gE = [GR.ap[:, i, :] for i in range(6)]
        P.act(lambda h: h.activation(out=gL[:, 0:N], in_=PS(pfg, 4, N), func=AF.Exp, scale=-1.0, bias=nbf.ap[:, 0:1]),
              R=(ps[pfg], nbf), W=(GR,))
        P.act(lambda h: h.activation(out=gL[:, 0:N], in_=gL[:, 0:N], func=AF.Ln, bias=1.0, scale=1.0), R=(GR,), W=(GR,))
        P.dve(lambda h: h.tensor_tensor_scan(out=gB[:, 0:N], data0=ones4.ap[:, 0:N], data1=gL[:, 0:N], initial=Bc.ap[:, 0:1],
                                             op0=ALU.mult, op1=ALU.subtract), R=(GR, ones4, Bc), W=(GR,))
        P.dve(lambda h: h.scalar_tensor_tensor(out=gA[:, 0:N], in0=PS(pig, 4, N), scalar=bif.ap[:, 0:1], in1=gB[:, 0:N],
                                               op0=ALU.add, op1=ALU.subtract), R=(ps[pig], bif, GR), W=(GR,))
        P.dve(lambda h: h.tensor_tensor_scan(out=gG[:, 0:N], data0=ones4.ap[:, 0:N], data1=gA[:, 0:N], initial=Gc.ap[:, 0:1],
                                             op0=ALU.mult, op1=ALU.max), R=(GR, ones4, Gc), W=(GR,))
        g3 = lambda a: a[:, 0:N].rearrange("p (c t) -> p c t", t=64)
        P.act(lambda h: h.activation(out=g3(gX), in_=g3(gG)[:, :, 63:64].broadcast_to([4, nch, 64]), func=AF.Copy),
              R=(GR,), W=(GR,))
        P.dve(lambda h: h.tensor_copy(out=gE[:, 0:1], in_=Gc.ap[:, 0:1]), R=(Gc,), W=(GR,))
        if nch > 1:
            P.dve(lambda h: h.tensor_copy(out=gE[:, 1:nch], in_=g3(gG)[:, 0:nch - 1, 63]), R=(GR,), W=(GR,))
        P.dve(lambda h: h.tensor_tensor(out=gE[:, 0:nch], in0=gE[:, 0:nch], in1=g3(gG)[:, :, 63], op=ALU.subtract),
              R=(GR,), W=(GR,))
        P.act(lambda h: h.activation(out=gE[:, 0:nch], in_=gE[:, 0:nch], func=AF.Exp), R=(GR,), W=(GR,))
        P.dve(lambda h: h.tensor_tensor(out=gA[:, 0:N], in0=gA[:, 0:N], in1=gX[:, 0:N], op=ALU.subtract), R=(GR,), W=(GR,))
        P.act(lambda h: h.activation(out=gA[:, 0:N], in_=gA[:, 0:N], func=AF.Exp), R=(GR,), W=(GR,))
        P.dve(lambda h: h.tensor_tensor(out=gX[:, 0:N], in0=gX[:, 0:N], in1=gB[:, 0:N], op=ALU.add), R=(GR,), W=(GR,))
        P.act(lambda h: h.activation(out=Bc.ap, in_=gB[:, N - 1:N], func=AF.Copy), R=(GR,), W=(Bc,))
        P.act(lambda h: h.activation(out=Gc.ap, in_=gG[:, N - 1:N], func=AF.Copy), R=(GR,), W=(Gc,))
        if DBG.get('stop') == 3:
            P.release(m0)
            return
        pi = nextps()
        for b in range(nb):
            P.pe(lambda h, b=b, pi=pi: h.transpose(out=PS(pi, 128, 4, b * 4), in_=gA[0:4, b * 128:(b + 1) * 128],
                                                   identity=ident.ap[0:4, 0:4]), R=(GR, ident), W=(ps[pi],))
        P.dve(lambda h, pi=pi: h.tensor_copy(out=wk_tm.ap[:, 0:nb, :], in_=PS(pi, 128, 4 * nb).rearrange("p (b f) -> p b f", f=4)),
              R=(ps[pi],), W=(wk_tm,))
        pi = nextps()
        for hh in range(4):
            P.pe(lambda h, hh=hh, pi=pi: h.matmul(PS(pi, 128, nch, hh * 8), lhsT=sel.ap[0:4, hh * 128:(hh + 1) * 128],
                                                  rhs=gE[0:4, 0:nch], start=True, stop=True), R=(sel, GR), W=(ps[pi],))
        P.dve(lambda h, pi=pi: h.tensor_copy(out=dec_b.ap[:, :, 0:nch],
                                             in_=PS(pi, 128, 32).rearrange("p (a c) -> p a c", c=8)[:, :, 0:nch]),
              R=(ps[pi],), W=(dec_b,))
        for pr in range(2):
            pi = nextps()
            for b in range(nb):
                bs = min(128, N - b * 128)
                P.pe(lambda h, b=b, bs=bs, pi=pi, pr=pr: h.transpose(
                    out=PSB(pi)[0:bs, b * 128:(b + 1) * 128], in_=rkT.ap[:, pr, b * 128:b * 128 + bs], identity=identb.ap),
                    R=(rkT, identb), W=(ps[pi],))
            for b in range(nb):
                bs = min(128, N - b * 128)
                P.dve(lambda h, b=b, bs=bs, pi=pi, pr=pr: h.tensor_tensor(
                    out=rk_tm.ap[0:bs, b, pr * 128:(pr + 1) * 128], in0=PSB(pi)[0:bs, b * 128:(b + 1) * 128],
                    in1=kdec.ap[0:bs, pr, :], op=ALU.mult), R=(ps[pi], kdec), W=(rk_tm,))

        if DBG.get('stop') == 4:
            P.release(m0)
            return
        DBG.get('phase_hook', lambda n, p: None)('m0_heads_ml', P)
        def headnorm_out(src, hidx, gcol, gate):
            fm_norm([(src, src.ap[:, 0:N])], N, 128.0, LN_EPS, True)
            P.dve(lambda h: h.scalar_tensor_tensor(out=mixed.ap[:, hidx, 0:N], in0=src.ap[:, 0:N], scalar=gcol,
                                                   in1=gate, op0=ALU.mult, op1=ALU.mult),
                  R=(src, vecp, sigo, silg), W=(mixed,))

        def run_interleaved(gens):
            live = list(gens)
            while live:
                nxt = []
                for g_ in live:
                    try:
                        next(g_)
                        nxt.append(g_)
                    except StopIteration:
                        pass
                live = nxt

        def mlstm_head(hh, sl, PIN, PDN):
            kp_tm, MW, PT = kp_tmL[sl], MWL[sl], PTL[sl]
            pi = nextps()
            for b in range(nb):
                bs = min(128, N - b * 128)
                P.pe(lambda h: h.transpose(out=PSB(pi)[0:bs, b * 128:(b + 1) * 128], in_=kTm.ap[:, hh, b * 128:b * 128 + bs],
                                           identity=identb.ap), R=(kTm, identb), W=(ps[pi],))
            for b in range(nb):
                bs = min(128, N - b * 128)
                P.dve(lambda h: h.tensor_scalar(out=kp_tm.ap[0:bs, b, :], in0=PSB(pi)[0:bs, b * 128:(b + 1) * 128],
                                                scalar1=wk_tm.ap[0:bs, b, hh:hh + 1], scalar2=None, op0=ALU.mult),
                      R=(ps[pi], wk_tm), W=(kp_tm,))
            P.pool(lambda h: h.tensor_tensor(out=MW.ap[:, 0:nb, :], in0=maskml.ap[:, None, :].broadcast_to([128, nb, 64]),
                                             in1=wk_tm.ap[:, 0:nb, hh:hh + 1].broadcast_to([128, nb, 64]), op=ALU.mult),
                   R=(maskml, wk_tm), W=(MW,))
            yield
            psc = nextps()
            for c in range(nch):
                b, hf = c // 2, c % 2
                P.pe(lambda h: h.matmul(PS(psc, 64, 64, b * 64, hf * 64), lhsT=kTm.ap[:, hh, c * 64:(c + 1) * 64],
                                        rhs=qTm.ap[:, hh, c * 64:(c + 1) * 64], start=True, stop=True),
                     R=(kTm, qTm), W=(ps[psc],))
            P.dve(lambda h: h.tensor_tensor(out=PT.ap[:, 0:nb, :], in0=PS(psc, 128, nb * 64).rearrange("p (b t) -> p b t", t=64),
                                            in1=MW.ap[:, 0:nb, :], op=ALU.mult), R=(ps[psc], MW), W=(PT,))
            yield
            for c in range(nch):
                b, hf = c // 2, c % 2
                r0 = hf * 64
                cs = slice(c * 64, (c + 1) * 64)
                dcol = dec_b.ap[:, hh, c:c + 1]
                Cold = CaugH[hh][ccur[hh]]
                Cnew = CaugH[hh][1 - ccur[hh]]
                ccur[hh] = 1 - ccur[hh]
                Cb = CbL[sl][c % 3]
                P.act(lambda h: h.activation(out=Cb.ap, in_=Cold.ap, func=AF.Identity, scale=dcol), R=(Cold, dec_b), W=(Cb,))
                pdc = nextps()
                P.pe(lambda h: h.matmul(PS(pdc, 128, 128), lhsT=kp_tm.ap[r0:r0 + 64, b, :],
                                        rhs=mv_tm.ap[r0:r0 + 64, b, hh * 128:(hh + 1) * 128], start=True, stop=True),
                     R=(kp_tm, mv_tm), W=(ps[pdc],))
                P.pe(lambda h: h.matmul(PS(pdc, 128, 128, 128), lhsT=kp_tm.ap[r0:r0 + 64, b, :], rhs=onesb.ap[r0:r0 + 64, :],
                                        start=True, stop=True), R=(kp_tm, onesb), W=(ps[pdc],))
                P.dve(lambda h: h.scalar_tensor_tensor(out=Cnew.ap, in0=Cold.ap, scalar=dcol, in1=PS(pdc, 128, 256),
                                                       op0=ALU.mult, op1=ALU.add), R=(Cold, dec_b, ps[pdc]), W=(Cnew,))
                P.pe(lambda h: h.matmul(PS(PIN, 128, 64, cs.start), lhsT=Cb.ap[:, 0:128], rhs=qTm.ap[:, hh, cs],
                                        start=True, stop=False), R=(Cb, qTm), W=(ps[PIN],))
                P.pe(lambda h: h.matmul(PS(PIN, 128, 64, cs.start), lhsT=mv_tm.ap[r0:r0 + 64, b, hh * 128:(hh + 1) * 128],
                                        rhs=PT.ap[r0:r0 + 64, b, :], start=False, stop=True), R=(mv_tm, PT), W=(ps[PIN],))
                P.pe(lambda h: h.matmul(PS(PDN, 128, 64, cs.start), lhsT=Cb.ap[:, 128:256], rhs=qTm.ap[:, hh, cs],
                                        start=True, stop=False), R=(Cb, qTm), W=(ps[PDN],))
                P.pe(lambda h: h.matmul(PS(PDN, 128, 64, cs.start), lhsT=onesb.ap[r0:r0 + 64, :], rhs=PT.ap[r0:r0 + 64, b, :],
                                        start=False, stop=True), R=(onesb, PT), W=(ps[PDN],))
                yield
            pe_ = nextps()
            P.pe(lambda h: h.matmul(PS(pe_, 128, N), lhsT=sel.ap[0:4, hh * 128:(hh + 1) * 128], rhs=gX[0:4, 0:N],
                                    start=True, stop=True), R=(sel, GR), W=(ps[pe_],))
            P.act(lambda h: h.activation(out=T[0].ap[:, 0:N], in_=PS(pe_, 128, N), func=AF.Exp, scale=-1.0),
                  R=(ps[pe_],), W=(T[0],))
            P.act(lambda h: h.activation(out=T[1].ap[:, 0:N], in_=PS(PDN, 128, N), func=AF.Abs), R=(ps[PDN],), W=(T[1],))
            P.dve(lambda h: h.tensor_tensor(out=T[1].ap[:, 0:N], in0=T[1].ap[:, 0:N], in1=T[0].ap[:, 0:N], op=ALU.max),
                  R=(T[1], T[0]), W=(T[1],))
            P.act(lambda h: h.activation(out=T[1].ap[:, 0:N], in_=T[1].ap[:, 0:N], func=AF.Ln), R=(T[1],), W=(T[1],))
            P.act(lambda h: h.activation(out=T[1].ap[:, 0:N], in_=T[1].ap[:, 0:N], func=AF.Exp, scale=-1.0), R=(T[1],), W=(T[1],))
            hbuf = hT[sl]
            P.dve(lambda h: h.tensor_tensor(out=hbuf.ap[:, 0:N], in0=PS(PIN, 128, N), in1=T[1].ap[:, 0:N], op=ALU.mult),
                  R=(ps[PIN], T[1]), W=(hbuf,))
            yield
            headnorm_out(hbuf, hh, vecp.ap[:, 0, hh:hh + 1], sigo.ap[:, hh, 0:N])

        for h0 in (0, 2):
            run_interleaved([mlstm_head(h0, 0, 4, 5), mlstm_head(h0 + 1, 1, 6, 7)])

        DBG.get('phase_hook', lambda n, p: None)('m0_heads_ret', P)
        def ret_head(hh, PIN):
            pr, ho = hh // 2, (hh % 2) * 64
            PT = PTL[hh]
            psc = nextps()
            for c in range(nch):
                b, hf = c // 2, c % 2
                P.pe(lambda h: h.matmul(PS(psc, 64, 64, b * 64, hf * 64), lhsT=rkT.ap[ho:ho + 64, pr, c * 64:(c + 1) * 64],
                                        rhs=rqT.ap[ho:ho + 64, pr, c * 64:(c + 1) * 64], start=True, stop=True),
                     R=(rkT, rqT), W=(ps[psc],))
            P.dve(lambda h: h.tensor_tensor(out=PT.ap[:, 0:nb, :], in0=PS(psc, 128, nb * 64).rearrange("p (b t) -> p b t", t=64),
                                            in1=mret.ap[:, hh:hh + 1, :].broadcast_to([128, nb, 64]), op=ALU.mult),
                  R=(ps[psc], mret), W=(PT,))
            yield
            for c in range(nch):
                b, hf = c // 2, c % 2
                r0 = hf * 64
                cs = slice(c * 64, (c + 1) * 64)
                Sold = SH[hh][scur[hh]]
                Snew = SH[hh][1 - scur[hh]]
                scur[hh] = 1 - scur[hh]
                Sb = SbL[hh][c % 3]
                P.act(lambda h: h.activation(out=Sb.ap[ho:ho + 64, :], in_=Sold.ap[ho:ho + 64, :], func=AF.Copy),
                      R=(Sold,), W=(Sb,))
                pdc = nextps()
                P.pe(lambda h: h.matmul(PS(pdc, 64, 128, 0, ho), lhsT=rk_tm.ap[r0:r0 + 64, b, pr * 128 + ho:pr * 128 + ho + 64],
                                        rhs=rv_tm.ap[r0:r0 + 64, b, hh * 128:(hh + 1) * 128], start=True, stop=True),
                     R=(rk_tm, rv_tm), W=(ps[pdc],))
                P.dve(lambda h: h.scalar_tensor_tensor(out=Snew.ap[ho:ho + 64, :], in0=Sold.ap[ho:ho + 64, :],
                                                       scalar=float(GAM[hh] ** 64), in1=PS(pdc, 64, 128, 0, ho),
                                                       op0=ALU.mult, op1=ALU.add), R=(Sold, ps[pdc]), W=(Snew,))
                P.pe(lambda h: h.matmul(PS(PIN, 128, 64, cs.start), lhsT=Sb.ap[:, :], rhs=rqd.ap[:, pr, cs],
                                        start=True, stop=False), R=(Sb, rqd), W=(ps[PIN],))
                P.pe(lambda h: h.matmul(PS(PIN, 128, 64, cs.start), lhsT=rv_tm.ap[r0:r0 + 64, b, hh * 128:(hh + 1) * 128],
                                        rhs=PT.ap[r0:r0 + 64, b, :], start=False, stop=True), R=(rv_tm, PT), W=(ps[PIN],))
                yield
            hbuf = hT[hh % 2]
            P.act(lambda h: h.activation(out=hbuf.ap[:, 0:N], in_=PS(PIN, 128, N), func=AF.Copy), R=(ps[PIN],), W=(hbuf,))
            headnorm_out(hbuf, 4 + hh, vecp.ap[:, 0, 4 + hh:5 + hh], silg.ap[:, hh, 0:N])

        run_interleaved([ret_head(hh, 4 + hh) for hh in range(4)])

        DBG.get('phase_hook', lambda n, p: None)('m0_oproj', P)
        for g in range(2):
            slot, w = wload("OA", g)
            for j in range(4):
                m = g * 4 + j
                pi = nextps()
                for kc in range(8):
                    P.pe(lambda h, kc=kc, j=j, pi=pi, w=w: h.matmul(
                        PS(pi, 128, N), lhsT=w[:, kc, j * 128:(j + 1) * 128], rhs=mixed.ap[:, kc, 0:N],
                        start=(kc == 0), stop=(kc == 7)), R=(slot, mixed), W=(ps[pi],))
                resid_chunk(m, pi, N)
        residual_ln(N, 0, 0)
        P.release(m0)

    krT2 = P.alloc([128, 4096], BF16)
    KT_d = dscr("kt_scr", [8, 128, 4096])
    V_d = dscr("v_scr", [4096 + 128, 1024])
    kvo = [Obj() for _ in range(9)]
    kvsem = [P.new_dma_sem() for _ in range(2)]
    tsem2 = P.new_dma_sem()

    def kv_expand(ckvT, N, kpos0):
        nb = (N + 127) // 128
        reg = kvo[kpos0 // 512]
        m0 = P.mark()
        Kst = P.alloc([128, 8, 512], BF16)
        Vst = P.alloc([128, 4, 1024], BF16)
        slot, w = wload("UKV", 0)
        for hh in range(8):
            pi = nextps()
            for kc in range(2):
                P.pe(lambda h, kc=kc: h.matmul(PS(pi, 128, N), lhsT=w[:, kc, hh * 128:(hh + 1) * 128], rhs=ckvT.ap[:, kc, 0:N],
                                               start=(kc == 0), stop=(kc == 1)), R=(slot, ckvT), W=(ps[pi],))
            if hh % 2 == 0:
                P.act(lambda h: h.activation(out=Kst.ap[:, hh, 0:N], in_=PS(pi, 128, N), func=AF.Copy), R=(ps[pi],), W=(Kst,))
            else:
                P.dve(lambda h: h.tensor_copy(out=Kst.ap[:, hh, 0:N], in_=PS(pi, 128, N)), R=(ps[pi],), W=(Kst,))
        P.store(KT_d[:, :, kpos0:kpos0 + N].rearrange("h d n -> d h n"), Kst.ap[:, :, 0:N], R=(Kst,), W=(reg,))
        slot, w = wload("UKV", 1)
        for b in range(nb):
            bs = min(128, N - b * 128)
            for half in range(2):
                pi = nextps()
                for kc in range(2):
                    P.pe(lambda h, kc=kc: h.matmul(PS(pi, bs, 512), lhsT=ckvT.ap[:, kc, b * 128:b * 128 + bs],
                                                   rhs=w[:, kc, half * 512:(half + 1) * 512],
                                                   start=(kc == 0), stop=(kc == 1)), R=(slot, ckvT), W=(ps[pi],))
                if half == 0:
                    P.act(lambda h: h.activation(out=Vst.ap[0:bs, b, 0:512], in_=PS(pi, bs, 512), func=AF.Copy),
                          R=(ps[pi],), W=(Vst,))
                else:
                    P.dve(lambda h: h.tensor_copy(out=Vst.ap[0:bs, b, 512:1024], in_=PS(pi, bs, 512)), R=(ps[pi],), W=(Vst,))
        if N % 128 == 0:
            P.store(V_d[kpos0:kpos0 + N, :].rearrange("(b p) c -> p b c", p=128), Vst.ap[:, 0:nb, :], R=(Vst,), W=(reg,))
        else:
            P.store(V_d[kpos0:kpos0 + N, :], Vst.ap[0:N, 0, :], R=(Vst,), W=(reg,))
        P.release(m0)

    def tm_to_T(ckv_b, kr_b, ckvT, b, bs, kpos):
        pi = nextps()
        for kc in range(2):
            P.pe(lambda h, kc=kc: h.transpose(out=PSB(pi)[:, kc * 128:kc * 128 + bs], in_=ckv_b.ap[0:bs, kc * 128:(kc + 1) * 128],
                                              identity=identb.ap[0:bs, 0:bs]), R=(ckv_b, identb), W=(ps[pi],))
        P.pe(lambda h: h.transpose(out=PSB(pi)[:, 256:256 + bs], in_=kr_b.ap[0:bs, :], identity=identb.ap[0:bs, 0:bs]),
             R=(kr_b, identb), W=(ps[pi],))
        P.act(lambda h: h.activation(out=ckvT.ap[:, :, b * 128:b * 128 + bs],
                                     in_=PSB(pi)[:, 0:256].rearrange("p (k c) -> p k c", k=2)[:, :, 0:bs], func=AF.Copy),
              R=(ps[pi],), W=(ckvT,))
        P.dve(lambda h: h.tensor_copy(out=krT2.ap[:, kpos:kpos + bs], in_=PSB(pi)[:, 256:256 + bs]), R=(ps[pi],), W=(krT2,))

    def kv_prefill(s):
        for c in range(PAST // 512):
            m0 = P.mark()
            cin = P.alloc([128, 4, 256], F32)
            kin = P.alloc([128, 4, 64], F32)
            P.load(cin.ap, I["cckv"][s, c * 512:(c + 1) * 512, :].rearrange("(b p) f -> p b f", p=128), tsem2, W=(cin,))
            P.load(kin.ap, I["ckr"][s, c * 512:(c + 1) * 512, :].rearrange("(b p) f -> p b f", p=128), tsem2, W=(kin,))
            ckvT = P.alloc([128, 2, 512], BF16)
            for b in range(4):
                ckv_b = P.alloc([128, 256], BF16)
                kr_b = P.alloc([128, 128], BF16)
                P.act(lambda h: h.activation(out=ckv_b.ap, in_=cin.ap[:, b, :], func=AF.Copy), R=(cin,), W=(ckv_b,))
                P.dve(lambda h: h.tensor_copy(out=kr_b.ap.rearrange("p (a c) -> p a c", a=2),
                                              in_=kin.ap[:, b:b + 1, :].broadcast_to([128, 2, 64])), R=(kin,), W=(kr_b,))
                tm_to_T(ckv_b, kr_b, ckvT, b, 128, c * 512 + b * 128)
            kv_expand(ckvT, 512, c * 512)
            P.release(m0)

    def mix1(N, pos0, kpos0, is_prompt, okv, okr, s, orow0):
        nb = (N + 127) // 128
        m0 = P.mark()
        qnT = P.alloc([128, 8, 512], BF16)
        qrT = P.alloc([128, 8, 512], BF16)
        P.pool(lambda h: h.memset(qrT.ap, 0.0), W=(qrT,))
        oT = P.alloc([128, 8, 512], BF16)
        m1 = P.mark()
        cqT = P.alloc([128, 4, 512], F32)
        cqTb = P.alloc([128, 4, 512], BF16)
        ckvT = P.alloc([128, 2, 512], BF16)
        ctm = P.alloc([128, 4, 32], F32)
        stm_ = P.alloc([128, 4, 32], F32)
        P.load(cosb.ap[:, 0:N], I["cosF"][:, pos0:pos0 + N], tsem, W=(cosb,))
        P.load(sinb.ap[:, 0:N], I["sinF"][:, pos0:pos0 + N], tsem, W=(sinb,))
        if N % 128 == 0:
            P.load(ctm.ap[:, 0:nb, :], I["cosT"][pos0:pos0 + N, :].rearrange("(b p) j -> p b j", p=128), tsem2, W=(ctm,))
            P.load(stm_.ap[:, 0:nb, :], I["sinT"][pos0:pos0 + N, :].rearrange("(b p) j -> p b j", p=128), tsem2, W=(stm_,))
        else:
            P.load(ctm.ap[0:N, 0, :], I["cosT"][pos0:pos0 + N, :], tsem2, W=(ctm,))
            P.load(stm_.ap[0:N, 0, :], I["sinT"][pos0:pos0 + N, :], tsem2, W=(stm_,))
        slot, w = wload("C", 0)
        for m in range(4):
            pi = nextps()
            for kc in range(8):
                P.pe(lambda h, kc=kc: h.matmul(PS(pi, 128, N), lhsT=w[:, kc, m * 128:(m + 1) * 128], rhs=xTb.ap[:, kc, 0:N],
                                               start=(kc == 0), stop=(kc == 7)), R=(slot, xTb), W=(ps[pi],))
            P.act(lambda h: h.activation(out=cqT.ap[:, m, 0:N], in_=PS(pi, 128, N), func=AF.Copy), R=(ps[pi],), W=(cqT,))
        fm_norm([(cqT, cqT.ap[:, m, 0:N]) for m in range(4)], N, 512.0, RMS_EPS, False)
        for m in range(4):
            P.act(lambda h: h.activation(out=cqTb.ap[:, m, 0:N], in_=cqT.ap[:, m, 0:N], func=AF.Identity,
                                         scale=vecp.ap[:, 1, m:m + 1]), R=(cqT, vecp), W=(cqTb,))
        DBG.get('phase_hook', lambda n, p: None)('m1_ckv', P)
        slot, w = wload("C", 1)
        for b in range(nb):
            bs = min(128, N - b * 128)
            mb = P.mark()
            ckvo = P.alloc([128, 256], F32)
            kro = P.alloc([128, 64], F32)
            ckv_b = P.alloc([128, 256], BF16)
            kr_b = P.alloc([128, 128], BF16)
            sm = P.alloc([128, 8], F32)
            rt = P.alloc([128, 4, 32], F32)
            pi = nextps()
            for kc in range(8):
                P.pe(lambda h, kc=kc: h.matmul(PS(pi, bs, 320), lhsT=xTb.ap[:, kc, b * 128:b * 128 + bs], rhs=w[:, kc, 0:320],
                                               start=(kc == 0), stop=(kc == 7)), R=(slot, xTb), W=(ps[pi],))
            P.act(lambda h: h.activation(out=ckvo.ap[0:bs, :], in_=PS(pi, bs, 256), func=AF.Square, accum_out=sm.ap[0:bs, 0:1]),
                  R=(ps[pi],), W=(ckvo, sm))
            P.act(lambda h: h.activation(out=sm.ap[0:bs, 1:2], in_=sm.ap[0:bs, 0:1], func=AF.Ln, scale=1.0 / 256,
                                         bias=epsb[RMS_EPS].ap[0:bs, 0:1]), R=(sm, epsb[RMS_EPS]), W=(sm,))
            P.act(lambda h: h.activation(out=sm.ap[0:bs, 2:3], in_=sm.ap[0:bs, 1:2], func=AF.Exp, scale=-0.5), R=(sm,), W=(sm,))
            P.dve(lambda h: h.scalar_tensor_tensor(out=ckvo.ap[0:bs, :], in0=PS(pi, bs, 256), scalar=sm.ap[0:bs, 2:3],
                                                   in1=gkvb.ap[0:bs, :], op0=ALU.mult, op1=ALU.mult),
                  R=(ps[pi], sm, gkvb), W=(ckvo,))
            P.store(O[okv][s, orow0 + b * 128:orow0 + b * 128 + bs, :], ckvo.ap[0:bs, :], R=(ckvo,))
            P.act(lambda h: h.activation(out=ckv_b.ap[0:bs, :], in_=ckvo.ap[0:bs, :], func=AF.Copy), R=(ckvo,), W=(ckv_b,))
            x1 = PS(pi, bs, 32, 256)
            x2 = PS(pi, bs, 32, 288)
            cs_, sn_ = ctm.ap[0:bs, b, :], stm_.ap[0:bs, b, :]
            for j, (xa, tb_) in enumerate(((x1, cs_), (x2, sn_), (x2, cs_), (x1, sn_))):
                P.dve(lambda h, j=j, xa=xa, tb_=tb_: h.tensor_tensor(out=rt.ap[0:bs, j, :], in0=xa, in1=tb_, op=ALU.mult),
                      R=(ps[pi], ctm, stm_), W=(rt,))
            P.dve(lambda h: h.tensor_tensor(out=kro.ap[0:bs, 0:32], in0=rt.ap[0:bs, 0, :], in1=rt.ap[0:bs, 1, :], op=ALU.subtract),
                  R=(rt,), W=(kro,))
            P.dve(lambda h: h.tensor_tensor(out=kro.ap[0:bs, 32:64], in0=rt.ap[0:bs, 2, :], in1=rt.ap[0:bs, 3, :], op=ALU.add),
                  R=(rt,), W=(kro,))
            P.store(O[okr][s, orow0 + b * 128:orow0 + b * 128 + bs, :], kro.ap[0:bs, :], R=(kro,))
            P.act(lambda h: h.activation(out=kr_b.ap[0:bs, :].rearrange("p (a c) -> p a c", a=2),
                                         in_=kro.ap[0:bs, None, :].broadcast_to([bs, 2, 64]), func=AF.Copy), R=(kro,), W=(kr_b,))
            tm_to_T(ckv_b, kr_b, ckvT, b, bs, kpos0 + b * 128)
            P.release(mb)
        DBG.get('phase_hook', lambda n, p: None)('m1_kvexp', P)
        kv_expand(ckvT, N, kpos0)
        slot, w = wload("UQ", 0)
        for hh in range(8):
            pi = nextps()
            for kc in range(4):
                P.pe(lambda h, kc=kc: h.matmul(PS(pi, 128, N), lhsT=w[:, kc, hh * 128:(hh + 1) * 128], rhs=cqTb.ap[:, kc, 0:N],
                                               start=(kc == 0), stop=(kc == 3)), R=(slot, cqTb), W=(ps[pi],))
            if hh % 2 == 0:
                P.act(lambda h: h.activation(out=qnT.ap[:, hh, 0:N], in_=PS(pi, 128, N), func=AF.Copy), R=(ps[pi],), W=(qnT,))
            else:
                P.dve(lambda h: h.tensor_copy(out=qnT.ap[:, hh, 0:N], in_=PS(pi, 128, N)), R=(ps[pi],), W=(qnT,))
        slot, w = wload("UQ", 1)
        wsw = wswap(slot, w, 4, 512)
        for pr in range(4):
            pa = nextps()
            for kc in range(4):
                P.pe(lambda h, kc=kc: h.matmul(PS(pa, 128, N), lhsT=w[:, kc, pr * 128:(pr + 1) * 128], rhs=cqTb.ap[:, kc, 0:N],
                                               start=(kc == 0), stop=(kc == 3)), R=(slot, cqTb), W=(ps[pa],))
            pb = nextps()
            for kc in range(4):
                P.pe(lambda h, kc=kc: h.matmul(PS(pb, 128, N), lhsT=wsw[:, kc, pr * 128:(pr + 1) * 128],
                                               rhs=cqTb.ap[:, kc, 0:N], start=(kc == 0), stop=(kc == 3)),
                     R=(slot, cqTb), W=(ps[pb],))
            P.dve(lambda h: h.tensor_tensor(out=T[0].ap[:, 0:N], in0=PS(pa, 128, N), in1=cosb.ap[:, 0:N], op=ALU.mult),
                  R=(ps[pa], cosb), W=(T[0],))
            P.dve(lambda h: h.tensor_tensor(out=T[1].ap[:, 0:N], in0=PS(pb, 128, N), in1=sinb.ap[:, 0:N], op=ALU.mult),
                  R=(ps[pb], sinb), W=(T[1],))
            P.pool(lambda h: h.tensor_tensor(out=qrT.ap[0:64, 2 * pr, 0:N], in0=T[0].ap[0:64, 0:N], in1=T[1].ap[0:64, 0:N],
                                             op=ALU.add), R=(T[0], T[1]), W=(qrT,))
            P.pool(lambda h: h.tensor_tensor(out=qrT.ap[64:128, 2 * pr + 1, 0:N], in0=T[0].ap[64:128, 0:N],
                                             in1=T[1].ap[64:128, 0:N], op=ALU.add), R=(T[0], T[1]), W=(qrT,))
        P.release(m1)
        DBG.get('phase_hook', lambda n, p: None)('m1_attn', P)
        stageL = [P.alloc([128, 4096], F32) for _ in range(2)]
        PbL = [P.alloc([128, 4096], BF16) for _ in range(2)]
        KTh = P.alloc([128, 4096], BF16)
        VhL = [P.alloc([128, 32, 128], BF16) for _ in range(2)]
        PTs = [P.alloc([128, 4, 128], BF16) for _ in range(4)]
        o_tm = P.alloc([128, 4, 1024], BF16)
        sm2L = [P.alloc([128, 16], F32) for _ in range(2)]
        qrr = [0]
        nk = kpos0 + N
        nreg = (nk + 511) // 512
        kregs = tuple(kvo[i] for i in range(nreg))
        ptrr = [0]
        porr = [0]
        iters = [(hh, qb) for hh in range(8) for qb in range(nb)]

        def stage_a(it):
            hh, qb = iters[it]
            pr, ho = hh // 2, (hh % 2) * 64
            Vh = VhL[hh % 2]
            if qb == 0:
                P.load(KTh.ap[:, 0:nk], KT_d[hh, :, 0:nk], kvsem[0], R=kregs, W=(KTh,))
                nfull = nk // 128
                P.load(Vh.ap[:, 0:nfull, :], V_d[0:nfull * 128, hh * 128:(hh + 1) * 128].rearrange("(j p) c -> p j c", p=128),
                       kvsem[1], R=kregs, W=(Vh,))
                if nk % 128:
                    rem = nk % 128
                    P.load(Vh.ap[0:rem, nfull, :], V_d[nfull * 128:nk, hh * 128:(hh + 1) * 128], kvsem[1], R=kregs, W=(Vh,))
            bq = min(128, N - qb * 128)
            qs = slice(qb * 128, qb * 128 + bq)
            nvis = kpos0 + (qb + 1) * 128 if is_prompt else nk
            stage, Pb, sm2 = stageL[it % 2], PbL[it % 2], sm2L[it % 2]
            ng = (nvis + 511) // 512
            for g in range(ng):
                gs = min(512, nvis - g * 512)
                pi = nextps()
                P.pe(lambda h: h.matmul(PS(pi, bq, gs), lhsT=qnT.ap[:, hh, qs], rhs=KTh.ap[:, g * 512:g * 512 + gs],
                                        start=True, stop=False), R=(qnT, KTh), W=(ps[pi],))
                P.pe(lambda h: h.matmul(PS(pi, bq, gs), lhsT=qrT.ap[:, hh, qs],
                                        rhs=krT2.ap[:, g * 512:g * 512 + gs], start=False, stop=True),
                     R=(qrT, krT2), W=(ps[pi],))
                if g == ng - 1:
                    P.act(lambda h: h.activation(out=stage.ap[0:bq, g * 512:g * 512 + gs], in_=PS(pi, bq, gs), func=AF.Copy),
                          R=(ps[pi],), W=(stage,))
                    if is_prompt:
                        P.pool(lambda h: h.memset(stage.ap[0:64, nvis - 64:nvis], NEG), W=(stage,))
                else:
                    P.dve(lambda h: h.tensor_scalar(out=stage.ap[0:bq, g * 512:g * 512 + gs], in0=PS(pi, bq, gs),
                                                    scalar1=1.0, scalar2=None, op0=ALU.mult, op1=ALU.max,
                                                    accum_out=sm2.ap[0:bq, 4 + g:5 + g]), R=(ps[pi],), W=(stage, sm2))

        def stage_a2(it):
            hh, qb = iters[it]
            bq = min(128, N - qb * 128)
            nvis = kpos0 + (qb + 1) * 128 if is_prompt else nk
            stage, Pb, sm2 = stageL[it % 2], PbL[it % 2], sm2L[it % 2]
            ng = (nvis + 511) // 512
            g = ng - 1
            gs = min(512, nvis - g * 512)
            P.dve(lambda h: h.reduce_max(out=sm2.ap[0:bq, 4 + g:5 + g], in_=stage.ap[0:bq, g * 512:g * 512 + gs],
                                         axis=AX.X), R=(stage,), W=(sm2,))
            P.dve(lambda h: h.reduce_max(out=sm2.ap[0:bq, 0:1], in_=sm2.ap[0:bq, 4:4 + ng], axis=AX.X), R=(sm2,), W=(sm2,))
            P.dve(lambda h: h.tensor_scalar(out=sm2.ap[0:bq, 1:2], in0=sm2.ap[0:bq, 0:1], scalar1=-MLA_SCALE, scalar2=None,
                                            op0=ALU.mult), R=(sm2,), W=(sm2,))
            P.act(lambda h: h.activation(out=Pb.ap[0:bq, 0:nvis], in_=stage.ap[0:bq, 0:nvis], func=AF.Exp, scale=MLA_SCALE,
                                         bias=sm2.ap[0:bq, 1:2], accum_out=sm2.ap[0:bq, 2:3]), R=(stage, sm2), W=(Pb, sm2))
            P.dve(lambda h: h.reciprocal(out=sm2.ap[0:bq, 3:4], in_=sm2.ap[0:bq, 2:3]), R=(sm2,), W=(sm2,))

        def stage_b(it):
            hh, qb = iters[it]
            Vh = VhL[hh % 2]
            bq = min(128, N - qb * 128)
            nvis = kpos0 + (qb + 1) * 128 if is_prompt else nk
            Pb, sm2 = PbL[it % 2], sm2L[it % 2]
            nkb = (nvis + 127) // 128
            po = 4
            groups = [(k0, min(nkb, k0 + 4)) for k0 in range(0, nkb, 4)]
            slots = []

            def emit_T(gi):
                k0, k1 = groups[gi]
                pt = 5 + ptrr[0] % 3
                pts = PTs[ptrr[0] % len(PTs)]
                ptrr[0] += 1
                for kb in range(k0, k1):
                    kbs = min(128, nvis - kb * 128)
                    j = kb - k0
                    P.pe(lambda h: h.transpose(out=PSB(pt)[0:kbs, j * 128:j * 128 + bq], in_=Pb.ap[0:bq, kb * 128:kb * 128 + kbs],
                                               identity=identb.ap[0:bq, 0:bq]), R=(Pb, identb), W=(ps[pt],))
                nj = k1 - k0
                src = PSB(pt)[:, 0:nj * 128].rearrange("p (j c) -> p j c", j=nj)[:, :, 0:bq]
                if gi % 2 == 0:
                    P.dve(lambda h: h.tensor_copy(out=pts.ap[:, 0:nj, 0:bq], in_=src), R=(ps[pt],), W=(pts,))
                else:
                    P.act(lambda h: h.activation(out=pts.ap[:, 0:nj, 0:bq], in_=src, func=AF.Copy), R=(ps[pt],), W=(pts,))
                slots.append(pts)

            def emit_PV(gi):
                k0, k1 = groups[gi]
                pts = slots[gi]
                for kb in range(k0, k1):
                    kbs = min(128, nvis - kb * 128)
                    j = kb - k0
                    P.pe(lambda h: h.matmul(PS(po, bq, 128), lhsT=pts.ap[0:kbs, j, 0:bq], rhs=Vh.ap[0:kbs, kb, :],
                                            start=(kb == 0), stop=(kb == nkb - 1)), R=(pts, Vh), W=(ps[po],))

            LA = 2
            for gi in range(min(LA, len(groups))):
                emit_T(gi)
            for gi in range(len(groups)):
                if gi + LA < len(groups):
                    emit_T(gi + LA)
                emit_PV(gi)
            P.act(lambda h: h.activation(out=o_tm.ap[0:bq, qb, hh * 128:(hh + 1) * 128], in_=PS(po, bq, 128), func=AF.Identity,
                                         scale=sm2.ap[0:bq, 3:4]), R=(ps[po], sm2), W=(o_tm,))

        stage_a(0)
        stage_a2(0)
        for it in range(len(iters)):
            if it + 1 < len(iters):
                stage_a(it + 1)
            stage_b(it)
            if it + 1 < len(iters):
                stage_a2(it + 1)
        DBG.get('phase_hook', lambda n, p: None)('m1_oT', P)
        for qb in range(nb):
            bq = min(128, N - qb * 128)
            for half in range(2):
                pt = 5 + ptrr[0] % 3
                ptrr[0] += 1
                for j in range(4):
                    m = half * 4 + j
                    P.pe(lambda h, j=j, m=m: h.transpose(out=PSB(pt)[:, j * 128:j * 128 + bq],
                                                         in_=o_tm.ap[0:bq, qb, m * 128:(m + 1) * 128],
                                                         identity=identb.ap[0:bq, 0:bq]), R=(o_tm, identb), W=(ps[pt],))
                src = PSB(pt)[:, 0:512].rearrange("p (j c) -> p j c", j=4)[:, :, 0:bq]
                if half == 0:
                    P.act(lambda h, src=src: h.activation(out=oT.ap[:, 0:4, qb * 128:qb * 128 + bq], in_=src, func=AF.Copy),
                          R=(ps[pt],), W=(oT,))
                else:
                    P.dve(lambda h, src=src: h.tensor_copy(out=oT.ap[:, 4:8, qb * 128:qb * 128 + bq], in_=src),
                          R=(ps[pt],), W=(oT,))
        for g in range(2):
            slot, w = wload("OC", g)
            for j in range(4):
                m = g * 4 + j
                pi = nextps()
                for kc in range(8):
                    P.pe(lambda h, kc=kc: h.matmul(PS(pi, 128, N), lhsT=w[:, kc, j * 128:(j + 1) * 128], rhs=oT.ap[:, kc, 0:N],
                                                   start=(kc == 0), stop=(kc == 7)), R=(slot, oT), W=(ps[pi],))
                resid_chunk(m, pi, N)
        residual_ln(N, 1, 0)
        P.release(m0)

    for s in range(n_pseq):
        if "mix0" in stages:
            state_zero()
        for t in range(n_ptiles):
            N = 512
            _ph = DBG.get("phase_hook", lambda n, p: None)
            _ph("load", P)
            if (t, "ld") not in DBG.get("skip", ()):
                load_x(I["xp"][s, t * N:(t + 1) * N, :], N)
            _ph("mix0", P)
            if "mix0" in stages:
                mix0(N, t * N)
                if t == n_ptiles - 1:
                    state_store("pC", "pn", "pm", "pS", s)
            _ph("ffn0", P)
            if "ffn0" in stages:
                ffn(N, 0)
            _ph("mix1", P)
            if "mix1" in stages:
                mix1(N, t * N, t * N, True, "pckv", "pkr", s, t * N)
            _ph("ffn1", P)
            if "ffn1" in stages:
                ffn(N, 1)
            _ph("store", P)
            if (t, "st") not in DBG.get("skip", ()):
                store_y(O["yp"][s, t * N:(t + 1) * N, :], N)
    DBG.get("phase_hook", lambda n, p: None)("sample", P)
    for s in range(n_sseq):
        N = DEC_SEQ
        if DBG.get("s_load", True):
            load_x(I["xs"][s, :, :], N)
        if "mix0" in stages:
            state_load(s)
            mix0(N, SEQ)
            state_store("sC", "sn", "sm", "sS", s)
        if "ffn0" in stages:
            ffn(N, 0)
        if "mix1" in stages:
            kv_prefill(s)
            mix1(N, SEQ, PAST, False, "sckv", "skr", s, 0)
        if "ffn1" in stages:
            ffn(N, 1)
        if DBG.get("s_store", True):
            store_y(O["ys"][s, :, :], N)

    P.finish(stack)
    stack.close()
    return nc, P


def _consts():
    c = {}
    c["ident"] = np.eye(128, dtype=np.float32)
    half = 32
    inv = (10000.0 ** (-np.arange(half, dtype=np.float32) / half)).astype(np.float32)
    pos = np.concatenate([np.arange(SEQ), PAST + np.arange(DEC_SEQ)]).astype(np.float32)
    ang = (pos[:, None] * inv[None, :]).astype(np.float32)
    cos = np.cos(ang).astype(np.float32)
    sin = np.sin(ang).astype(np.float32)
    c["cosT"] = cos
    c["sinT"] = sin
    p = np.arange(128)
    j = (p % 64) % 32
    sign = np.where((p % 64) < 32, -1.0, 1.0).astype(np.float32)
    c["cosF"] = np.ascontiguousarray(cos[:, j].T)
    c["sinF"] = np.ascontiguousarray((sin[:, j] * sign[None, :]).T)
    s_idx = (p % 64)[:, None]
    t_idx = np.arange(64)[None, :]
    c["mask_ml"] = (s_idx <= t_idx).astype(np.float32)
    gam = (1.0 - 2.0 ** (-5.0 - np.arange(4, dtype=np.float64)))
    mret = np.zeros((128, 4, 64), np.float32)
    for h in range(4):
        rel = t_idx - s_idx
        mret[:, h, :] = np.where(rel >= 0, gam[h] ** np.maximum(rel, 0), 0.0) * (64.0 ** -0.5)
    c["mret"] = mret
    kdec = np.zeros((128, 2, 128), np.float32)
    for pr in range(2):
        for jj in range(128):
            h = 2 * pr + jj // 64
            kdec[:, pr, jj] = gam[h] ** (63 - (p % 64)) * (64.0 ** -0.5)
    c["kdec"] = kdec
    qdec = np.zeros((128, 2, 64), np.float32)
    for pr in range(2):
        for pp in range(128):
            h = 2 * pr + pp // 64
            qdec[pp, pr, :] = gam[h] ** (np.arange(64) + 1.0)
    c["qdec"] = qdec
    sel = np.zeros((4, 4, 128), np.float32)
    for h in range(4):
        sel[h, h, :] = 1.0
    c["sel"] = sel
    dm = np.zeros((128, 128), np.float32)
    dm[0:64, 64:128] = NEG
    c["dmask"] = dm
    return c


_CACHE = {}


def kernel(**inp):
    f = lambda a: np.ascontiguousarray(np.asarray(a, dtype=np.float32))
    key = "full"
    if key not in _CACHE:
        _CACHE[key] = build()
    nc, _ = _CACHE[key]
    cst = _consts()
    ln = np.concatenate([f(inp["ln_mix_g"]), f(inp["ln_mix_b"]), f(inp["ln_ffn_g"]), f(inp["ln_ffn_b"])], axis=0)
    shared = {
        "w_in_a": f(inp["w_in_a"]), "b_if": f(inp["b_if_a"]).reshape(8), "g_ml": f(inp["g_ml"]).reshape(512),
        "g_ret": f(inp["g_ret"]).reshape(512), "w_out_a": f(inp["w_out_a"]), "w_in_c": f(inp["w_in_c"]),
        "g_q": f(inp["g_q"]).reshape(512), "g_kv": f(inp["g_kv"]).reshape(256), "w_uq": f(inp["w_uq"]),
        "w_ukv": f(inp["w_ukv"]), "w_out_c": f(inp["w_out_c"]), "ln": ln,
        "w_gu": f(inp["w_gu"]), "w_down": f(inp["w_down"]),
    }
    shared.update(cst)
    in_maps = []
    for c in range(NCORES):
        sl = slice(2 * c, 2 * c + 2)
        m = dict(shared)
        m["xp"] = f(inp["x_prompt"][sl])
        m["xs"] = f(inp["x_sample"][sl])
        m["stC"] = f(inp["state_mlstm_C"][0, sl])
        m["stn"] = f(inp["state_mlstm_n"][0, sl])
        m["stm"] = f(inp["state_mlstm_m"][0, sl])
        m["stS"] = f(inp["state_ret_S"][0, sl])
        m["cckv"] = f(inp["cache_ckv"][0, sl])
        m["ckr"] = f(inp["cache_krope"][0, sl])
        in_maps.append(m)
    res = run_bass_kernel_spmd(nc, in_maps, core_ids=list(range(NCORES)))
    R = res.results
    cat = lambda k: np.concatenate([np.asarray(r[k], dtype=np.float32) for r in R], axis=0)
    outs = (cat("yp"), cat("ys"),
            cat("pC")[None], cat("pn")[None], cat("pm")[None], cat("pS")[None], cat("pckv")[None], cat("pkr")[None],
            cat("sC")[None], cat("sn")[None], cat("sm")[None], cat("sS")[None], cat("sckv")[None], cat("skr")[None])
    return outs
```

```python
import math
import numpy as np
import concourse.bass as bass
import concourse.mybir as mybir
from concourse.bass_utils import run_bass_kernel_spmd

F32 = mybir.dt.float32
BF16 = mybir.dt.bfloat16
AF = mybir.ActivationFunctionType
ALU = mybir.AluOpType
AX = mybir.AxisListType

NCORES = 8
D = 1024
SEQ = 4096
DEC_SEQ = 64
PAST = 2048
DFF = 2816
PROJ_A = 3592
ALPHA = 4.0 ** 0.25
LN_EPS = 1e-5
RMS_EPS = 1e-6
MLA_SCALE = 192.0 ** -0.5
NPOS = SEQ + DEC_SEQ
NEG = -1.0e30
DBG = {}


def _freeze(fn):
    if getattr(fn, "__closure__", None) is None:
        return fn
    import types
    cells = []
    for c in fn.__closure__:
        try:
            cells.append(types.CellType(c.cell_contents))
        except ValueError:
            cells.append(c)
    return types.FunctionType(fn.__code__, fn.__globals__, fn.__name__, fn.__defaults__, tuple(cells))


class Res:
    __slots__ = ("w", "r")

    def __init__(self):
        self.w = None
        self.r = {}


class Obj:
    def __init__(self, n=1):
        self.res = [Res() for _ in range(n)]


class Buf:
    def __init__(self, ap, res):
        self.ap = ap
        self.res = res

    def __getitem__(self, idx):
        return self.ap[idx]


class Eng:
    def __init__(self, name, key):
        self.name = name
        self.key = key
        self.cnt = 0
        self.known = {}
        self.ops = []


class Prog:
    BLK = 256

    def __init__(self, nc, arena_bytes):
        self.nc = nc
        self.engs = {n: Eng(n, i) for i, n in enumerate(["pe", "act", "dve", "pool", "sp"])}
        self.nsem = 5
        self.dma_cnt = {}
        self.dma_last = {}
        self.arena_bytes = arena_bytes
        self.arena = nc.alloc_sbuf_tensor("arena", [128, arena_bytes // 2], BF16)
        self.ares = [Res() for _ in range(arena_bytes // self.BLK)]
        self.top = 0
        self.psum = []
        for i in range(8):
            t = nc.alloc_psum_tensor("ps%d" % i, [128, 512], F32)
            o = Obj()
            o.t = t
            o.excl = True
            self.psum.append(o)
        self.store_sems = []
        self.store_rr = 0
        self.store_eng = "pool"

    def new_dma_sem(self):
        k = self.nsem
        self.nsem += 1
        self.dma_cnt[k] = 0
        return k

    def alloc(self, shape, dtype, at=None):
        esz = 4 if dtype == F32 else 2
        free = 1
        for s in shape[1:]:
            free *= s
        nbytes = free * esz
        nbytes_r = (nbytes + self.BLK - 1) // self.BLK * self.BLK
        if at is not None:
            off = at
            assert off % self.BLK == 0 and off + nbytes_r <= self.arena_bytes
        else:
            off = self.top
            self.top += nbytes_r
            self.peak = max(getattr(self, "peak", 0), self.top)
            assert self.top <= self.arena_bytes, "arena overflow %d" % self.top
        ap = self.arena[0:shape[0], off // 2:(off + nbytes) // 2]
        if dtype == F32:
            ap = ap.bitcast(F32)
        if len(shape) == 3:
            ap = ap.rearrange("p (a b) -> p a b", a=shape[1])
        elif len(shape) == 4:
            ap = ap.rearrange("p (a b c) -> p a b c", a=shape[1], b=shape[2])
        res = self.ares[off // self.BLK:(off + nbytes_r) // self.BLK]
        return Buf(ap, res)

    def mark(self):
        return self.top

    def release(self, m):
        self.top = m

    def emit(self, ename, fn, R=(), W=(), dma_sem=None, chain=False):
        eng = self.engs[ename]
        need = {}
        if any(getattr(b, "excl", False) for b in R):
            W = tuple(W) + tuple(b for b in R if getattr(b, "excl", False))
            R = tuple(b for b in R if not getattr(b, "excl", False))

        def req(s, v):
            if ename == "pe" and s == eng.key:
                return
            if eng.known.get(s, 0) >= v:
                return
            if need.get(s, 0) < v:
                need[s] = v

        for b in R:
            for r in b.res:
                if r.w is not None:
                    req(*r.w)
        for b in W:
            for r in b.res:
                if r.w is not None:
                    req(*r.w)
                for s, v in r.r.items():
                    req(s, v)
        if dma_sem is not None:
            lt = self.dma_last.get(dma_sem)
            if lt is not None and not chain:
                req(*lt)
            self.dma_cnt[dma_sem] += 16
            tok = (dma_sem, self.dma_cnt[dma_sem])
            self.dma_last[dma_sem] = tok
        else:
            eng.cnt += 1
            tok = (eng.key, eng.cnt)
        for s, v in need.items():
            eng.known[s] = v
        for b in R:
            for r in b.res:
                if r.r.get(tok[0], 0) < tok[1]:
                    r.r[tok[0]] = tok[1]
        for b in W:
            for r in b.res:
                r.w = tok
                r.r = {}
        eng.ops.append((_freeze(fn), sorted(need.items()), tok, dma_sem is not None))
        return tok

    def pe(self, fn, R=(), W=()):
        return self.emit("pe", fn, R, W)

    def act(self, fn, R=(), W=()):
        return self.emit("act", fn, R, W)

    def dve(self, fn, R=(), W=()):
        return self.emit("dve", fn, R, W)

    def pool(self, fn, R=(), W=()):
        return self.emit("pool", fn, R, W)

    def load(self, out_ap, in_ap, sem, R=(), W=(), **kw):
        return self.emit("sp", lambda h: h.dma_start(out=out_ap, in_=in_ap, **kw), R, W, dma_sem=sem)

    def store(self, out_ap, in_ap, R=(), W=(), **kw):
        sem = self.store_sems[self.store_rr % len(self.store_sems)]
        self.store_rr += 1
        return self.emit(self.store_eng, lambda h: h.dma_start(out=out_ap, in_=in_ap, **kw), R, W, dma_sem=sem)

    def finish(self, stack):
        nc = self.nc
        pool = self.engs["pool"]
        fin = []
        for s, v in self.dma_cnt.items():
            if v > 0 and pool.known.get(s, 0) < v:
                fin.append((s, v))
        sems = [stack.enter_context(nc.semaphore("s%d" % i)) for i in range(self.nsem)]
        block = stack.enter_context(nc.Block())

        def replay(ename, h, extra=()):
            for fn, waits, tok, is_dma in self.engs[ename].ops:
                for s, v in waits:
                    h.wait_ge(sems[s], v)
                fn(h).then_inc(sems[tok[0]], 16 if is_dma else 1)
            for s, v in extra:
                h.wait_ge(sems[s], v)

        @block.tensor
        def _(h):
            replay("pe", h)

        @block.scalar
        def _(h):
            replay("act", h)

        @block.vector
        def _(h):
            replay("dve", h)

        @block.gpsimd
        def _(h):
            replay("pool", h, fin)

        @block.sync
        def _(h):
            replay("sp", h)


def _wspec():
    S = {}
    a = []
    a.append([(0, 512)])
    a.append([(512, 512)])
    a.append([(1536, 512)])
    rq0, rk0 = 2056, 2312

    def swp(base):
        p = []
        for h in range(4):
            p.append((base + 64 * h + 32, 32))
            p.append((base + 64 * h, 32))
        return p
    a.append([(rq0, 256)])
    a.append([(rk0, 256)])
    a.append([(3080, 512)])
    a.append([(1024, 512)])
    a.append([(2568, 512)])
    a.append([(2048, 8)])
    S["A"] = ("w_in_a", 0, 1024, a)
    S["OA"] = ("w_out_a", 0, 1024, [[(0, 512)], [(512, 512)]])
    for l in range(2):
        g = []
        for grp in range(11):
            m0 = 2 * grp
            g.append([(m0 * 128, 256), (DFF + m0 * 128, 256)])
        S["GU%d" % l] = ("w_gu", l, 1024, g)
        S["DN%d" % l] = ("w_down", l, DFF, [[(m * 128, 128)] for m in range(8)])
    S["C"] = ("w_in_c", 0, 1024, [[(0, 512)], [(512, 320)]])
    uq0 = [(h * 192, 128) for h in range(8)]
    uq1 = [(h * 192 + 128, 64) for h in range(8)]
    S["UQ"] = ("w_uq", 0, 512, [uq0, uq1])
    S["UKV"] = ("w_ukv", 0, 256, [[(h * 256, 128) for h in range(8)], [(h * 256 + 128, 128) for h in range(8)]])
    S["OC"] = ("w_out_c", 0, 1024, [[(0, 512)], [(512, 512)]])
    return S


def _merge(pieces):
    out = []
    for c0, n in pieces:
        if out and out[-1][0] + out[-1][1] == c0:
            out[-1] = (out[-1][0], out[-1][1] + n)
        else:
            out.append((c0, n))
    return out


def build(n_ptiles=8, n_pseq=2, n_sseq=2, stages=("mix0", "ffn0", "mix1", "ffn1")):
    from contextlib import ExitStack
    nc = bass.Bass("TRN2", target_bir_lowering=False)
    stack = ExitStack()

    def din(name, shape, dt=F32):
        return nc.dram_tensor(name, list(shape), dt, kind="ExternalInput").ap()

    def dout(name, shape, dt=F32):
        return nc.dram_tensor(name, list(shape), dt, kind="ExternalOutput").ap()

    def dscr(name, shape, dt=BF16):
        return nc.dram_tensor(name, list(shape), dt, kind="Internal").ap()

    I = {}
    I["xp"] = din("xp", [2, SEQ, D])
    I["xs"] = din("xs", [2, DEC_SEQ, D])
    I["stC"] = din("stC", [2, 4, 128, 128])
    I["stn"] = din("stn", [2, 4, 128])
    I["stm"] = din("stm", [2, 4])
    I["stS"] = din("stS", [2, 4, 64, 128])
    I["cckv"] = din("cckv", [2, PAST, 256])
    I["ckr"] = din("ckr", [2, PAST, 64])
    I["w_in_a"] = din("w_in_a", [1, D, PROJ_A])
    I["b_if"] = din("b_if", [8])
    I["g_ml"] = din("g_ml", [512])
    I["g_ret"] = din("g_ret", [512])
    I["w_out_a"] = din("w_out_a", [1, D, D])
    I["w_in_c"] = din("w_in_c", [1, D, 832])
    I["g_q"] = din("g_q", [512])
    I["g_kv"] = din("g_kv", [256])
    I["w_uq"] = din("w_uq", [1, 512, 1536])
    I["w_ukv"] = din("w_ukv", [1, 256, 2048])
    I["w_out_c"] = din("w_out_c", [1, D, D])
    I["ln"] = din("ln", [8, D])
    I["w_gu"] = din("w_gu", [2, D, 2 * DFF])
    I["w_down"] = din("w_down", [2, DFF, D])
    I["ident"] = din("ident", [128, 128])
    I["cosF"] = din("cosF", [128, NPOS])
    I["sinF"] = din("sinF", [128, NPOS])
    I["cosT"] = din("cosT", [NPOS, 32])
    I["sinT"] = din("sinT", [NPOS, 32])
    I["mask_ml"] = din("mask_ml", [128, 64])
    I["mret"] = din("mret", [128, 4, 64])
    I["kdec"] = din("kdec", [128, 2, 128])
    I["qdec"] = din("qdec", [128, 2, 64])
    I["sel"] = din("sel", [4, 4, 128])
    I["dmask"] = din("dmask", [128, 128])

    O = {}
    O["yp"] = dout("yp", [2, SEQ, D])
    O["ys"] = dout("ys", [2, DEC_SEQ, D])
    O["pC"] = dout("pC", [2, 4, 128, 128])
    O["pn"] = dout("pn", [2, 4, 128])
    O["pm"] = dout("pm", [2, 4])
    O["pS"] = dout("pS", [2, 4, 64, 128])
    O["pckv"] = dout("pckv", [2, SEQ, 256])
    O["pkr"] = dout("pkr", [2, SEQ, 64])
    O["sC"] = dout("sC", [2, 4, 128, 128])
    O["sn"] = dout("sn", [2, 4, 128])
    O["sm"] = dout("sm", [2, 4])
    O["sS"] = dout("sS", [2, 4, 64, 128])
    O["sckv"] = dout("sckv", [2, DEC_SEQ, 256])
    O["skr"] = dout("skr", [2, DEC_SEQ, 64])

    P = Prog(nc, arena_bytes=207 * 1024)
    P.store_sems = [P.new_dma_sem() for _ in range(8)]
    ps = P.psum

    def PS(i, parts=128, cols=512, c0=0, p0=0):
        return ps[i].t.ap()[p0:p0 + parts, c0:c0 + cols]

    def PSB(i):
        return ps[i].t.ap().bitcast(BF16)

    spec = _wspec()
    WS = {}
    need = set()
    if "mix0" in stages:
        need |= {"A", "OA"}
    if "ffn0" in stages:
        need |= {"GU0", "DN0"}
    if "mix1" in stages:
        need |= {"C", "UQ", "UKV", "OC"}
    if "ffn1" in stages:
        need |= {"GU1", "DN1"}
    for name in ("A", "OA", "GU0", "DN0", "C", "UKV", "UQ", "OC", "GU1", "DN1"):
        (src, li, K, groups) = spec[name]
        if name not in need:
            continue
        M = I[src].shape[2]
        scr = dscr("ws_%s" % name, [K, M])
        o = Obj()
        for c0 in range(0, M, 2048):
            c1 = min(M, c0 + 2048)
            P.store(scr[:, c0:c1], I[src][li, :, c0:c1], R=(), W=(o,))
        WS[name] = (scr, o, K // 128, [_merge(g) for g in groups])

    sem_c = P.new_dma_sem()
    ident = P.alloc([128, 128], F32)
    P.load(ident.ap, I["ident"], sem_c, W=(ident,))
    identb = P.alloc([128, 128], BF16)
    P.dve(lambda h: h.tensor_copy(out=identb.ap, in_=ident.ap), R=(ident,), W=(identb,))
    onesb = P.alloc([128, 128], BF16)
    P.pool(lambda h: h.memset(onesb.ap, 1.0), W=(onesb,))
    lnp = P.alloc([128, 8, 8], F32)
    lnrow = P.alloc([8, 1024], F32, at=P.arena_bytes - 4096)
    P.load(lnrow.ap, I["ln"], sem_c, W=(lnrow,))
    for m in range(8):
        P.pe(lambda h, m=m: h.transpose(out=ps[0].t.ap()[:, m * 8:(m + 1) * 8], in_=lnrow.ap[0:8, m * 128:(m + 1) * 128],
                                        identity=ident.ap[0:8, 0:8]), R=(lnrow, ident), W=(ps[0],))
    P.dve(lambda h: h.tensor_copy(out=lnp.ap.rearrange("p w m -> p m w"), in_=ps[0].t.ap()[:, 0:64].rearrange("p (m w) -> p m w", m=8)),
          R=(ps[0],), W=(lnp,))

    NW = 3
    WSLOT = 4096
    wring = [P.alloc([128, WSLOT], BF16) for _ in range(NW)]
    wsem = [P.new_dma_sem() for _ in range(NW)]
    wrr = [0]

    def wload(name, gi):
        scr, o, KC, groups = WS[name]
        pieces = groups[gi]
        i = wrr[0] % NW
        wrr[0] += 1
        slot = wring[i]
        n = pieces[0][1]
        cnt = len(pieces)
        ncols = cnt * n
        assert KC * ncols <= WSLOT
        if cnt == 1:
            c0 = pieces[0][0]
            srcap = scr[:, c0:c0 + n].rearrange("(kc p) c -> p kc c", p=128)
            dstap = slot.ap[:, 0:KC * n].rearrange("p (k c) -> p k c", k=KC)
            v = slot.ap[:, 0:KC * n].rearrange("p (k c) -> p k c", k=KC)
            P.load(dstap, srcap, wsem[i], R=(o,), W=(slot,))
            return slot, v
        assert all(p_[1] == n for p_ in pieces)
        v = slot.ap[:, 0:KC * ncols].rearrange("p (k c) -> p k c", k=KC)
        for j, (c0, _) in enumerate(pieces):
            srcap = scr[:, c0:c0 + n].rearrange("(kc p) c -> p kc c", p=128)
            dstap = v[:, :, j * n:(j + 1) * n]
            if j == 0:
                P.load(dstap, srcap, wsem[i], R=(o,), W=(slot,))
            else:
                P.emit("sp", lambda h: h.dma_start(out=dstap, in_=srcap), (), (), dma_sem=wsem[i], chain=True)
        ftok = (wsem[i], P.dma_cnt[wsem[i]])
        for r in slot.res:
            r.w = ftok
        return slot, v

    def wswap(slot, w, KC, ncols):
        wsw = slot.ap[:, KC * ncols:2 * KC * ncols].rearrange("p (k c) -> p k c", k=KC)
        src = w.rearrange("p k (h t j) -> p k h t j", t=2, j=32)
        dst = wsw.rearrange("p k (h t j) -> p k h t j", t=2, j=32)
        P.dve(lambda h: h.tensor_copy(out=dst[:, :, :, 0, :], in_=src[:, :, :, 1, :]), R=(slot,), W=(slot,))
        P.act(lambda h: h.activation(out=dst[:, :, :, 1, :], in_=src[:, :, :, 0, :], func=AF.Copy), R=(slot,), W=(slot,))
        return wsw

    xT = P.alloc([128, 8, 512], F32)
    xTb = P.alloc([128, 8, 512], BF16)
    def subs(buf, n):
        k = len(buf.res) // n
        assert k * n == len(buf.res)
        return [Buf(buf.ap[:, i], buf.res[i * k:(i + 1) * k]) for i in range(n)]
    xTc = subs(xT, 8)
    xTbc = subs(xTb, 8)
    T = [P.alloc([128, 512], F32) for _ in range(4)]
    TB = [P.alloc([128, 512], BF16) for _ in range(4)]
    xsem = [P.new_dma_sem() for _ in range(2)]
    base_mark = P.mark()

    psrr = [0]

    def nextps(lo=0, hi=4):
        i = lo + psrr[0] % (hi - lo)
        psrr[0] += 1
        return i

    def load_x(src_rows, N):
        nb = (N + 127) // 128
        m0 = P.mark()
        for b in range(nb):
            bs = min(128, N - b * 128)
            xin = P.alloc([128, 1024], F32)
            P.load(xin.ap[0:bs, :], src_rows[b * 128:b * 128 + bs, :], xsem[b % 2], W=(xin,))
            for half in range(2):
                pi = nextps()
                for j in range(4):
                    m = half * 4 + j
                    P.pe(lambda h, m=m, j=j, pi=pi, xin=xin, bs=bs: h.transpose(
                        out=PS(pi, 128, 128, j * 128), in_=xin.ap[:, m * 128:(m + 1) * 128],
                        identity=ident.ap), R=(xin, ident), W=(ps[pi],))
                src = ps[pi].t.ap().rearrange("p (j c) -> p j c", j=4)[:, :, 0:bs]
                P.act(lambda h, src=src, half=half, b=b, bs=bs: h.activation(
                    out=xT.ap[:, half * 4:half * 4 + 4, b * 128:b * 128 + bs], in_=src, func=AF.Copy),
                    R=(ps[pi],), W=(xT,))
                P.dve(lambda h, src=src, half=half, b=b, bs=bs: h.tensor_copy(
                    out=xTb.ap[:, half * 4:half * 4 + 4, b * 128:b * 128 + bs], in_=src),
                    R=(ps[pi],), W=(xTb,))
            P.release(P.mark())
        if not DBG.get("norel"):
            P.release(m0)

    def store_y(dst_rows, N):
        nb = (N + 127) // 128
        m0 = P.mark()
        for b in range(nb):
            bs = min(128, N - b * 128)
            yo = P.alloc([128, 1024], F32)
            for half in range(2):
                pi = nextps()
                for j in range(4):
                    m = half * 4 + j
                    P.pe(lambda h, m=m, j=j, pi=pi, b=b, bs=bs: h.transpose(
                        out=PS(pi, 128, 128, j * 128), in_=xT.ap[:, m, b * 128:(b + 1) * 128],
                        identity=ident.ap), R=(xT, ident), W=(ps[pi],))
                if half == 0:
                    P.act(lambda h, pi=pi, yo=yo, bs=bs: h.activation(
                        out=yo.ap[0:bs, 0:512], in_=PS(pi, bs, 512), func=AF.Copy), R=(ps[pi],), W=(yo,))
                else:
                    P.dve(lambda h, pi=pi, yo=yo, bs=bs: h.tensor_copy(
                        out=yo.ap[0:bs, 512:1024], in_=PS(pi, bs, 512)), R=(ps[pi],), W=(yo,))
            P.store(dst_rows[b * 128:b * 128 + bs, :], yo.ap[0:bs, :], R=(yo,))
        if not DBG.get("norel"):
            P.release(m0)

    def fm_norm(chunks, N, F, eps, center, tmp=None):
        n = len(chunks)
        p2 = nextps()
        p1 = nextps() if center else None
        tmean, tvar, tsq, tzb = tmp if tmp is not None else (T[3], T[2], None, None)
        for i, (b, ap) in enumerate(chunks):
            sq = tsq if tsq is not None else TB[i % 2]
            P.act(lambda h: h.activation(out=sq.ap[:, 0:N], in_=ap, func=AF.Square), R=(b,), W=(sq,))
            P.pe(lambda h: h.matmul(PS(p2, 128, N), lhsT=onesb.ap, rhs=sq.ap[:, 0:N], start=(i == 0), stop=(i == n - 1)),
                 R=(sq, onesb), W=(ps[p2],))
            if center:
                zb = tzb if tzb is not None else TB[2 + i % 2]
                P.dve(lambda h: h.tensor_copy(out=zb.ap[:, 0:N], in_=ap), R=(b,), W=(zb,))
                P.pe(lambda h: h.matmul(PS(p1, 128, N), lhsT=onesb.ap, rhs=zb.ap[:, 0:N], start=(i == 0), stop=(i == n - 1)),
                     R=(zb, onesb), W=(ps[p1],))
        mean, var = tmean, tvar
        if center:
            P.act(lambda h: h.activation(out=mean.ap[:, 0:N], in_=PS(p1, 128, N), func=AF.Identity, scale=1.0 / F),
                  R=(ps[p1],), W=(mean,))
            P.act(lambda h: h.activation(out=var.ap[:, 0:N], in_=mean.ap[:, 0:N], func=AF.Square), R=(mean,), W=(var,))
            P.dve(lambda h: h.scalar_tensor_tensor(out=var.ap[:, 0:N], in0=PS(p2, 128, N), scalar=1.0 / F, in1=var.ap[:, 0:N],
                                                   op0=ALU.mult, op1=ALU.subtract), R=(ps[p2], var), W=(var,))
            P.dve(lambda h: h.tensor_scalar(out=var.ap[:, 0:N], in0=var.ap[:, 0:N], scalar1=0.0, scalar2=float(eps),
                                            op0=ALU.max, op1=ALU.add), R=(var,), W=(var,))
        else:
            P.dve(lambda h: h.tensor_scalar(out=var.ap[:, 0:N], in0=PS(p2, 128, N), scalar1=1.0 / F, scalar2=float(eps),
                                            op0=ALU.mult, op1=ALU.add), R=(ps[p2],), W=(var,))
        P.act(lambda h: h.activation(out=var.ap[:, 0:N], in_=var.ap[:, 0:N], func=AF.Ln), R=(var,), W=(var,))
        P.act(lambda h: h.activation(out=var.ap[:, 0:N], in_=var.ap[:, 0:N], func=AF.Exp, scale=-0.5), R=(var,), W=(var,))
        for i, (b, ap) in enumerate(chunks):
            if center:
                P.dve(lambda h: h.tensor_tensor(out=ap, in0=ap, in1=mean.ap[:, 0:N], op=ALU.subtract), R=(b, mean), W=(b,))
            P.dve(lambda h: h.tensor_tensor(out=ap, in0=ap, in1=var.ap[:, 0:N], op=ALU.mult), R=(b, var), W=(b,))

    epsb = {}
    for e in (LN_EPS, RMS_EPS):
        eb = P.alloc([128, 1], F32)
        P.pool(lambda h, eb=eb, e=e: h.memset(eb.ap, e), W=(eb,))
        epsb[e] = eb
    base_mark = P.mark()

    def ln_affine(N, gi, bi):
        for m in range(8):
            P.act(lambda h: h.activation(out=xTb.ap[:, m, 0:N], in_=xT.ap[:, m, 0:N], func=AF.Identity,
                                         scale=lnp.ap[:, gi, m:m + 1], bias=lnp.ap[:, bi, m:m + 1]),
                  R=(xTc[m], lnp), W=(xTbc[m],))
            P.act(lambda h: h.activation(out=xT.ap[:, m, 0:N], in_=xT.ap[:, m, 0:N], func=AF.Identity,
                                         scale=lnp.ap[:, gi, m:m + 1], bias=lnp.ap[:, bi, m:m + 1]),
                  R=(xTc[m], lnp), W=(xTc[m],))

    ln_pending = []
    LN1, LN2 = 4, 5

    def resid_chunk(m, pi, N):
        while ln_pending:
            ln_pending.pop(0)()
        zb, sq = TB[2 + m % 2], TB[m % 2]
        P.dve(lambda h: h.scalar_tensor_tensor(out=xT.ap[:, m, 0:N], in0=xT.ap[:, m, 0:N], scalar=ALPHA, in1=PS(pi, 128, N),
                                               op0=ALU.mult, op1=ALU.add), R=(xTc[m], ps[pi]), W=(xTc[m],))
        P.dve(lambda h: h.tensor_copy(out=zb.ap[:, 0:N], in_=xT.ap[:, m, 0:N]), R=(xTc[m],), W=(zb,))
        P.act(lambda h: h.activation(out=sq.ap[:, 0:N], in_=xT.ap[:, m, 0:N], func=AF.Square), R=(xTc[m],), W=(sq,))
        def stat_mm(m=m, zb=zb, sq=sq, N=N):
            P.pe(lambda h: h.matmul(PS(LN1, 128, N), lhsT=onesb.ap, rhs=zb.ap[:, 0:N], start=(m == 0), stop=(m == 7)),
                 R=(zb, onesb), W=(ps[LN1],))
            P.pe(lambda h: h.matmul(PS(LN2, 128, N), lhsT=onesb.ap, rhs=sq.ap[:, 0:N], start=(m == 0), stop=(m == 7)),
                 R=(sq, onesb), W=(ps[LN2],))
        ln_pending.append(stat_mm)

    def residual_ln(N, lay, which):
        while ln_pending:
            ln_pending.pop(0)()
        F = float(D)
        gi = (0 if which == 0 else 4) + lay
        bi = gi + 2
        mean, var = T[3], T[2]
        P.act(lambda h: h.activation(out=mean.ap[:, 0:N], in_=PS(LN1, 128, N), func=AF.Identity, scale=1.0 / F),
              R=(ps[LN1],), W=(mean,))
        P.act(lambda h: h.activation(out=var.ap[:, 0:N], in_=mean.ap[:, 0:N], func=AF.Square), R=(mean,), W=(var,))
        P.dve(lambda h: h.scalar_tensor_tensor(out=var.ap[:, 0:N], in0=PS(LN2, 128, N), scalar=1.0 / F, in1=var.ap[:, 0:N],
                                               op0=ALU.mult, op1=ALU.subtract), R=(ps[LN2], var), W=(var,))
        P.dve(lambda h: h.tensor_scalar(out=var.ap[:, 0:N], in0=var.ap[:, 0:N], scalar1=0.0, scalar2=float(LN_EPS),
                                        op0=ALU.max, op1=ALU.add), R=(var,), W=(var,))
        P.act(lambda h: h.activation(out=var.ap[:, 0:N], in_=var.ap[:, 0:N], func=AF.Ln), R=(var,), W=(var,))
        P.act(lambda h: h.activation(out=var.ap[:, 0:N], in_=var.ap[:, 0:N], func=AF.Exp, scale=-0.5), R=(var,), W=(var,))
        for m in range(8):
            P.dve(lambda h: h.tensor_tensor(out=xT.ap[:, m, 0:N], in0=xT.ap[:, m, 0:N], in1=mean.ap[:, 0:N], op=ALU.subtract),
                  R=(xTc[m], mean), W=(xTc[m],))
            P.dve(lambda h: h.tensor_tensor(out=xT.ap[:, m, 0:N], in0=xT.ap[:, m, 0:N], in1=var.ap[:, 0:N], op=ALU.mult),
                  R=(xTc[m], var), W=(xTc[m],))
            P.act(lambda h: h.activation(out=xTb.ap[:, m, 0:N], in_=xT.ap[:, m, 0:N], func=AF.Identity,
                                         scale=lnp.ap[:, gi, m:m + 1], bias=lnp.ap[:, bi, m:m + 1]),
                  R=(xTc[m], lnp), W=(xTbc[m],))
        for m in range(8):
            P.act(lambda h: h.activation(out=xT.ap[:, m, 0:N], in_=xT.ap[:, m, 0:N], func=AF.Identity,
                                         scale=lnp.ap[:, gi, m:m + 1], bias=lnp.ap[:, bi, m:m + 1]),
                  R=(xTc[m], lnp), W=(xTc[m],))

    def ffn(N, lay):
        m0 = P.mark()
        aT = P.alloc([128, 22, 512], BF16)
        gu = "GU%d" % lay
        for grp in range(11):
            slot, w = wload(gu, grp)
            for j in range(2):
                m = 2 * grp + j
                pg = nextps()
                pu = nextps()
                for kc in range(8):
                    P.pe(lambda h, kc=kc, j=j, w=w, pg=pg: h.matmul(
                        PS(pg, 128, N), lhsT=w[:, kc, j * 128:(j + 1) * 128], rhs=xTb.ap[:, kc, 0:N],
                        start=(kc == 0), stop=(kc == 7)), R=(slot, xTb), W=(ps[pg],))
                for kc in range(8):
                    P.pe(lambda h, kc=kc, j=j, w=w, pu=pu: h.matmul(
                        PS(pu, 128, N), lhsT=w[:, kc, 256 + j * 128:256 + (j + 1) * 128], rhs=xTb.ap[:, kc, 0:N],
                        start=(kc == 0), stop=(kc == 7)), R=(slot, xTb), W=(ps[pu],))
                sg = T[m % 2]
                P.act(lambda h, sg=sg, pg=pg: h.activation(out=sg.ap[:, 0:N], in_=PS(pg, 128, N), func=AF.Silu),
                      R=(ps[pg],), W=(sg,))
                P.dve(lambda h, sg=sg, pu=pu, m=m: h.tensor_tensor(
                    out=aT.ap[:, m, 0:N], in0=PS(pu, 128, N), in1=sg.ap[:, 0:N], op=ALU.mult),
                    R=(ps[pu], sg), W=(aT,))
        dn = "DN%d" % lay
        for m in range(8):
            slot, w = wload(dn, m)
            pi = nextps()
            for kc in range(22):
                P.pe(lambda h, kc=kc, w=w, pi=pi: h.matmul(
                    PS(pi, 128, N), lhsT=w[:, kc, :], rhs=aT.ap[:, kc, 0:N],
                    start=(kc == 0), stop=(kc == 21)), R=(slot, aT), W=(ps[pi],))
            resid_chunk(m, pi, N)
        residual_ln(N, lay, 1)
        P.release(m0)

    vrow = P.alloc([2, 1024], F32, at=P.arena_bytes - 8192)
    P.load(vrow.ap[0:1, 0:512], I["g_ml"].rearrange("(o n) -> o n", o=1), sem_c, W=(vrow,))
    P.load(vrow.ap[0:1, 512:1024], I["g_ret"].rearrange("(o n) -> o n", o=1), sem_c, W=(vrow,))
    P.load(vrow.ap[1:2, 0:512], I["g_q"].rearrange("(o n) -> o n", o=1), sem_c, W=(vrow,))
    P.load(vrow.ap[1:2, 512:1024], I["g_q"].rearrange("(o n) -> o n", o=1), sem_c, W=(vrow,))
    vecp = P.alloc([128, 2, 8], F32)
    for m in range(8):
        P.pe(lambda h, m=m: h.transpose(out=ps[1].t.ap()[:, m * 2:(m + 1) * 2], in_=vrow.ap[0:2, m * 128:(m + 1) * 128],
                                        identity=ident.ap[0:2, 0:2]), R=(vrow, ident), W=(ps[1],))
    P.dve(lambda h: h.tensor_copy(out=vecp.ap.rearrange("p w m -> p m w"),
                                  in_=ps[1].t.ap()[:, 0:16].rearrange("p (m w) -> p m w", m=8)), R=(ps[1],), W=(vecp,))
    bif = P.alloc([4, 2], F32)
    P.load(bif.ap, I["b_if"].rearrange("(t h) -> h t", t=2), sem_c, W=(bif,), allow_slow_non_contiguous=True)
    nbf = P.alloc([4, 1], F32)
    P.dve(lambda h: h.tensor_scalar(out=nbf.ap, in0=bif.ap[:, 1:2], scalar1=-1.0, scalar2=None, op0=ALU.mult),
          R=(bif,), W=(nbf,))
    ones4 = P.alloc([4, 512], F32)
    P.pool(lambda h: h.memset(ones4.ap, 1.0), W=(ones4,))
    sel = P.alloc([4, 512], F32)
    P.load(sel.ap, I["sel"].rearrange("k h c -> k (h c)"), sem_c, W=(sel,))
    maskml = P.alloc([128, 64], F32)
    P.load(maskml.ap, I["mask_ml"], sem_c, W=(maskml,))
    mret = P.alloc([128, 4, 64], F32)
    P.load(mret.ap, I["mret"], sem_c, W=(mret,))
    kdec = P.alloc([128, 2, 128], F32)
    P.load(kdec.ap, I["kdec"], sem_c, W=(kdec,))
    qdec = P.alloc([128, 2, 64], F32)
    P.load(qdec.ap, I["qdec"], sem_c, W=(qdec,))
    gkvb = P.alloc([128, 256], F32)
    P.load(gkvb.ap, I["g_kv"].partition_broadcast(128), sem_c, W=(gkvb,))
    dmask = P.alloc([128, 128], F32)
    P.load(dmask.ap, I["dmask"], sem_c, W=(dmask,))
    CaugH = [[P.alloc([128, 256], F32) for _ in range(2)] for _ in range(4)]
    SH = [[P.alloc([128, 128], F32) for _ in range(2)] for _ in range(4)]
    ccur = [0, 0, 0, 0]
    scur = [0, 0, 0, 0]
    Bc = P.alloc([4, 1], F32)
    Gc = P.alloc([4, 1], F32)
    cosb = P.alloc([128, 512], F32)
    sinb = P.alloc([128, 512], F32)
    tsem = P.new_dma_sem()
    GAM = [1.0 - 2.0 ** (-5.0 - h) for h in range(4)]

    def state_zero():
        for hh in range(4):
            ccur[hh] = 0
            scur[hh] = 0
            P.pool(lambda h: h.memset(CaugH[hh][0].ap, 0.0), W=(CaugH[hh][0],))
            P.pool(lambda h: h.memset(SH[hh][0].ap, 0.0), W=(SH[hh][0],))
        P.pool(lambda h: h.memset(Bc.ap, 0.0), W=(Bc,))
        P.pool(lambda h: h.memset(Gc.ap, 0.0), W=(Gc,))

    def state_load(s):
        m0 = P.mark()
        cin = P.alloc([128, 4, 128], F32)
        P.load(cin.ap, I["stC"][s].rearrange("h v d -> v h d"), tsem, W=(cin,))
        nrow = P.alloc([4, 128], F32)
        P.load(nrow.ap, I["stn"][s], tsem, W=(nrow,))
        pi = nextps()
        for hh in range(4):
            P.pe(lambda h: h.transpose(out=PS(pi, 128, 128, hh * 128), in_=cin.ap[:, hh, :], identity=ident.ap),
                 R=(cin, ident), W=(ps[pi],))
        pj = nextps()
        P.pe(lambda h: h.transpose(out=PS(pj, 128, 4), in_=nrow.ap[0:4, :], identity=ident.ap[0:4, 0:4]),
             R=(nrow, ident), W=(ps[pj],))
        ncol = P.alloc([128, 4], F32)
        P.act(lambda h: h.activation(out=ncol.ap, in_=PS(pj, 128, 4), func=AF.Copy), R=(ps[pj],), W=(ncol,))
        for hh in range(4):
            ccur[hh] = 0
            scur[hh] = 0
            ho = (hh % 2) * 64
            P.act(lambda h: h.activation(out=CaugH[hh][0].ap[:, 0:128], in_=PS(pi, 128, 128, hh * 128), func=AF.Copy),
                  R=(ps[pi],), W=(CaugH[hh][0],))
            P.dve(lambda h: h.tensor_copy(out=CaugH[hh][0].ap[:, 128:256], in_=ncol.ap[:, hh:hh + 1].broadcast_to([128, 128])),
                  R=(ncol,), W=(CaugH[hh][0],))
            P.load(SH[hh][0].ap[ho:ho + 64, :], I["stS"][s, hh], tsem, W=(SH[hh][0],))
        P.load(Gc.ap, I["stm"][s].rearrange("(h o) -> h o", o=1), tsem, W=(Gc,))
        P.pool(lambda h: h.memset(Bc.ap, 0.0), W=(Bc,))
        P.release(m0)

    def state_store(kC, kn, km, kS, s):
        m0 = P.mark()
        co = P.alloc([128, 4, 128], F32)
        no = P.alloc([128, 4], F32)
        pi = nextps()
        for hh in range(4):
            cb_ = CaugH[hh][ccur[hh]]
            P.pe(lambda h: h.transpose(out=PS(pi, 128, 128, hh * 128), in_=cb_.ap[:, 0:128], identity=ident.ap),
                 R=(cb_, ident), W=(ps[pi],))
            P.dve(lambda h: h.tensor_copy(out=no.ap[:, hh:hh + 1], in_=cb_.ap[:, 128:129]), R=(cb_,), W=(no,))
        P.act(lambda h: h.activation(out=co.ap, in_=ps[pi].t.ap().rearrange("p (h c) -> p h c", h=4), func=AF.Copy),
              R=(ps[pi],), W=(co,))
        P.store(O[kC][s].rearrange("h v d -> v h d"), co.ap, R=(co,))
        P.store(O[kn][s].rearrange("h d -> d h"), no.ap, R=(no,), allow_slow_non_contiguous=True)
        mo_ = P.alloc([4, 1], F32)
        P.dve(lambda h: h.tensor_tensor(out=mo_.ap, in0=Bc.ap, in1=Gc.ap, op=ALU.add), R=(Bc, Gc), W=(mo_,))
        P.store(O[km][s].rearrange("(h o) -> h o", o=1), mo_.ap, R=(mo_,))
        for hh in range(4):
            ho = (hh % 2) * 64
            sb_ = SH[hh][scur[hh]]
            P.store(O[kS][s, hh], sb_.ap[ho:ho + 64, :], R=(sb_,))
        P.release(m0)

    def mix0(N, pos0):
        nb = (N + 127) // 128
        nch = N // 64
        m0 = P.mark()
        qTm = P.alloc([128, 4, 512], BF16)
        kTm = P.alloc([128, 4, 512], BF16)
        sigo = P.alloc([128, 4, 512], F32)
        silg = P.alloc([128, 4, 512], F32)
        rqT = P.alloc([128, 2, 512], BF16)
        rqd = P.alloc([128, 2, 512], BF16)
        rkT = P.alloc([128, 2, 512], BF16)
        mv_tm = P.alloc([128, 4, 512], BF16)
        rv_tm = P.alloc([128, 4, 512], BF16)
        rk_tm = P.alloc([128, 4, 256], BF16)
        mixed = P.alloc([128, 8, 512], BF16)
        GR = P.alloc([4, 6, 512], F32)
        wk_tm = P.alloc([128, 4, 4], F32)
        dec_b = P.alloc([128, 4, 8], F32)
        kp_tmL = [P.alloc([128, 4, 128], BF16) for _ in range(2)]
        MWL = [P.alloc([128, 4, 64], F32) for _ in range(2)]
        PTL = [P.alloc([128, 4, 64], BF16) for _ in range(4)]
        CbL = [[P.alloc([128, 256], BF16) for _ in range(3)] for _ in range(2)]
        SbL = [[P.alloc([128, 128], BF16) for _ in range(3)] for _ in range(4)]
        for hh_ in range(4):
            for b_ in SbL[hh_]:
                P.pool(lambda h: h.memset(b_.ap, 0.0), W=(b_,))
        hT = [P.alloc([128, 512], F32) for _ in range(6)]
        t1L = [P.alloc([128, 512], F32) for _ in range(4)]
        ebL = [P.alloc([128, 512], F32) for _ in range(2)]
        ntmp = [(P.alloc([128, 512], F32), P.alloc([128, 512], F32), P.alloc([128, 512], BF16), P.alloc([128, 512], BF16))
                for _ in range(2)]
        nrr = [0]
        P.load(cosb.ap[:, 0:N], I["cosF"][:, pos0:pos0 + N], tsem, W=(cosb,))
        P.load(sinb.ap[:, 0:N], I["sinF"][:, pos0:pos0 + N], tsem, W=(sinb,))

        def fm_chain(w, c0, pi, slot):
            for kc in range(8):
                P.pe(lambda h, kc=kc: h.matmul(PS(pi, 128, N), lhsT=w[:, kc, c0:c0 + 128], rhs=xTb.ap[:, kc, 0:N],
                                               start=(kc == 0), stop=(kc == 7)), R=(slot, xTb), W=(ps[pi],))

        for grp, dst, fn, sc in ((0, qTm, AF.Identity, 1.0), (1, kTm, AF.Identity, 128.0 ** -0.5),
                                 (2, sigo, AF.Sigmoid, 1.0), (5, silg, AF.Silu, 1.0)):
            slot, w = wload("A", grp)
            for hh in range(4):
                pi = nextps()
                fm_chain(w, hh * 128, pi, slot)
                P.act(lambda h, hh=hh, pi=pi, dst=dst, fn=fn, sc=sc: h.activation(
                    out=dst.ap[:, hh, 0:N], in_=PS(pi, 128, N), func=fn, scale=sc), R=(ps[pi],), W=(dst,))
        for grp, dst in ((3, rqT), (4, rkT)):
            slot, w = wload("A", grp)
            wsw = wswap(slot, w, 8, 256)
            for pr in range(2):
                pa = nextps()
                fm_chain(w, pr * 128, pa, slot)
                pb = nextps()
                fm_chain(wsw, pr * 128, pb, slot)
                P.dve(lambda h, pa=pa: h.tensor_tensor(out=T[0].ap[:, 0:N], in0=PS(pa, 128, N), in1=cosb.ap[:, 0:N], op=ALU.mult),
                      R=(ps[pa], cosb), W=(T[0],))
                P.dve(lambda h, pb=pb: h.tensor_tensor(out=T[1].ap[:, 0:N], in0=PS(pb, 128, N), in1=sinb.ap[:, 0:N], op=ALU.mult),
                      R=(ps[pb], sinb), W=(T[1],))
                P.pool(lambda h: h.tensor_tensor(out=T[0].ap[:, 0:N], in0=T[0].ap[:, 0:N], in1=T[1].ap[:, 0:N], op=ALU.add),
                       R=(T[0], T[1]), W=(T[0],))
                P.act(lambda h, dst=dst, pr=pr: h.activation(out=dst.ap[:, pr, 0:N], in_=T[0].ap[:, 0:N], func=AF.Copy),
                      R=(T[0],), W=(dst,))
                if grp == 3:
                    P.dve(lambda h, pr=pr: h.tensor_tensor(
                        out=rqd.ap[:, pr, 0:N].rearrange("p (c t) -> p c t", t=64),
                        in0=T[0].ap[:, 0:N].rearrange("p (c t) -> p c t", t=64),
                        in1=qdec.ap[:, pr:pr + 1, :].broadcast_to([128, nch, 64]), op=ALU.mult),
                        R=(T[0], qdec), W=(rqd,))
        if DBG.get('stop') == 1:
            P.release(m0)
            return
        DBG.get('phase_hook', lambda n, p: None)('m0_vproj', P)
        for grp, dst in ((6, mv_tm), (7, rv_tm)):
            slot, w = wload("A", grp)
            for b in range(nb):
                bs = min(128, N - b * 128)
                pi = nextps()
                for kc in range(8):
                    P.pe(lambda h, kc=kc, b=b, bs=bs, pi=pi, w=w: h.matmul(
                        PS(pi, bs, 512), lhsT=xTb.ap[:, kc, b * 128:b * 128 + bs], rhs=w[:, kc, 0:512],
                        start=(kc == 0), stop=(kc == 7)), R=(slot, xTb), W=(ps[pi],))
                if b % 2 == 0:
                    P.act(lambda h, b=b, bs=bs, pi=pi, dst=dst: h.activation(out=dst.ap[0:bs, b, :], in_=PS(pi, bs, 512), func=AF.Copy),
                          R=(ps[pi],), W=(dst,))
                else:
                    P.dve(lambda h, b=b, bs=bs, pi=pi, dst=dst: h.tensor_copy(out=dst.ap[0:bs, b, :], in_=PS(pi, bs, 512)),
                          R=(ps[pi],), W=(dst,))
        if DBG.get('stop') == 2:
            P.release(m0)
            return
        DBG.get('phase_hook', lambda n, p: None)('m0_gates', P)
        slot, w = wload("A", 8)
        pig = nextps()
        pfg = nextps()
        for kc in range(8):
            P.pe(lambda h, kc=kc: h.matmul(PS(pig, 4, N), lhsT=w[:, kc, 0:4], rhs=xTb.ap[:, kc, 0:N],
                                           start=(kc == 0), stop=(kc == 7)), R=(slot, xTb), W=(ps[pig],))
        for kc in range(8):
            P.pe(lambda h, kc=kc: h.matmul(PS(pfg, 4, N), lhsT=w[:, kc, 4:8], rhs=xTb.ap[:, kc, 0:N],
                                           start=(kc == 0), stop=(kc == 7)), R=(slot, xTb), W=(ps[pfg],))
        gL, gB, gA, gG, gX, gE = [GR.ap[:, i, :] for i in range(6)]
        P.act(lambda h: h.activation(out=gL[:, 0:N], in_=PS(pfg, 4, N), func=AF.Exp, scale=-1.0, bias=nbf.ap[:, 0:1]),
              R=(ps[pfg], nbf), W=(GR,))
        P.act(lambda h: h.activation(out=gL[:, 0:N], in_=gL[:, 0:N], func=AF.Ln, bias=1.0, scale=1.0), R=(GR,), W=(GR,))
        P.dve(lambda h: h.tensor_tensor_scan(out=gB[:, 0:N], data0=ones4.ap[:, 0:N], data1=gL[:, 0:N], initial=Bc.ap[:, 0:1],
                                             op0=ALU.mult, op1=ALU.subtract), R=(GR, ones4, Bc), W=(GR,))
        P.dve(lambda h: h.scalar_tensor_tensor(out=gA[:, 0:N], in0=PS(pig, 4, N), scalar=bif.ap[:, 0:1], in1=gB[:, 0:N],
                                               op0=ALU.add, op1=ALU.subtract), R=(ps[pig], bif, GR), W=(GR,))
        P.dve(lambda h: h.tensor_tensor_scan(out=gG[:, 0:N], data0=ones4.ap[:, 0:N], data1=gA[:, 0:N], initial=Gc.ap[:, 0:1],
                                             op0=ALU.mult, op1=ALU.max), R=(GR, ones4, Gc), W=(GR,))
        g3 = lambda a: a[:, 0:N].rearrange("p (c t) -> p c t", t=64)
        P.act(lambda h: h.activation(out=g3(gX), in_=g3(gG)[:, :, 63:64].broadcast_to([4, nch, 64]), func=AF.Copy),
              R=(GR,), W=(GR,))
        P.dve(lambda h: h.tensor_copy(out=gE[:, 0:1], in_=Gc.ap[:, 0:1]), R=(Gc,), W=(GR,))
        if nch > 1:
            P.dve(lambda h: h.tensor_copy(out=gE[:, 1:nch], in_=g3(gG)[:, 0:nch - 1, 63]), R=(GR,), W=(GR,))
        P.dve(lambda h: h.tensor_tensor(out=gE[:, 0:nch], in0=gE[:, 0:nch], in1=g3(gG)[:, :, 63], op=ALU.subtract),
              R=(GR,), W=(GR,))
        P.act(lambda h: h.activation(out=gE[:, 0:nch], in_=gE[:, 0:nch], func=AF.Exp), R=(GR,), W=(GR,))
        P.dve(lambda h: h.tensor_tensor(out=gA[:, 0:N], in0=gA[:, 0:N], in1=gX[:, 0:N], op=ALU.subtract), R=(GR,), W=(GR,))
        P.act(lambda h: h.activation(out=gA[:, 0:N], in_=gA[:, 0:N], func=AF.Exp), R=(GR,), W=(GR,))
        P.dve(lambda h: h.tensor_tensor(out=gX[:, 0:N], in0=gX[:, 0:N], in1=gB[:, 0:N], op=ALU.add), R=(GR,), W=(GR,))
        P.act(lambda h: h.activation(out=Bc.ap, in_=gB[:, N - 1:N], func=AF.Copy), R=(GR,), W=(Bc,))
        P.act(lambda h: h.activation(out=Gc.ap, in_=gG[:, N - 1:N], func=AF.Copy), R=(GR,), W=(Gc,))
        if DBG.get('stop') == 3:
            P.release(m0)
            return
        pi = nextps()
        for b in range(nb):
            P.pe(lambda h, b=b, pi=pi: h.transpose(out=PS(pi, 128, 4, b * 4), in_=gA[0:4, b * 128:(b + 1) * 128],
                                                   identity=ident.ap[0:4, 0:4]), R=(GR, ident), W=(ps[pi],))
        P.dve(lambda h, pi=pi: h.tensor_copy(out=wk_tm.ap[:, 0:nb, :], in_=PS(pi, 128, 4 * nb).rearrange("p (b f) -> p b f", f=4)),
              R=(ps[pi],), W=(wk_tm,))
        pi = nextps()
        for hh in range(4):
            P.pe(lambda h, hh=hh, pi=pi: h.matmul(PS(pi, 128, nch, hh * 8), lhsT=sel.ap[0:4, hh * 128:(hh + 1) * 128],
                                                  rhs=gE[0:4, 0:nch], start=True, stop=True), R=(sel, GR), W=(ps[pi],))
        P.dve(lambda h, pi=pi: h.tensor_copy(out=dec_b.ap[:, :, 0:nch],
                                             in_=PS(pi, 128, 32).rearrange("p (a c) -> p a c", c=8)[:, :, 0:nch]),
              R=(ps[pi],), W=(dec_b,))
        for pr in range(2):
            pi = nextps()
            for b in range(nb):
                bs = min(128, N - b * 128)
                P.pe(lambda h, b=b, bs=bs, pi=pi, pr=pr: h.transpose(
                    out=PSB(pi)[0:bs, b * 128:(b + 1) * 128], in_=rkT.ap[:, pr, b * 128:b * 128 + bs], identity=identb.ap),
                    R=(rkT, identb), W=(ps[pi],))
            for b in range(nb):
                bs = min(128, N - b * 128)
                P.dve(lambda h, b=b, bs=bs, pi=pi, pr=pr: h.tensor_tensor(
                    out=rk_tm.ap[0:bs, b, pr * 128:(pr + 1) * 128], in0=PSB(pi)[0:bs, b * 128:(b + 1) * 128],
                    in1=kdec.ap[0:bs, pr, :], op=ALU.mult), R=(ps[pi], kdec), W=(rk_tm,))

        if DBG.get('stop') == 4:
            P.release(m0)
            return
        DBG.get('phase_hook', lambda n, p: None)('m0_heads_ml', P)
        def headnorm_out(src, hidx, gcol, gate):
            tmp = ntmp[nrr[0] % len(ntmp)]
            nrr[0] += 1
            fm_norm([(src, src.ap[:, 0:N])], N, 128.0, LN_EPS, True, tmp=tmp)
            P.dve(lambda h: h.scalar_tensor_tensor(out=mixed.ap[:, hidx, 0:N], in0=src.ap[:, 0:N], scalar=gcol,
                                                   in1=gate, op0=ALU.mult, op1=ALU.mult),
                  R=(src, vecp, sigo, silg), W=(mixed,))

        def run_interleaved(gens):
            live = list(gens)
            while live:
                nxt = []
                for g_ in live:
                    try:
                        next(g_)
                        nxt.append(g_)
                    except StopIteration:
                        pass
                live = nxt

        def mlstm_head(hh, sl, PIN, PDN):
            kp_tm, MW, PT = kp_tmL[sl], MWL[sl], PTL[sl]
            pi = nextps()
            for b in range(nb):
                bs = min(128, N - b * 128)
                P.pe(lambda h: h.transpose(out=PSB(pi)[0:bs, b * 128:(b + 1) * 128], in_=kTm.ap[:, hh, b * 128:b * 128 + bs],
                                           identity=identb.ap), R=(kTm, identb), W=(ps[pi],))
            for b in range(nb):
                bs = min(128, N - b * 128)
                P.dve(lambda h: h.tensor_scalar(out=kp_tm.ap[0:bs, b, :], in0=PSB(pi)[0:bs, b * 128:(b + 1) * 128],
                                                scalar1=wk_tm.ap[0:bs, b, hh:hh + 1], scalar2=None, op0=ALU.mult),
                      R=(ps[pi], wk_tm), W=(kp_tm,))
            P.pool(lambda h: h.tensor_tensor(out=MW.ap[:, 0:nb, :], in0=maskml.ap[:, None, :].broadcast_to([128, nb, 64]),
                                             in1=wk_tm.ap[:, 0:nb, hh:hh + 1].broadcast_to([128, nb, 64]), op=ALU.mult),
                   R=(maskml, wk_tm), W=(MW,))
            yield
            psc = nextps()
            for c in range(nch):
                b, hf = c // 2, c % 2
                P.pe(lambda h: h.matmul(PS(psc, 64, 64, b * 64, hf * 64), lhsT=kTm.ap[:, hh, c * 64:(c + 1) * 64],
                                        rhs=qTm.ap[:, hh, c * 64:(c + 1) * 64], start=True, stop=True),
                     R=(kTm, qTm), W=(ps[psc],))
            P.dve(lambda h: h.tensor_tensor(out=PT.ap[:, 0:nb, :], in0=PS(psc, 128, nb * 64).rearrange("p (b t) -> p b t", t=64),
                                            in1=MW.ap[:, 0:nb, :], op=ALU.mult), R=(ps[psc], MW), W=(PT,))
            yield
            for c in range(nch):
                b, hf = c // 2, c % 2
                r0 = hf * 64
                cs = slice(c * 64, (c + 1) * 64)
                dcol = dec_b.ap[:, hh, c:c + 1]
                Cold = CaugH[hh][ccur[hh]]
                Cnew = CaugH[hh][1 - ccur[hh]]
                ccur[hh] = 1 - ccur[hh]
                Cb = CbL[sl][c % 3]
                P.act(lambda h: h.activation(out=Cb.ap, in_=Cold.ap, func=AF.Identity, scale=dcol), R=(Cold, dec_b), W=(Cb,))
                pdc = nextps()
                P.pe(lambda h: h.matmul(PS(pdc, 128, 128), lhsT=kp_tm.ap[r0:r0 + 64, b, :],
                                        rhs=mv_tm.ap[r0:r0 + 64, b, hh * 128:(hh + 1) * 128], start=True, stop=True),
                     R=(kp_tm, mv_tm), W=(ps[pdc],))
                P.pe(lambda h: h.matmul(PS(pdc, 128, 128, 128), lhsT=kp_tm.ap[r0:r0 + 64, b, :], rhs=onesb.ap[r0:r0 + 64, :],
                                        start=True, stop=True), R=(kp_tm, onesb), W=(ps[pdc],))
                P.dve(lambda h: h.scalar_tensor_tensor(out=Cnew.ap, in0=Cold.ap, scalar=dcol, in1=PS(pdc, 128, 256),
                                                       op0=ALU.mult, op1=ALU.add), R=(Cold, dec_b, ps[pdc]), W=(Cnew,))
                P.pe(lambda h: h.matmul(PS(PIN, 128, 64, cs.start), lhsT=Cb.ap[:, 0:128], rhs=qTm.ap[:, hh, cs],
                                        start=True, stop=False), R=(Cb, qTm), W=(ps[PIN],))
                P.pe(lambda h: h.matmul(PS(PIN, 128, 64, cs.start), lhsT=mv_tm.ap[r0:r0 + 64, b, hh * 128:(hh + 1) * 128],
                                        rhs=PT.ap[r0:r0 + 64, b, :], start=False, stop=True), R=(mv_tm, PT), W=(ps[PIN],))
                P.pe(lambda h: h.matmul(PS(PDN, 128, 64, cs.start), lhsT=Cb.ap[:, 128:256], rhs=qTm.ap[:, hh, cs],
                                        start=True, stop=False), R=(Cb, qTm), W=(ps[PDN],))
                P.pe(lambda h: h.matmul(PS(PDN, 128, 64, cs.start), lhsT=onesb.ap[r0:r0 + 64, :], rhs=PT.ap[r0:r0 + 64, b, :],
                                        start=False, stop=True), R=(onesb, PT), W=(ps[PDN],))
                yield
            t1, hbuf = t1L[hh], hT[hh]
            P.act(lambda h: h.activation(out=t1.ap[:, 0:N], in_=PS(PDN, 128, N), func=AF.Abs), R=(ps[PDN],), W=(t1,))
            P.act(lambda h: h.activation(out=hbuf.ap[:, 0:N], in_=PS(PIN, 128, N), func=AF.Copy), R=(ps[PIN],), W=(hbuf,))

        def mlstm_tail(hh):
            t1, hbuf, eb = t1L[hh], hT[hh], ebL[hh % 2]
            pe_ = nextps()
            P.pe(lambda h: h.matmul(PS(pe_, 128, N), lhsT=sel.ap[0:4, hh * 128:(hh + 1) * 128], rhs=gX[0:4, 0:N],
                                    start=True, stop=True), R=(sel, GR), W=(ps[pe_],))
            P.act(lambda h: h.activation(out=eb.ap[:, 0:N], in_=PS(pe_, 128, N), func=AF.Exp, scale=-1.0),
                  R=(ps[pe_],), W=(eb,))
            yield
            P.dve(lambda h: h.tensor_tensor(out=t1.ap[:, 0:N], in0=t1.ap[:, 0:N], in1=eb.ap[:, 0:N], op=ALU.max),
                  R=(t1, eb), W=(t1,))
            P.act(lambda h: h.activation(out=t1.ap[:, 0:N], in_=t1.ap[:, 0:N], func=AF.Ln), R=(t1,), W=(t1,))
            P.act(lambda h: h.activation(out=t1.ap[:, 0:N], in_=t1.ap[:, 0:N], func=AF.Exp, scale=-1.0), R=(t1,), W=(t1,))
            yield
            P.dve(lambda h: h.tensor_tensor(out=hbuf.ap[:, 0:N], in0=hbuf.ap[:, 0:N], in1=t1.ap[:, 0:N], op=ALU.mult),
                  R=(hbuf, t1), W=(hbuf,))
            yield
            headnorm_out(hbuf, hh, vecp.ap[:, 0, hh:hh + 1], sigo.ap[:, hh, 0:N])

        run_interleaved([mlstm_head(0, 0, 4, 5), mlstm_head(1, 1, 6, 7)])
        run_interleaved([mlstm_head(2, 0, 4, 5), mlstm_head(3, 1, 6, 7), mlstm_tail(0), mlstm_tail(1)])

        DBG.get('phase_hook', lambda n, p: None)('m0_heads_ret', P)
        def ret_head(hh, PIN):
            pr, ho = hh // 2, (hh % 2) * 64
            PT = PTL[hh]
            psc = nextps()
            for c in range(nch):
                b, hf = c // 2, c % 2
                P.pe(lambda h: h.matmul(PS(psc, 64, 64, b * 64, hf * 64), lhsT=rkT.ap[ho:ho + 64, pr, c * 64:(c + 1) * 64],
                                        rhs=rqT.ap[ho:ho + 64, pr, c * 64:(c + 1) * 64], start=True, stop=True),
                     R=(rkT, rqT), W=(ps[psc],))
            P.dve(lambda h: h.tensor_tensor(out=PT.ap[:, 0:nb, :], in0=PS(psc, 128, nb * 64).rearrange("p (b t) -> p b t", t=64),
                                            in1=mret.ap[:, hh:hh + 1, :].broadcast_to([128, nb, 64]), op=ALU.mult),
                  R=(ps[psc], mret), W=(PT,))
            yield
            for c in range(nch):
                b, hf = c // 2, c % 2
                r0 = hf * 64
                cs = slice(c * 64, (c + 1) * 64)
                Sold = SH[hh][scur[hh]]
                Snew = SH[hh][1 - scur[hh]]
                scur[hh] = 1 - scur[hh]
                Sb = SbL[hh][c % 3]
                P.act(lambda h: h.activation(out=Sb.ap[ho:ho + 64, :], in_=Sold.ap[ho:ho + 64, :], func=AF.Copy),
                      R=(Sold,), W=(Sb,))
                pdc = nextps()
                P.pe(lambda h: h.matmul(PS(pdc, 64, 128, 0, ho), lhsT=rk_tm.ap[r0:r0 + 64, b, pr * 128 + ho:pr * 128 + ho + 64],
                                        rhs=rv_tm.ap[r0:r0 + 64, b, hh * 128:(hh + 1) * 128], start=True, stop=True),
                     R=(rk_tm, rv_tm), W=(ps[pdc],))
                P.dve(lambda h: h.scalar_tensor_tensor(out=Snew.ap[ho:ho + 64, :], in0=Sold.ap[ho:ho + 64, :],
                                                       scalar=float(GAM[hh] ** 64), in1=PS(pdc, 64, 128, 0, ho),
                                                       op0=ALU.mult, op1=ALU.add), R=(Sold, ps[pdc]), W=(Snew,))
                P.pe(lambda h: h.matmul(PS(PIN, 128, 64, cs.start), lhsT=Sb.ap[:, :], rhs=rqd.ap[:, pr, cs],
                                        start=True, stop=False), R=(Sb, rqd), W=(ps[PIN],))
                P.pe(lambda h: h.matmul(PS(PIN, 128, 64, cs.start), lhsT=rv_tm.ap[r0:r0 + 64, b, hh * 128:(hh + 1) * 128],
                                        rhs=PT.ap[r0:r0 + 64, b, :], start=False, stop=True), R=(rv_tm, PT), W=(ps[PIN],))
                yield
            hbuf = hT[(4 + hh) % 6]
            P.act(lambda h: h.activation(out=hbuf.ap[:, 0:N], in_=PS(PIN, 128, N), func=AF.Copy), R=(ps[PIN],), W=(hbuf,))

        def ret_tail(hh):
            hbuf = hT[(4 + hh) % 6]
            yield
            headnorm_out(hbuf, 4 + hh, vecp.ap[:, 0, 4 + hh:5 + hh], silg.ap[:, hh, 0:N])

        run_interleaved([ret_head(hh, 4 + hh) for hh in range(4)] + [mlstm_tail(2), mlstm_tail(3)])
        run_interleaved([ret_tail(hh) for hh in range(4)])

        DBG.get('phase_hook', lambda n, p: None)('m0_oproj', P)
        for g in range(2):
            slot, w = wload("OA", g)
            for j in range(4):
                m = g * 4 + j
                pi = nextps()
                for kc in range(8):
                    P.pe(lambda h, kc=kc, j=j, pi=pi, w=w: h.matmul(
                        PS(pi, 128, N), lhsT=w[:, kc, j * 128:(j + 1) * 128], rhs=mixed.ap[:, kc, 0:N],
                        start=(kc == 0), stop=(kc == 7)), R=(slot, mixed), W=(ps[pi],))
                resid_chunk(m, pi, N)
        residual_ln(N, 0, 0)
        P.release(m0)

    krT2 = P.alloc([128, 4096], BF16)
    KT_d = dscr("kt_scr", [8, 128, 4096])
    V_d = dscr("v_scr", [4096 + 128, 1024])
    kvo = [Obj() for _ in range(9)]
    kvsem = [P.new_dma_sem() for _ in range(2)]
    tsem2 = P.new_dma_sem()

    def kv_expand(ckvT, N, kpos0):
        nb = (N + 127) // 128
        reg = kvo[kpos0 // 512]
        m0 = P.mark()
        Kst = P.alloc([128, 8, 512], BF16)
        Vst = P.alloc([128, 4, 1024], BF16)
        slot, w = wload("UKV", 0)
        for hh in range(8):
            pi = nextps()
            for kc in range(2):
                P.pe(lambda h, kc=kc: h.matmul(PS(pi, 128, N), lhsT=w[:, kc, hh * 128:(hh + 1) * 128], rhs=ckvT.ap[:, kc, 0:N],
                                               start=(kc == 0), stop=(kc == 1)), R=(slot, ckvT), W=(ps[pi],))
            if hh % 2 == 0:
                P.act(lambda h: h.activation(out=Kst.ap[:, hh, 0:N], in_=PS(pi, 128, N), func=AF.Copy), R=(ps[pi],), W=(Kst,))
            else:
                P.dve(lambda h: h.tensor_copy(out=Kst.ap[:, hh, 0:N], in_=PS(pi, 128, N)), R=(ps[pi],), W=(Kst,))
        P.store(KT_d[:, :, kpos0:kpos0 + N].rearrange("h d n -> d h n"), Kst.ap[:, :, 0:N], R=(Kst,), W=(reg,))
        slot, w = wload("UKV", 1)
        for b in range(nb):
            bs = min(128, N - b * 128)
            for half in range(2):
                pi = nextps()
                for kc in range(2):
                    P.pe(lambda h, kc=kc: h.matmul(PS(pi, bs, 512), lhsT=ckvT.ap[:, kc, b * 128:b * 128 + bs],
                                                   rhs=w[:, kc, half * 512:(half + 1) * 512],
                                                   start=(kc == 0), stop=(kc == 1)), R=(slot, ckvT), W=(ps[pi],))
                if half == 0:
                    P.act(lambda h: h.activation(out=Vst.ap[0:bs, b, 0:512], in_=PS(pi, bs, 512), func=AF.Copy),
                          R=(ps[pi],), W=(Vst,))
                else:
                    P.dve(lambda h: h.tensor_copy(out=Vst.ap[0:bs, b, 512:1024], in_=PS(pi, bs, 512)), R=(ps[pi],), W=(Vst,))
        if N % 128 == 0:
            P.store(V_d[kpos0:kpos0 + N, :].rearrange("(b p) c -> p b c", p=128), Vst.ap[:, 0:nb, :], R=(Vst,), W=(reg,))
        else:
            P.store(V_d[kpos0:kpos0 + N, :], Vst.ap[0:N, 0, :], R=(Vst,), W=(reg,))
        P.release(m0)

    def tm_to_T(ckv_b, kr_b, ckvT, b, bs, kpos):
        pi = nextps()
        for kc in range(2):
            P.pe(lambda h, kc=kc: h.transpose(out=PSB(pi)[:, kc * 128:kc * 128 + bs], in_=ckv_b.ap[0:bs, kc * 128:(kc + 1) * 128],
                                              identity=identb.ap[0:bs, 0:bs]), R=(ckv_b, identb), W=(ps[pi],))
        P.pe(lambda h: h.transpose(out=PSB(pi)[:, 256:256 + bs], in_=kr_b.ap[0:bs, :], identity=identb.ap[0:bs, 0:bs]),
             R=(kr_b, identb), W=(ps[pi],))
        P.act(lambda h: h.activation(out=ckvT.ap[:, :, b * 128:b * 128 + bs],
                                     in_=PSB(pi)[:, 0:256].rearrange("p (k c) -> p k c", k=2)[:, :, 0:bs], func=AF.Copy),
              R=(ps[pi],), W=(ckvT,))
        P.dve(lambda h: h.tensor_copy(out=krT2.ap[:, kpos:kpos + bs], in_=PSB(pi)[:, 256:256 + bs]), R=(ps[pi],), W=(krT2,))

    def kv_prefill(s):
        for c in range(PAST // 512):
            m0 = P.mark()
            cin = P.alloc([128, 4, 256], F32)
            kin = P.alloc([128, 4, 64], F32)
            P.load(cin.ap, I["cckv"][s, c * 512:(c + 1) * 512, :].rearrange("(b p) f -> p b f", p=128), tsem2, W=(cin,))
            P.load(kin.ap, I["ckr"][s, c * 512:(c + 1) * 512, :].rearrange("(b p) f -> p b f", p=128), tsem2, W=(kin,))
            ckvT = P.alloc([128, 2, 512], BF16)
            for b in range(4):
                ckv_b = P.alloc([128, 256], BF16)
                kr_b = P.alloc([128, 128], BF16)
                P.act(lambda h: h.activation(out=ckv_b.ap, in_=cin.ap[:, b, :], func=AF.Copy), R=(cin,), W=(ckv_b,))
                P.dve(lambda h: h.tensor_copy(out=kr_b.ap.rearrange("p (a c) -> p a c", a=2),
                                              in_=kin.ap[:, b:b + 1, :].broadcast_to([128, 2, 64])), R=(kin,), W=(kr_b,))
                tm_to_T(ckv_b, kr_b, ckvT, b, 128, c * 512 + b * 128)
            kv_expand(ckvT, 512, c * 512)
            P.release(m0)

    def mix1(N, pos0, kpos0, is_prompt, okv, okr, s, orow0):
        nb = (N + 127) // 128
        m0 = P.mark()
        qnT = P.alloc([128, 8, 512], BF16)
        qrT = P.alloc([128, 8, 512], BF16)
        P.pool(lambda h: h.memset(qrT.ap, 0.0), W=(qrT,))
        oT = P.alloc([128, 8, 512], BF16)
        m1 = P.mark()
        cqT = P.alloc([128, 4, 512], F32)
        cqTb = P.alloc([128, 4, 512], BF16)
        ckvT = P.alloc([128, 2, 512], BF16)
        ctm = P.alloc([128, 4, 32], F32)
        stm_ = P.alloc([128, 4, 32], F32)
        P.load(cosb.ap[:, 0:N], I["cosF"][:, pos0:pos0 + N], tsem, W=(cosb,))
        P.load(sinb.ap[:, 0:N], I["sinF"][:, pos0:pos0 + N], tsem, W=(sinb,))
        if N % 128 == 0:
            P.load(ctm.ap[:, 0:nb, :], I["cosT"][pos0:pos0 + N, :].rearrange("(b p) j -> p b j", p=128), tsem2, W=(ctm,))
            P.load(stm_.ap[:, 0:nb, :], I["sinT"][pos0:pos0 + N, :].rearrange("(b p) j -> p b j", p=128), tsem2, W=(stm_,))
        else:
            P.load(ctm.ap[0:N, 0, :], I["cosT"][pos0:pos0 + N, :], tsem2, W=(ctm,))
            P.load(stm_.ap[0:N, 0, :], I["sinT"][pos0:pos0 + N, :], tsem2, W=(stm_,))
        slot, w = wload("C", 0)
        for m in range(4):
            pi = nextps()
            for kc in range(8):
                P.pe(lambda h, kc=kc: h.matmul(PS(pi, 128, N), lhsT=w[:, kc, m * 128:(m + 1) * 128], rhs=xTb.ap[:, kc, 0:N],
                                               start=(kc == 0), stop=(kc == 7)), R=(slot, xTb), W=(ps[pi],))
            P.act(lambda h: h.activation(out=cqT.ap[:, m, 0:N], in_=PS(pi, 128, N), func=AF.Copy), R=(ps[pi],), W=(cqT,))
        fm_norm([(cqT, cqT.ap[:, m, 0:N]) for m in range(4)], N, 512.0, RMS_EPS, False)
        for m in range(4):
            P.act(lambda h: h.activation(out=cqTb.ap[:, m, 0:N], in_=cqT.ap[:, m, 0:N], func=AF.Identity,
                                         scale=vecp.ap[:, 1, m:m + 1]), R=(cqT, vecp), W=(cqTb,))
        DBG.get('phase_hook', lambda n, p: None)('m1_ckv', P)
        slot, w = wload("C", 1)
        for b in range(nb):
            bs = min(128, N - b * 128)
            mb = P.mark()
            ckvo = P.alloc([128, 256], F32)
            kro = P.alloc([128, 64], F32)
            ckv_b = P.alloc([128, 256], BF16)
            kr_b = P.alloc([128, 128], BF16)
            sm = P.alloc([128, 8], F32)
            rt = P.alloc([128, 4, 32], F32)
            pi = nextps()
            for kc in range(8):
                P.pe(lambda h, kc=kc: h.matmul(PS(pi, bs, 320), lhsT=xTb.ap[:, kc, b * 128:b * 128 + bs], rhs=w[:, kc, 0:320],
                                               start=(kc == 0), stop=(kc == 7)), R=(slot, xTb), W=(ps[pi],))
            P.act(lambda h: h.activation(out=ckvo.ap[0:bs, :], in_=PS(pi, bs, 256), func=AF.Square, accum_out=sm.ap[0:bs, 0:1]),
                  R=(ps[pi],), W=(ckvo, sm))
            P.act(lambda h: h.activation(out=sm.ap[0:bs, 1:2], in_=sm.ap[0:bs, 0:1], func=AF.Ln, scale=1.0 / 256,
                                         bias=epsb[RMS_EPS].ap[0:bs, 0:1]), R=(sm, epsb[RMS_EPS]), W=(sm,))
            P.act(lambda h: h.activation(out=sm.ap[0:bs, 2:3], in_=sm.ap[0:bs, 1:2], func=AF.Exp, scale=-0.5), R=(sm,), W=(sm,))
            P.dve(lambda h: h.scalar_tensor_tensor(out=ckvo.ap[0:bs, :], in0=PS(pi, bs, 256), scalar=sm.ap[0:bs, 2:3],
                                                   in1=gkvb.ap[0:bs, :], op0=ALU.mult, op1=ALU.mult),
                  R=(ps[pi], sm, gkvb), W=(ckvo,))
            P.store(O[okv][s, orow0 + b * 128:orow0 + b * 128 + bs, :], ckvo.ap[0:bs, :], R=(ckvo,))
            P.act(lambda h: h.activation(out=ckv_b.ap[0:bs, :], in_=ckvo.ap[0:bs, :], func=AF.Copy), R=(ckvo,), W=(ckv_b,))
            x1 = PS(pi, bs, 32, 256)
            x2 = PS(pi, bs, 32, 288)
            cs_, sn_ = ctm.ap[0:bs, b, :], stm_.ap[0:bs, b, :]
            for j, (xa, tb_) in enumerate(((x1, cs_), (x2, sn_), (x2, cs_), (x1, sn_))):
                P.dve(lambda h, j=j, xa=xa, tb_=tb_: h.tensor_tensor(out=rt.ap[0:bs, j, :], in0=xa, in1=tb_, op=ALU.mult),
                      R=(ps[pi], ctm, stm_), W=(rt,))
            P.dve(lambda h: h.tensor_tensor(out=kro.ap[0:bs, 0:32], in0=rt.ap[0:bs, 0, :], in1=rt.ap[0:bs, 1, :], op=ALU.subtract),
                  R=(rt,), W=(kro,))
            P.dve(lambda h: h.tensor_tensor(out=kro.ap[0:bs, 32:64], in0=rt.ap[0:bs, 2, :], in1=rt.ap[0:bs, 3, :], op=ALU.add),
                  R=(rt,), W=(kro,))
            P.store(O[okr][s, orow0 + b * 128:orow0 + b * 128 + bs, :], kro.ap[0:bs, :], R=(kro,))
            P.act(lambda h: h.activation(out=kr_b.ap[0:bs, :].rearrange("p (a c) -> p a c", a=2),
                                         in_=kro.ap[0:bs, None, :].broadcast_to([bs, 2, 64]), func=AF.Copy), R=(kro,), W=(kr_b,))
            tm_to_T(ckv_b, kr_b, ckvT, b, bs, kpos0 + b * 128)
            P.release(mb)
        DBG.get('phase_hook', lambda n, p: None)('m1_kvexp', P)
        kv_expand(ckvT, N, kpos0)
        slot, w = wload("UQ", 0)
        for hh in range(8):
            pi = nextps()
            for kc in range(4):
                P.pe(lambda h, kc=kc: h.matmul(PS(pi, 128, N), lhsT=w[:, kc, hh * 128:(hh + 1) * 128], rhs=cqTb.ap[:, kc, 0:N],
                                               start=(kc == 0), stop=(kc == 3)), R=(slot, cqTb), W=(ps[pi],))
            if hh % 2 == 0:
                P.act(lambda h: h.activation(out=qnT.ap[:, hh, 0:N], in_=PS(pi, 128, N), func=AF.Copy), R=(ps[pi],), W=(qnT,))
            else:
                P.dve(lambda h: h.tensor_copy(out=qnT.ap[:, hh, 0:N], in_=PS(pi, 128, N)), R=(ps[pi],), W=(qnT,))
        slot, w = wload("UQ", 1)
        wsw = wswap(slot, w, 4, 512)
        for pr in range(4):
            pa = nextps()
            for kc in range(4):
                P.pe(lambda h, kc=kc: h.matmul(PS(pa, 128, N), lhsT=w[:, kc, pr * 128:(pr + 1) * 128], rhs=cqTb.ap[:, kc, 0:N],
                                               start=(kc == 0), stop=(kc == 3)), R=(slot, cqTb), W=(ps[pa],))
            pb = nextps()
            for kc in range(4):
                P.pe(lambda h, kc=kc: h.matmul(PS(pb, 128, N), lhsT=wsw[:, kc, pr * 128:(pr + 1) * 128],
                                               rhs=cqTb.ap[:, kc, 0:N], start=(kc == 0), stop=(kc == 3)),
                     R=(slot, cqTb), W=(ps[pb],))
            P.dve(lambda h: h.tensor_tensor(out=T[0].ap[:, 0:N], in0=PS(pa, 128, N), in1=cosb.ap[:, 0:N], op=ALU.mult),
                  R=(ps[pa], cosb), W=(T[0],))
            P.dve(lambda h: h.tensor_tensor(out=T[1].ap[:, 0:N], in0=PS(pb, 128, N), in1=sinb.ap[:, 0:N], op=ALU.mult),
                  R=(ps[pb], sinb), W=(T[1],))
            P.pool(lambda h: h.tensor_tensor(out=qrT.ap[0:64, 2 * pr, 0:N], in0=T[0].ap[0:64, 0:N], in1=T[1].ap[0:64, 0:N],
                                             op=ALU.add), R=(T[0], T[1]), W=(qrT,))
            P.pool(lambda h: h.tensor_tensor(out=qrT.ap[64:128, 2 * pr + 1, 0:N], in0=T[0].ap[64:128, 0:N],
                                             in1=T[1].ap[64:128, 0:N], op=ALU.add), R=(T[0], T[1]), W=(qrT,))
        P.release(m1)
        DBG.get('phase_hook', lambda n, p: None)('m1_attn', P)
        stageL = [P.alloc([128, 4096], F32) for _ in range(2)]
        PbL = [P.alloc([128, 4096], BF16) for _ in range(2)]
        KTh = P.alloc([128, 4096], BF16)
        VhL = [P.alloc([128, 32, 128], BF16) for _ in range(2)]
        PTs = [P.alloc([128, 4, 128], BF16) for _ in range(4)]
        o_tm = P.alloc([128, 4, 1024], BF16)
        sm2L = [P.alloc([128, 16], F32) for _ in range(2)]
        qrr = [0]
        nk = kpos0 + N
        nreg = (nk + 511) // 512
        kregs = tuple(kvo[i] for i in range(nreg))
        ptrr = [0]
        porr = [0]
        iters = [(hh, qb) for hh in range(8) for qb in range(nb)]

        def stage_a(it):
            hh, qb = iters[it]
            pr, ho = hh // 2, (hh % 2) * 64
            Vh = VhL[hh % 2]
            if qb == 0:
                P.load(KTh.ap[:, 0:nk], KT_d[hh, :, 0:nk], kvsem[0], R=kregs, W=(KTh,))
                nfull = nk // 128
                P.load(Vh.ap[:, 0:nfull, :], V_d[0:nfull * 128, hh * 128:(hh + 1) * 128].rearrange("(j p) c -> p j c", p=128),
                       kvsem[1], R=kregs, W=(Vh,))
                if nk % 128:
                    rem = nk % 128
                    P.load(Vh.ap[0:rem, nfull, :], V_d[nfull * 128:nk, hh * 128:(hh + 1) * 128], kvsem[1], R=kregs, W=(Vh,))
            bq = min(128, N - qb * 128)
            qs = slice(qb * 128, qb * 128 + bq)
            nvis = kpos0 + (qb + 1) * 128 if is_prompt else nk
            stage, Pb, sm2 = stageL[it % 2], PbL[it % 2], sm2L[it % 2]
            ng = (nvis + 511) // 512
            for g in range(ng):
                gs = min(512, nvis - g * 512)
                pi = nextps()
                P.pe(lambda h: h.matmul(PS(pi, bq, gs), lhsT=qnT.ap[:, hh, qs], rhs=KTh.ap[:, g * 512:g * 512 + gs],
                                        start=True, stop=False), R=(qnT, KTh), W=(ps[pi],))
                P.pe(lambda h: h.matmul(PS(pi, bq, gs), lhsT=qrT.ap[:, hh, qs],
                                        rhs=krT2.ap[:, g * 512:g * 512 + gs], start=False, stop=True),
                     R=(qrT, krT2), W=(ps[pi],))
                if g == ng - 1:
                    P.act(lambda h: h.activation(out=stage.ap[0:bq, g * 512:g * 512 + gs], in_=PS(pi, bq, gs), func=AF.Copy),
                          R=(ps[pi],), W=(stage,))
                    if is_prompt:
                        P.pool(lambda h: h.memset(stage.ap[0:64, nvis - 64:nvis], NEG), W=(stage,))
                else:
                    P.dve(lambda h: h.tensor_scalar(out=stage.ap[0:bq, g * 512:g * 512 + gs], in0=PS(pi, bq, gs),
                                                    scalar1=1.0, scalar2=None, op0=ALU.mult, op1=ALU.max,
                                                    accum_out=sm2.ap[0:bq, 4 + g:5 + g]), R=(ps[pi],), W=(stage, sm2))

        def stage_a2(it):
            hh, qb = iters[it]
            bq = min(128, N - qb * 128)
            nvis = kpos0 + (qb + 1) * 128 if is_prompt else nk
            stage, Pb, sm2 = stageL[it % 2], PbL[it % 2], sm2L[it % 2]
            ng = (nvis + 511) // 512
            g = ng - 1
            gs = min(512, nvis - g * 512)
            P.dve(lambda h: h.reduce_max(out=sm2.ap[0:bq, 4 + g:5 + g], in_=stage.ap[0:bq, g * 512:g * 512 + gs],
                                         axis=AX.X), R=(stage,), W=(sm2,))
            P.dve(lambda h: h.reduce_max(out=sm2.ap[0:bq, 0:1], in_=sm2.ap[0:bq, 4:4 + ng], axis=AX.X), R=(sm2,), W=(sm2,))
            P.dve(lambda h: h.tensor_scalar(out=sm2.ap[0:bq, 1:2], in0=sm2.ap[0:bq, 0:1], scalar1=-MLA_SCALE, scalar2=None,
                                            op0=ALU.mult), R=(sm2,), W=(sm2,))
            P.act(lambda h: h.activation(out=Pb.ap[0:bq, 0:nvis], in_=stage.ap[0:bq, 0:nvis], func=AF.Exp, scale=MLA_SCALE,
                                         bias=sm2.ap[0:bq, 1:2], accum_out=sm2.ap[0:bq, 2:3]), R=(stage, sm2), W=(Pb, sm2))
            P.dve(lambda h: h.reciprocal(out=sm2.ap[0:bq, 3:4], in_=sm2.ap[0:bq, 2:3]), R=(sm2,), W=(sm2,))

        def stage_b(it):
            hh, qb = iters[it]
            Vh = VhL[hh % 2]
            bq = min(128, N - qb * 128)
            nvis = kpos0 + (qb + 1) * 128 if is_prompt else nk
            Pb, sm2 = PbL[it % 2], sm2L[it % 2]
            nkb = (nvis + 127) // 128
            po = 4
            groups = [(k0, min(nkb, k0 + 4)) for k0 in range(0, nkb, 4)]
            slots = []

            def emit_T(gi):
                k0, k1 = groups[gi]
                pt = 5 + ptrr[0] % 3
                pts = PTs[ptrr[0] % len(PTs)]
                ptrr[0] += 1
                for kb in range(k0, k1):
                    kbs = min(128, nvis - kb * 128)
                    j = kb - k0
                    P.pe(lambda h: h.transpose(out=PSB(pt)[0:kbs, j * 128:j * 128 + bq], in_=Pb.ap[0:bq, kb * 128:kb * 128 + kbs],
                                               identity=identb.ap[0:bq, 0:bq]), R=(Pb, identb), W=(ps[pt],))
                nj = k1 - k0
                src = PSB(pt)[:, 0:nj * 128].rearrange("p (j c) -> p j c", j=nj)[:, :, 0:bq]
                if gi % 2 == 0:
                    P.dve(lambda h: h.tensor_copy(out=pts.ap[:, 0:nj, 0:bq], in_=src), R=(ps[pt],), W=(pts,))
                else:
                    P.act(lambda h: h.activation(out=pts.ap[:, 0:nj, 0:bq], in_=src, func=AF.Copy), R=(ps[pt],), W=(pts,))
                slots.append(pts)

            def emit_PV(gi):
                k0, k1 = groups[gi]
                pts = slots[gi]
                for kb in range(k0, k1):
                    kbs = min(128, nvis - kb * 128)
                    j = kb - k0
                    P.pe(lambda h: h.matmul(PS(po, bq, 128), lhsT=pts.ap[0:kbs, j, 0:bq], rhs=Vh.ap[0:kbs, kb, :],
                                            start=(kb == 0), stop=(kb == nkb - 1)), R=(pts, Vh), W=(ps[po],))

            LA = 2
            for gi in range(min(LA, len(groups))):
                emit_T(gi)
            for gi in range(len(groups)):
                if gi + LA < len(groups):
                    emit_T(gi + LA)
                emit_PV(gi)
            P.act(lambda h: h.activation(out=o_tm.ap[0:bq, qb, hh * 128:(hh + 1) * 128], in_=PS(po, bq, 128), func=AF.Identity,
                                         scale=sm2.ap[0:bq, 3:4]), R=(ps[po], sm2), W=(o_tm,))

        stage_a(0)
        stage_a2(0)
        for it in range(len(iters)):
            if it + 1 < len(iters):
                stage_a(it + 1)
            stage_b(it)
            if it + 1 < len(iters):
                stage_a2(it + 1)
        DBG.get('phase_hook', lambda n, p: None)('m1_oT', P)
        for qb in range(nb):
            bq = min(128, N - qb * 128)
            for half in range(2):
                pt = 5 + ptrr[0] % 3
                ptrr[0] += 1
                for j in range(4):
                    m = half * 4 + j
                    P.pe(lambda h, j=j, m=m: h.transpose(out=PSB(pt)[:, j * 128:j * 128 + bq],
                                                         in_=o_tm.ap[0:bq, qb, m * 128:(m + 1) * 128],
                                                         identity=identb.ap[0:bq, 0:bq]), R=(o_tm, identb), W=(ps[pt],))
                src = PSB(pt)[:, 0:512].rearrange("p (j c) -> p j c", j=4)[:, :, 0:bq]
                if half == 0:
                    P.act(lambda h, src=src: h.activation(out=oT.ap[:, 0:4, qb * 128:qb * 128 + bq], in_=src, func=AF.Copy),
                          R=(ps[pt],), W=(oT,))
                else:
                    P.dve(lambda h, src=src: h.tensor_copy(out=oT.ap[:, 4:8, qb * 128:qb * 128 + bq], in_=src),
                          R=(ps[pt],), W=(oT,))
        for g in range(2):
            slot, w = wload("OC", g)
            for j in range(4):
                m = g * 4 + j
                pi = nextps()
                for kc in range(8):
                    P.pe(lambda h, kc=kc: h.matmul(PS(pi, 128, N), lhsT=w[:, kc, j * 128:(j + 1) * 128], rhs=oT.ap[:, kc, 0:N],
                                                   start=(kc == 0), stop=(kc == 7)), R=(slot, oT), W=(ps[pi],))
                resid_chunk(m, pi, N)
        residual_ln(N, 1, 0)
        P.release(m0)

    for s in range(n_pseq):
        if "mix0" in stages:
            state_zero()
        for t in range(n_ptiles):
            N = 512
            _ph = DBG.get("phase_hook", lambda n, p: None)
            _ph("load", P)
            if (t, "ld") not in DBG.get("skip", ()):
                load_x(I["xp"][s, t * N:(t + 1) * N, :], N)
            _ph("mix0", P)
            if "mix0" in stages:
                mix0(N, t * N)
                if t == n_ptiles - 1:
                    state_store("pC", "pn", "pm", "pS", s)
            _ph("ffn0", P)
            if "ffn0" in stages:
                ffn(N, 0)
            _ph("mix1", P)
            if "mix1" in stages:
                mix1(N, t * N, t * N, True, "pckv", "pkr", s, t * N)
            _ph("ffn1", P)
            if "ffn1" in stages:
                ffn(N, 1)
            _ph("store", P)
            if (t, "st") not in DBG.get("skip", ()):
                store_y(O["yp"][s, t * N:(t + 1) * N, :], N)
    DBG.get("phase_hook", lambda n, p: None)("sample", P)
    for s in range(n_sseq):
        N = DEC_SEQ
        if DBG.get("s_load", True):
            load_x(I["xs"][s, :, :], N)
        if "mix0" in stages:
            state_load(s)
            mix0(N, SEQ)
            state_store("sC", "sn", "sm", "sS", s)
        if "ffn0" in stages:
            ffn(N, 0)
        if "mix1" in stages:
            kv_prefill(s)
            mix1(N, SEQ, PAST, False, "sckv", "skr", s, 0)
        if "ffn1" in stages:
            ffn(N, 1)
        if DBG.get("s_store", True):
            store_y(O["ys"][s, :, :], N)

    P.finish(stack)
    stack.close()
    return nc, P


def _consts():
    c = {}
    c["ident"] = np.eye(128, dtype=np.float32)
    half = 32
    inv = (10000.0 ** (-np.arange(half, dtype=np.float32) / half)).astype(np.float32)
    pos = np.concatenate([np.arange(SEQ), PAST + np.arange(DEC_SEQ)]).astype(np.float32)
    ang = (pos[:, None] * inv[None, :]).astype(np.float32)
    cos = np.cos(ang).astype(np.float32)
    sin = np.sin(ang).astype(np.float32)
    c["cosT"] = cos
    c["sinT"] = sin
    p = np.arange(128)
    j = (p % 64) % 32
    sign = np.where((p % 64) < 32, -1.0, 1.0).astype(np.float32)
    c["cosF"] = np.ascontiguousarray(cos[:, j].T)
    c["sinF"] = np.ascontiguousarray((sin[:, j] * sign[None, :]).T)
    s_idx = (p % 64)[:, None]
    t_idx = np.arange(64)[None, :]
    c["mask_ml"] = (s_idx <= t_idx).astype(np.float32)
    gam = (1.0 - 2.0 ** (-5.0 - np.arange(4, dtype=np.float64)))
    mret = np.zeros((128, 4, 64), np.float32)
    for h in range(4):
        rel = t_idx - s_idx
        mret[:, h, :] = np.where(rel >= 0, gam[h] ** np.maximum(rel, 0), 0.0) * (64.0 ** -0.5)
    c["mret"] = mret
    kdec = np.zeros((128, 2, 128), np.float32)
    for pr in range(2):
        for jj in range(128):
            h = 2 * pr + jj // 64
            kdec[:, pr, jj] = gam[h] ** (63 - (p % 64)) * (64.0 ** -0.5)
    c["kdec"] = kdec
    qdec = np.zeros((128, 2, 64), np.float32)
    for pr in range(2):
        for pp in range(128):
            h = 2 * pr + pp // 64
            qdec[pp, pr, :] = gam[h] ** (np.arange(64) + 1.0)
    c["qdec"] = qdec
    sel = np.zeros((4, 4, 128), np.float32)
    for h in range(4):
        sel[h, h, :] = 1.0
    c["sel"] = sel
    dm = np.zeros((128, 128), np.float32)
    dm[0:64, 64:128] = NEG
    c["dmask"] = dm
    return c


_CACHE = {}


def kernel(**inp):
    f = lambda a: np.ascontiguousarray(np.asarray(a, dtype=np.float32))
    key = "full"
    if key not in _CACHE:
        _CACHE[key] = build()
    nc, _ = _CACHE[key]
    cst = _consts()
    ln = np.concatenate([f(inp["ln_mix_g"]), f(inp["ln_mix_b"]), f(inp["ln_ffn_g"]), f(inp["ln_ffn_b"])], axis=0)
    shared = {
        "w_in_a": f(inp["w_in_a"]), "b_if": f(inp["b_if_a"]).reshape(8), "g_ml": f(inp["g_ml"]).reshape(512),
        "g_ret": f(inp["g_ret"]).reshape(512), "w_out_a": f(inp["w_out_a"]), "w_in_c": f(inp["w_in_c"]),
        "g_q": f(inp["g_q"]).reshape(512), "g_kv": f(inp["g_kv"]).reshape(256), "w_uq": f(inp["w_uq"]),
        "w_ukv": f(inp["w_ukv"]), "w_out_c": f(inp["w_out_c"]), "ln": ln,
        "w_gu": f(inp["w_gu"]), "w_down": f(inp["w_down"]),
    }
    shared.update(cst)
    in_maps = []
    for c in range(NCORES):
        sl = slice(2 * c, 2 * c + 2)
        m = dict(shared)
        m["xp"] = f(inp["x_prompt"][sl])
        m["xs"] = f(inp["x_sample"][sl])
        m["stC"] = f(inp["state_mlstm_C"][0, sl])
        m["stn"] = f(inp["state_mlstm_n"][0, sl])
        m["stm"] = f(inp["state_mlstm_m"][0, sl])
        m["stS"] = f(inp["state_ret_S"][0, sl])
        m["cckv"] = f(inp["cache_ckv"][0, sl])
        m["ckr"] = f(inp["cache_krope"][0, sl])
        in_maps.append(m)
    res = run_bass_kernel_spmd(nc, in_maps, core_ids=list(range(NCORES)))
    R = res.results
    cat = lambda k: np.concatenate([np.asarray(r[k], dtype=np.float32) for r in R], axis=0)
    outs = (cat("yp"), cat("ys"),
            cat("pC")[None], cat("pn")[None], cat("pm")[None], cat("pS")[None], cat("pckv")[None], cat("pkr")[None],
            cat("sC")[None], cat("sn")[None], cat("sm")[None], cat("sS")[None], cat("sckv")[None], cat("skr")[None])
    return outs
```

```python
import math
import numpy as np
import concourse.bass as bass
import concourse.mybir as mybir
from concourse.bass_utils import run_bass_kernel_spmd

F32 = mybir.dt.float32
BF16 = mybir.dt.bfloat16
AF = mybir.ActivationFunctionType
ALU = mybir.AluOpType
AX = mybir.AxisListType

NCORES = 8
D = 1024
SEQ = 4096
DEC_SEQ = 64
PAST = 2048
DFF = 2816
PROJ_A = 3592
ALPHA = 4.0 ** 0.25
LN_EPS = 1e-5
RMS_EPS = 1e-6
MLA_SCALE = 192.0 ** -0.5
NPOS = SEQ + DEC_SEQ
NEG = -1.0e30
DBG = {}


def _freeze(fn):
    if getattr(fn, "__closure__", None) is None:
        return fn
    import types
    cells = []
    for c in fn.__closure__:
        try:
            cells.append(types.CellType(c.cell_contents))
        except ValueError:
            cells.append(c)
    return types.FunctionType(fn.__code__, fn.__globals__, fn.__name__, fn.__defaults__, tuple(cells))


class Res:
    __slots__ = ("w", "r")

    def __init__(self):
        self.w = None
        self.r = {}


class Obj:
    def __init__(self, n=1):
        self.res = [Res() for _ in range(n)]


class Buf:
    def __init__(self, ap, res):
        self.ap = ap
        self.res = res

    def __getitem__(self, idx):
        return self.ap[idx]


class Eng:
    def __init__(self, name, key):
        self.name = name
        self.key = key
        self.cnt = 0
        self.known = {}
        self.ops = []


class Prog:
    BLK = 256

    def __init__(self, nc, arena_bytes):
        self.nc = nc
        self.engs = {n: Eng(n, i) for i, n in enumerate(["pe", "act", "dve", "pool", "sp"])}
        self.nsem = 5
        self.dma_cnt = {}
        self.dma_last = {}
        self.arena_bytes = arena_bytes
        self.arena = nc.alloc_sbuf_tensor("arena", [128, arena_bytes // 2], BF16)
        self.ares = [Res() for _ in range(arena_bytes // self.BLK)]
        self.top = 0
        self.psum = []
        for i in range(8):
            t = nc.alloc_psum_tensor("ps%d" % i, [128, 512], F32)
            o = Obj()
            o.t = t
            o.excl = True
            self.psum.append(o)
        self.store_sems = []
        self.store_rr = 0
        self.store_eng = "pool"

    def new_dma_sem(self):
        k = self.nsem
        self.nsem += 1
        self.dma_cnt[k] = 0
        return k

    def alloc(self, shape, dtype, at=None):
        esz = 4 if dtype == F32 else 2
        free = 1
        for s in shape[1:]:
            free *= s
        nbytes = free * esz
        nbytes_r = (nbytes + self.BLK - 1) // self.BLK * self.BLK
        if at is not None:
            off = at
            assert off % self.BLK == 0 and off + nbytes_r <= self.arena_bytes
        else:
            off = self.top
            self.top += nbytes_r
            self.peak = max(getattr(self, "peak", 0), self.top)
            assert self.top <= self.arena_bytes, "arena overflow %d" % self.top
        ap = self.arena[0:shape[0], off // 2:(off + nbytes) // 2]
        if dtype == F32:
            ap = ap.bitcast(F32)
        if len(shape) == 3:
            ap = ap.rearrange("p (a b) -> p a b", a=shape[1])
        elif len(shape) == 4:
            ap = ap.rearrange("p (a b c) -> p a b c", a=shape[1], b=shape[2])
        res = self.ares[off // self.BLK:(off + nbytes_r) // self.BLK]
        return Buf(ap, res)

    def mark(self):
        return self.top

    def release(self, m):
        self.top = m

    def emit(self, ename, fn, R=(), W=(), dma_sem=None, chain=False):
        eng = self.engs[ename]
        need = {}
        if any(getattr(b, "excl", False) for b in R):
            W = tuple(W) + tuple(b for b in R if getattr(b, "excl", False))
            R = tuple(b for b in R if not getattr(b, "excl", False))

        def req(s, v):
            if ename == "pe" and s == eng.key:
                return
            if eng.known.get(s, 0) >= v:
                return
            if need.get(s, 0) < v:
                need[s] = v

        for b in R:
            for r in b.res:
                if r.w is not None:
                    req(*r.w)
        for b in W:
            for r in b.res:
                if r.w is not None:
                    req(*r.w)
                for s, v in r.r.items():
                    req(s, v)
        if dma_sem is not None:
            lt = self.dma_last.get(dma_sem)
            if lt is not None and not chain:
                req(*lt)
            self.dma_cnt[dma_sem] += 16
            tok = (dma_sem, self.dma_cnt[dma_sem])
            self.dma_last[dma_sem] = tok
        else:
            eng.cnt += 1
            tok = (eng.key, eng.cnt)
        for s, v in need.items():
            eng.known[s] = v
        for b in R:
            for r in b.res:
                if r.r.get(tok[0], 0) < tok[1]:
                    r.r[tok[0]] = tok[1]
        for b in W:
            for r in b.res:
                r.w = tok
                r.r = {}
        eng.ops.append((_freeze(fn), sorted(need.items()), tok, dma_sem is not None))
        return tok

    def pe(self, fn, R=(), W=()):
        return self.emit("pe", fn, R, W)

    def act(self, fn, R=(), W=()):
        return self.emit("act", fn, R, W)

    def dve(self, fn, R=(), W=()):
        return self.emit("dve", fn, R, W)

    def pool(self, fn, R=(), W=()):
        return self.emit("pool", fn, R, W)

    def load(self, out_ap, in_ap, sem, R=(), W=(), **kw):
        return self.emit("sp", lambda h: h.dma_start(out=out_ap, in_=in_ap, **kw), R, W, dma_sem=sem)

    def store(self, out_ap, in_ap, R=(), W=(), **kw):
        sem = self.store_sems[self.store_rr % len(self.store_sems)]
        self.store_rr += 1
        return self.emit(self.store_eng, lambda h: h.dma_start(out=out_ap, in_=in_ap, **kw), R, W, dma_sem=sem)

    def finish(self, stack):
        nc = self.nc
        pool = self.engs["pool"]
        fin = []
        for s, v in self.dma_cnt.items():
            if v > 0 and pool.known.get(s, 0) < v:
                fin.append((s, v))
        sems = [stack.enter_context(nc.semaphore("s%d" % i)) for i in range(self.nsem)]
        block = stack.enter_context(nc.Block())

        def replay(ename, h, extra=()):
            for fn, waits, tok, is_dma in self.engs[ename].ops:
                for s, v in waits:
                    h.wait_ge(sems[s], v)
                fn(h).then_inc(sems[tok[0]], 16 if is_dma else 1)
            for s, v in extra:
                h.wait_ge(sems[s], v)

        @block.tensor
        def _(h):
            replay("pe", h)

        @block.scalar
        def _(h):
            replay("act", h)

        @block.vector
        def _(h):
            replay("dve", h)

        @block.gpsimd
        def _(h):
            replay("pool", h, fin)

        @block.sync
        def _(h):
            replay("sp", h)


def _wspec():
    S = {}
    a = []
    a.append([(0, 512)])
    a.append([(512, 512)])
    a.append([(1536, 512)])
    rq0, rk0 = 2056, 2312

    def swp(base):
        p = []
        for h in range(4):
            p.append((base + 64 * h + 32, 32))
            p.append((base + 64 * h, 32))
        return p
    a.append([(rq0, 256)])
    a.append([(rk0, 256)])
    a.append([(3080, 512)])
    a.append([(1024, 512)])
    a.append([(2568, 512)])
    a.append([(2048, 8)])
    S["A"] = ("w_in_a", 0, 1024, a)
    S["OA"] = ("w_out_a", 0, 1024, [[(0, 512)], [(512, 512)]])
    for l in range(2):
        g = []
        for grp in range(11):
            m0 = 2 * grp
            g.append([(m0 * 128, 256), (DFF + m0 * 128, 256)])
        S["GU%d" % l] = ("w_gu", l, 1024, g)
        S["DN%d" % l] = ("w_down", l, DFF, [[(m * 128, 128)] for m in range(8)])
    S["C"] = ("w_in_c", 0, 1024, [[(0, 512)], [(512, 320)]])
    uq0 = [(h * 192, 128) for h in range(8)]
    uq1 = [(h * 192 + 128, 64) for h in range(8)]
    S["UQ"] = ("w_uq", 0, 512, [uq0, uq1])
    S["UKV"] = ("w_ukv", 0, 256, [[(h * 256, 128) for h in range(8)], [(h * 256 + 128, 128) for h in range(8)]])
    S["OC"] = ("w_out_c", 0, 1024, [[(0, 512)], [(512, 512)]])
    return S


def _merge(pieces):
    out = []
    for c0, n in pieces:
        if out and out[-1][0] + out[-1][1] == c0:
            out[-1] = (out[-1][0], out[-1][1] + n)
        else:
            out.append((c0, n))
    return out


def build(n_ptiles=8, n_pseq=2, n_sseq=2, stages=("mix0", "ffn0", "mix1", "ffn1")):
    from contextlib import ExitStack
    nc = bass.Bass("TRN2", target_bir_lowering=False)
    stack = ExitStack()

    def din(name, shape, dt=F32):
        return nc.dram_tensor(name, list(shape), dt, kind="ExternalInput").ap()

    def dout(name, shape, dt=F32):
        return nc.dram_tensor(name, list(shape), dt, kind="ExternalOutput").ap()

    def dscr(name, shape, dt=BF16):
        return nc.dram_tensor(name, list(shape), dt, kind="Internal").ap()

    I = {}
    I["xp"] = din("xp", [2, SEQ, D])
    I["xs"] = din("xs", [2, DEC_SEQ, D])
    I["stC"] = din("stC", [2, 4, 128, 128])
    I["stn"] = din("stn", [2, 4, 128])
    I["stm"] = din("stm", [2, 4])
    I["stS"] = din("stS", [2, 4, 64, 128])
    I["cckv"] = din("cckv", [2, PAST, 256])
    I["ckr"] = din("ckr", [2, PAST, 64])
    I["w_in_a"] = din("w_in_a", [1, D, PROJ_A])
    I["b_if"] = din("b_if", [8])
    I["g_ml"] = din("g_ml", [512])
    I["g_ret"] = din("g_ret", [512])
    I["w_out_a"] = din("w_out_a", [1, D, D])
    I["w_in_c"] = din("w_in_c", [1, D, 832])
    I["g_q"] = din("g_q", [512])
    I["g_kv"] = din("g_kv", [256])
    I["w_uq"] = din("w_uq", [1, 512, 1536])
    I["w_ukv"] = din("w_ukv", [1, 256, 2048])
    I["w_out_c"] = din("w_out_c", [1, D, D])
    I["ln"] = din("ln", [8, D])
    I["w_gu"] = din("w_gu", [2, D, 2 * DFF])
    I["w_down"] = din("w_down", [2, DFF, D])
    I["ident"] = din("ident", [128, 128])
    I["cosF"] = din("cosF", [128, NPOS])
    I["sinF"] = din("sinF", [128, NPOS])
    I["cosT"] = din("cosT", [NPOS, 32])
    I["sinT"] = din("sinT", [NPOS, 32])
    I["mask_ml"] = din("mask_ml", [128, 64])
    I["mret"] = din("mret", [128, 4, 64])
    I["kdec"] = din("kdec", [128, 2, 128])
    I["qdec"] = din("qdec", [128, 2, 64])
    I["sel"] = din("sel", [4, 4, 128])
    I["dmask"] = din("dmask", [128, 128])

    O = {}
    O["yp"] = dout("yp", [2, SEQ, D])
    O["ys"] = dout("ys", [2, DEC_SEQ, D])
    O["pC"] = dout("pC", [2, 4, 128, 128])
    O["pn"] = dout("pn", [2, 4, 128])
    O["pm"] = dout("pm", [2, 4])
    O["pS"] = dout("pS", [2, 4, 64, 128])
    O["pckv"] = dout("pckv", [2, SEQ, 256])
    O["pkr"] = dout("pkr", [2, SEQ, 64])
    O["sC"] = dout("sC", [2, 4, 128, 128])
    O["sn"] = dout("sn", [2, 4, 128])
    O["sm"] = dout("sm", [2, 4])
    O["sS"] = dout("sS", [2, 4, 64, 128])
    O["sckv"] = dout("sckv", [2, DEC_SEQ, 256])
    O["skr"] = dout("skr", [2, DEC_SEQ, 64])

    P = Prog(nc, arena_bytes=207 * 1024)
    P.store_sems = [P.new_dma_sem() for _ in range(8)]
    ps = P.psum

    def PS(i, parts=128, cols=512, c0=0, p0=0):
        return ps[i].t.ap()[p0:p0 + parts, c0:c0 + cols]

    def PSB(i):
        return ps[i].t.ap().bitcast(BF16)

    spec = _wspec()
    WS = {}
    need = set()
    if "mix0" in stages:
        need |= {"A", "OA"}
    if "ffn0" in stages:
        need |= {"GU0", "DN0"}
    if "mix1" in stages:
        need |= {"C", "UQ", "UKV", "OC"}
    if "ffn1" in stages:
        need |= {"GU1", "DN1"}
    for name in ("A", "OA", "GU0", "DN0", "C", "UKV", "UQ", "OC", "GU1", "DN1"):
        (src, li, K, groups) = spec[name]
        if name not in need:
            continue
        M = I[src].shape[2]
        scr = dscr("ws_%s" % name, [K, M])
        o = Obj()
        for c0 in range(0, M, 2048):
            c1 = min(M, c0 + 2048)
            P.store(scr[:, c0:c1], I[src][li, :, c0:c1], R=(), W=(o,))
        WS[name] = (scr, o, K // 128, [_merge(g) for g in groups])

    sem_c = P.new_dma_sem()
    ident = P.alloc([128, 128], F32)
    P.load(ident.ap, I["ident"], sem_c, W=(ident,))
    identb = P.alloc([128, 128], BF16)
    P.dve(lambda h: h.tensor_copy(out=identb.ap, in_=ident.ap), R=(ident,), W=(identb,))
    onesb = P.alloc([128, 128], BF16)
    P.pool(lambda h: h.memset(onesb.ap, 1.0), W=(onesb,))
    lnp = P.alloc([128, 8, 8], F32)
    lnrow = P.alloc([8, 1024], F32, at=P.arena_bytes - 4096)
    P.load(lnrow.ap, I["ln"], sem_c, W=(lnrow,))
    for m in range(8):
        P.pe(lambda h, m=m: h.transpose(out=ps[0].t.ap()[:, m * 8:(m + 1) * 8], in_=lnrow.ap[0:8, m * 128:(m + 1) * 128],
                                        identity=ident.ap[0:8, 0:8]), R=(lnrow, ident), W=(ps[0],))
    P.dve(lambda h: h.tensor_copy(out=lnp.ap.rearrange("p w m -> p m w"), in_=ps[0].t.ap()[:, 0:64].rearrange("p (m w) -> p m w", m=8)),
          R=(ps[0],), W=(lnp,))

    NW = 3
    WSLOT = 4096
    wring = [P.alloc([128, WSLOT], BF16) for _ in range(NW)]
    wsem = [P.new_dma_sem() for _ in range(NW)]
    wrr = [0]

    def wload(name, gi):
        scr, o, KC, groups = WS[name]
        pieces = groups[gi]
        i = wrr[0] % NW
        wrr[0] += 1
        slot = wring[i]
        n = pieces[0][1]
        cnt = len(pieces)
        ncols = cnt * n
        assert KC * ncols <= WSLOT
        if cnt == 1:
            c0 = pieces[0][0]
            srcap = scr[:, c0:c0 + n].rearrange("(kc p) c -> p kc c", p=128)
            dstap = slot.ap[:, 0:KC * n].rearrange("p (k c) -> p k c", k=KC)
            v = slot.ap[:, 0:KC * n].rearrange("p (k c) -> p k c", k=KC)
            P.load(dstap, srcap, wsem[i], R=(o,), W=(slot,))
            return slot, v
        assert all(p_[1] == n for p_ in pieces)
        v = slot.ap[:, 0:KC * ncols].rearrange("p (k c) -> p k c", k=KC)
        for j, (c0, _) in enumerate(pieces):
            srcap = scr[:, c0:c0 + n].rearrange("(kc p) c -> p kc c", p=128)
            dstap = v[:, :, j * n:(j + 1) * n]
            if j == 0:
                P.load(dstap, srcap, wsem[i], R=(o,), W=(slot,))
            else:
                P.emit("sp", lambda h: h.dma_start(out=dstap, in_=srcap), (), (), dma_sem=wsem[i], chain=True)
        ftok = (wsem[i], P.dma_cnt[wsem[i]])
        for r in slot.res:
            r.w = ftok
        return slot, v

    def wswap(slot, w, KC, ncols):
        wsw = slot.ap[:, KC * ncols:2 * KC * ncols].rearrange("p (k c) -> p k c", k=KC)
        src = w.rearrange("p k (h t j) -> p k h t j", t=2, j=32)
        dst = wsw.rearrange("p k (h t j) -> p k h t j", t=2, j=32)
        P.dve(lambda h: h.tensor_copy(out=dst[:, :, :, 0, :], in_=src[:, :, :, 1, :]), R=(slot,), W=(slot,))
        P.act(lambda h: h.activation(out=dst[:, :, :, 1, :], in_=src[:, :, :, 0, :], func=AF.Copy), R=(slot,), W=(slot,))
        return wsw

    xT = P.alloc([128, 8, 512], F32)
    xTb = P.alloc([128, 8, 512], BF16)
    def subs(buf, n):
        k = len(buf.res) // n
        assert k * n == len(buf.res)
        return [Buf(buf.ap[:, i], buf.res[i * k:(i + 1) * k]) for i in range(n)]
    xTc = subs(xT, 8)
    xTbc = subs(xTb, 8)
    T = [P.alloc([128, 512], F32) for _ in range(4)]
    TB = [P.alloc([128, 512], BF16) for _ in range(4)]
    xsem = [P.new_dma_sem() for _ in range(2)]
    base_mark = P.mark()

    psrr = [0]

    def nextps(lo=0, hi=4):
        i = lo + psrr[0] % (hi - lo)
        psrr[0] += 1
        return i

    def load_x(src_rows, N):
        nb = (N + 127) // 128
        m0 = P.mark()
        for b in range(nb):
            bs = min(128, N - b * 128)
            xin = P.alloc([128, 1024], F32, at=P.arena_bytes - (b + 1) * 4096)
            P.load(xin.ap[0:bs, :], src_rows[b * 128:b * 128 + bs, :], xsem[b % 2], W=(xin,))
            for half in range(2):
                pi = nextps()
                for j in range(4):
                    m = half * 4 + j
                    P.pe(lambda h, m=m, j=j, pi=pi, xin=xin, bs=bs: h.transpose(
                        out=PS(pi, 128, 128, j * 128), in_=xin.ap[:, m * 128:(m + 1) * 128],
                        identity=ident.ap), R=(xin, ident), W=(ps[pi],))
                src = ps[pi].t.ap().rearrange("p (j c) -> p j c", j=4)[:, :, 0:bs]
                P.act(lambda h, src=src, half=half, b=b, bs=bs: h.activation(
                    out=xT.ap[:, half * 4:half * 4 + 4, b * 128:b * 128 + bs], in_=src, func=AF.Copy),
                    R=(ps[pi],), W=(xT,))
                P.dve(lambda h, src=src, half=half, b=b, bs=bs: h.tensor_copy(
                    out=xTb.ap[:, half * 4:half * 4 + 4, b * 128:b * 128 + bs], in_=src),
                    R=(ps[pi],), W=(xTb,))
            P.release(P.mark())
        if not DBG.get("norel"):
            P.release(m0)

    def store_y(dst_rows, N):
        nb = (N + 127) // 128
        m0 = P.mark()
        for b in range(nb):
            bs = min(128, N - b * 128)
            yo = P.alloc([128, 1024], F32)
            for half in range(2):
                pi = nextps()
                for j in range(4):
                    m = half * 4 + j
                    P.pe(lambda h, m=m, j=j, pi=pi, b=b, bs=bs: h.transpose(
                        out=PS(pi, 128, 128, j * 128), in_=xT.ap[:, m, b * 128:(b + 1) * 128],
                        identity=ident.ap), R=(xT, ident), W=(ps[pi],))
                if half == 0:
                    P.act(lambda h, pi=pi, yo=yo, bs=bs: h.activation(
                        out=yo.ap[0:bs, 0:512], in_=PS(pi, bs, 512), func=AF.Copy), R=(ps[pi],), W=(yo,))
                else:
                    P.dve(lambda h, pi=pi, yo=yo, bs=bs: h.tensor_copy(
                        out=yo.ap[0:bs, 512:1024], in_=PS(pi, bs, 512)), R=(ps[pi],), W=(yo,))
            P.store(dst_rows[b * 128:b * 128 + bs, :], yo.ap[0:bs, :], R=(yo,))
        if not DBG.get("norel"):
            P.release(m0)

    def fm_norm(chunks, N, F, eps, center):
        n = len(chunks)
        p2 = nextps()
        p1 = nextps() if center else None
        for i, (b, ap) in enumerate(chunks):
            sq = TB[i % 2]
            P.act(lambda h: h.activation(out=sq.ap[:, 0:N], in_=ap, func=AF.Square), R=(b,), W=(sq,))
            P.pe(lambda h: h.matmul(PS(p2, 128, N), lhsT=onesb.ap, rhs=sq.ap[:, 0:N], start=(i == 0), stop=(i == n - 1)),
                 R=(sq, onesb), W=(ps[p2],))
            if center:
                zb = TB[2 + i % 2]
                P.dve(lambda h: h.tensor_copy(out=zb.ap[:, 0:N], in_=ap), R=(b,), W=(zb,))
                P.pe(lambda h: h.matmul(PS(p1, 128, N), lhsT=onesb.ap, rhs=zb.ap[:, 0:N], start=(i == 0), stop=(i == n - 1)),
                     R=(zb, onesb), W=(ps[p1],))
        mean, var = T[3], T[2]
        if center:
            P.act(lambda h: h.activation(out=mean.ap[:, 0:N], in_=PS(p1, 128, N), func=AF.Identity, scale=1.0 / F),
                  R=(ps[p1],), W=(mean,))
            P.act(lambda h: h.activation(out=var.ap[:, 0:N], in_=mean.ap[:, 0:N], func=AF.Square), R=(mean,), W=(var,))
            P.dve(lambda h: h.scalar_tensor_tensor(out=var.ap[:, 0:N], in0=PS(p2, 128, N), scalar=1.0 / F, in1=var.ap[:, 0:N],
                                                   op0=ALU.mult, op1=ALU.subtract), R=(ps[p2], var), W=(var,))
            P.dve(lambda h: h.tensor_scalar(out=var.ap[:, 0:N], in0=var.ap[:, 0:N], scalar1=0.0, scalar2=float(eps),
                                            op0=ALU.max, op1=ALU.add), R=(var,), W=(var,))
        else:
            P.dve(lambda h: h.tensor_scalar(out=var.ap[:, 0:N], in0=PS(p2, 128, N), scalar1=1.0 / F, scalar2=float(eps),
                                            op0=ALU.mult, op1=ALU.add), R=(ps[p2],), W=(var,))
        P.act(lambda h: h.activation(out=var.ap[:, 0:N], in_=var.ap[:, 0:N], func=AF.Ln), R=(var,), W=(var,))
        P.act(lambda h: h.activation(out=var.ap[:, 0:N], in_=var.ap[:, 0:N], func=AF.Exp, scale=-0.5), R=(var,), W=(var,))
        for i, (b, ap) in enumerate(chunks):
            if center:
                P.dve(lambda h: h.tensor_tensor(out=ap, in0=ap, in1=mean.ap[:, 0:N], op=ALU.subtract), R=(b, mean), W=(b,))
            P.dve(lambda h: h.tensor_tensor(out=ap, in0=ap, in1=var.ap[:, 0:N], op=ALU.mult), R=(b, var), W=(b,))

    epsb = {}
    for e in (LN_EPS, RMS_EPS):
        eb = P.alloc([128, 1], F32)
        P.pool(lambda h, eb=eb, e=e: h.memset(eb.ap, e), W=(eb,))
        epsb[e] = eb
    base_mark = P.mark()

    def ln_affine(N, gi, bi):
        for m in range(8):
            P.act(lambda h: h.activation(out=xTb.ap[:, m, 0:N], in_=xT.ap[:, m, 0:N], func=AF.Identity,
                                         scale=lnp.ap[:, gi, m:m + 1], bias=lnp.ap[:, bi, m:m + 1]),
                  R=(xTc[m], lnp), W=(xTbc[m],))
            P.act(lambda h: h.activation(out=xT.ap[:, m, 0:N], in_=xT.ap[:, m, 0:N], func=AF.Identity,
                                         scale=lnp.ap[:, gi, m:m + 1], bias=lnp.ap[:, bi, m:m + 1]),
                  R=(xTc[m], lnp), W=(xTc[m],))

    ln_pending = []
    LN1, LN2 = 4, 5

    def resid_chunk(m, pi, N):
        while ln_pending:
            ln_pending.pop(0)()
        zb, sq = TB[2 + m % 2], TB[m % 2]
        P.dve(lambda h: h.scalar_tensor_tensor(out=xT.ap[:, m, 0:N], in0=xT.ap[:, m, 0:N], scalar=ALPHA, in1=PS(pi, 128, N),
                                               op0=ALU.mult, op1=ALU.add), R=(xTc[m], ps[pi]), W=(xTc[m],))
        P.dve(lambda h: h.tensor_copy(out=zb.ap[:, 0:N], in_=xT.ap[:, m, 0:N]), R=(xTc[m],), W=(zb,))
        P.act(lambda h: h.activation(out=sq.ap[:, 0:N], in_=xT.ap[:, m, 0:N], func=AF.Square), R=(xTc[m],), W=(sq,))
        def stat_mm(m=m, zb=zb, sq=sq, N=N):
            P.pe(lambda h: h.matmul(PS(LN1, 128, N), lhsT=onesb.ap, rhs=zb.ap[:, 0:N], start=(m == 0), stop=(m == 7)),
                 R=(zb, onesb), W=(ps[LN1],))
            P.pe(lambda h: h.matmul(PS(LN2, 128, N), lhsT=onesb.ap, rhs=sq.ap[:, 0:N], start=(m == 0), stop=(m == 7)),
                 R=(sq, onesb), W=(ps[LN2],))
        ln_pending.append(stat_mm)

    def residual_ln(N, lay, which):
        while ln_pending:
            ln_pending.pop(0)()
        F = float(D)
        gi = (0 if which == 0 else 4) + lay
        bi = gi + 2
        mean, var = T[3], T[2]
        P.act(lambda h: h.activation(out=mean.ap[:, 0:N], in_=PS(LN1, 128, N), func=AF.Identity, scale=1.0 / F),
              R=(ps[LN1],), W=(mean,))
        P.act(lambda h: h.activation(out=var.ap[:, 0:N], in_=mean.ap[:, 0:N], func=AF.Square), R=(mean,), W=(var,))
        P.dve(lambda h: h.scalar_tensor_tensor(out=var.ap[:, 0:N], in0=PS(LN2, 128, N), scalar=1.0 / F, in1=var.ap[:, 0:N],
                                               op0=ALU.mult, op1=ALU.subtract), R=(ps[LN2], var), W=(var,))
        P.dve(lambda h: h.tensor_scalar(out=var.ap[:, 0:N], in0=var.ap[:, 0:N], scalar1=0.0, scalar2=float(LN_EPS),
                                        op0=ALU.max, op1=ALU.add), R=(var,), W=(var,))
        P.act(lambda h: h.activation(out=var.ap[:, 0:N], in_=var.ap[:, 0:N], func=AF.Ln), R=(var,), W=(var,))
        P.act(lambda h: h.activation(out=var.ap[:, 0:N], in_=var.ap[:, 0:N], func=AF.Exp, scale=-0.5), R=(var,), W=(var,))
        for m in range(8):
            P.dve(lambda h: h.tensor_tensor(out=xT.ap[:, m, 0:N], in0=xT.ap[:, m, 0:N], in1=mean.ap[:, 0:N], op=ALU.subtract),
                  R=(xTc[m], mean), W=(xTc[m],))
            P.dve(lambda h: h.tensor_tensor(out=xT.ap[:, m, 0:N], in0=xT.ap[:, m, 0:N], in1=var.ap[:, 0:N], op=ALU.mult),
                  R=(xTc[m], var), W=(xTc[m],))
            P.act(lambda h: h.activation(out=xTb.ap[:, m, 0:N], in_=xT.ap[:, m, 0:N], func=AF.Identity,
                                         scale=lnp.ap[:, gi, m:m + 1], bias=lnp.ap[:, bi, m:m + 1]),
                  R=(xTc[m], lnp), W=(xTbc[m],))
        for m in range(8):
            P.act(lambda h: h.activation(out=xT.ap[:, m, 0:N], in_=xT.ap[:, m, 0:N], func=AF.Identity,
                                         scale=lnp.ap[:, gi, m:m + 1], bias=lnp.ap[:, bi, m:m + 1]),
                  R=(xTc[m], lnp), W=(xTc[m],))

    def ffn(N, lay):
        m0 = P.mark()
        aT = P.alloc([128, 22, 512], BF16)
        gu = "GU%d" % lay
        for grp in range(11):
            slot, w = wload(gu, grp)
            for j in range(2):
                m = 2 * grp + j
                pg = nextps()
                pu = nextps()
                for kc in range(8):
                    P.pe(lambda h, kc=kc, j=j, w=w, pg=pg: h.matmul(
                        PS(pg, 128, N), lhsT=w[:, kc, j * 128:(j + 1) * 128], rhs=xTb.ap[:, kc, 0:N],
                        start=(kc == 0), stop=(kc == 7)), R=(slot, xTb), W=(ps[pg],))
                for kc in range(8):
                    P.pe(lambda h, kc=kc, j=j, w=w, pu=pu: h.matmul(
                        PS(pu, 128, N), lhsT=w[:, kc, 256 + j * 128:256 + (j + 1) * 128], rhs=xTb.ap[:, kc, 0:N],
                        start=(kc == 0), stop=(kc == 7)), R=(slot, xTb), W=(ps[pu],))
                sg = T[m % 2]
                P.act(lambda h, sg=sg, pg=pg: h.activation(out=sg.ap[:, 0:N], in_=PS(pg, 128, N), func=AF.Silu),
                      R=(ps[pg],), W=(sg,))
                P.dve(lambda h, sg=sg, pu=pu, m=m: h.tensor_tensor(
                    out=aT.ap[:, m, 0:N], in0=PS(pu, 128, N), in1=sg.ap[:, 0:N], op=ALU.mult),
                    R=(ps[pu], sg), W=(aT,))
        dn = "DN%d" % lay
        for m in range(8):
            slot, w = wload(dn, m)
            pi = nextps()
            for kc in range(22):
                P.pe(lambda h, kc=kc, w=w, pi=pi: h.matmul(
                    PS(pi, 128, N), lhsT=w[:, kc, :], rhs=aT.ap[:, kc, 0:N],
                    start=(kc == 0), stop=(kc == 21)), R=(slot, aT), W=(ps[pi],))
            resid_chunk(m, pi, N)
        residual_ln(N, lay, 1)
        P.release(m0)

    vrow = P.alloc([2, 1024], F32, at=P.arena_bytes - 8192)
    P.load(vrow.ap[0:1, 0:512], I["g_ml"].rearrange("(o n) -> o n", o=1), sem_c, W=(vrow,))
    P.load(vrow.ap[0:1, 512:1024], I["g_ret"].rearrange("(o n) -> o n", o=1), sem_c, W=(vrow,))
    P.load(vrow.ap[1:2, 0:512], I["g_q"].rearrange("(o n) -> o n", o=1), sem_c, W=(vrow,))
    P.load(vrow.ap[1:2, 512:1024], I["g_q"].rearrange("(o n) -> o n", o=1), sem_c, W=(vrow,))
    vecp = P.alloc([128, 2, 8], F32)
    for m in range(8):
        P.pe(lambda h, m=m: h.transpose(out=ps[1].t.ap()[:, m * 2:(m + 1) * 2], in_=vrow.ap[0:2, m * 128:(m + 1) * 128],
                                        identity=ident.ap[0:2, 0:2]), R=(vrow, ident), W=(ps[1],))
    P.dve(lambda h: h.tensor_copy(out=vecp.ap.rearrange("p w m -> p m w"),
                                  in_=ps[1].t.ap()[:, 0:16].rearrange("p (m w) -> p m w", m=8)), R=(ps[1],), W=(vecp,))
    bif = P.alloc([4, 2], F32)
    P.load(bif.ap, I["b_if"].rearrange("(t h) -> h t", t=2), sem_c, W=(bif,), allow_slow_non_contiguous=True)
    nbf = P.alloc([4, 1], F32)
    P.dve(lambda h: h.tensor_scalar(out=nbf.ap, in0=bif.ap[:, 1:2], scalar1=-1.0, scalar2=None, op0=ALU.mult),
          R=(bif,), W=(nbf,))
    ones4 = P.alloc([4, 512], F32)
    P.pool(lambda h: h.memset(ones4.ap, 1.0), W=(ones4,))
    sel = P.alloc([4, 512], F32)
    P.load(sel.ap, I["sel"].rearrange("k h c -> k (h c)"), sem_c, W=(sel,))
    maskml = P.alloc([128, 64], F32)
    P.load(maskml.ap, I["mask_ml"], sem_c, W=(maskml,))
    mret = P.alloc([128, 4, 64], F32)
    P.load(mret.ap, I["mret"], sem_c, W=(mret,))
    kdec = P.alloc([128, 2, 128], F32)
    P.load(kdec.ap, I["kdec"], sem_c, W=(kdec,))
    qdec = P.alloc([128, 2, 64], F32)
    P.load(qdec.ap, I["qdec"], sem_c, W=(qdec,))
    gkvb = P.alloc([128, 256], F32)
    P.load(gkvb.ap, I["g_kv"].partition_broadcast(128), sem_c, W=(gkvb,))
    dmask = P.alloc([128, 128], F32)
    P.load(dmask.ap, I["dmask"], sem_c, W=(dmask,))
    CaugH = [[P.alloc([128, 256], F32) for _ in range(2)] for _ in range(4)]
    SH = [[P.alloc([128, 128], F32) for _ in range(2)] for _ in range(4)]
    ccur = [0, 0, 0, 0]
    scur = [0, 0, 0, 0]
    Bc = P.alloc([4, 1], F32)
    Gc = P.alloc([4, 1], F32)
    cosb = P.alloc([128, 512], F32)
    sinb = P.alloc([128, 512], F32)
    tsem = P.new_dma_sem()
    GAM = [1.0 - 2.0 ** (-5.0 - h) for h in range(4)]

    def state_zero():
        for hh in range(4):
            ccur[hh] = 0
            scur[hh] = 0
            P.pool(lambda h: h.memset(CaugH[hh][0].ap, 0.0), W=(CaugH[hh][0],))
            P.pool(lambda h: h.memset(SH[hh][0].ap, 0.0), W=(SH[hh][0],))
        P.pool(lambda h: h.memset(Bc.ap, 0.0), W=(Bc,))
        P.pool(lambda h: h.memset(Gc.ap, 0.0), W=(Gc,))

    def state_load(s):
        m0 = P.mark()
        cin = P.alloc([128, 4, 128], F32)
        P.load(cin.ap, I["stC"][s].rearrange("h v d -> v h d"), tsem, W=(cin,))
        nrow = P.alloc([4, 128], F32)
        P.load(nrow.ap, I["stn"][s], tsem, W=(nrow,))
        pi = nextps()
        for hh in range(4):
            P.pe(lambda h: h.transpose(out=PS(pi, 128, 128, hh * 128), in_=cin.ap[:, hh, :], identity=ident.ap),
                 R=(cin, ident), W=(ps[pi],))
        pj = nextps()
        P.pe(lambda h: h.transpose(out=PS(pj, 128, 4), in_=nrow.ap[0:4, :], identity=ident.ap[0:4, 0:4]),
             R=(nrow, ident), W=(ps[pj],))
        ncol = P.alloc([128, 4], F32)
        P.act(lambda h: h.activation(out=ncol.ap, in_=PS(pj, 128, 4), func=AF.Copy), R=(ps[pj],), W=(ncol,))
        for hh in range(4):
            ccur[hh] = 0
            scur[hh] = 0
            ho = (hh % 2) * 64
            P.act(lambda h: h.activation(out=CaugH[hh][0].ap[:, 0:128], in_=PS(pi, 128, 128, hh * 128), func=AF.Copy),
                  R=(ps[pi],), W=(CaugH[hh][0],))
            P.dve(lambda h: h.tensor_copy(out=CaugH[hh][0].ap[:, 128:256], in_=ncol.ap[:, hh:hh + 1].broadcast_to([128, 128])),
                  R=(ncol,), W=(CaugH[hh][0],))
            P.load(SH[hh][0].ap[ho:ho + 64, :], I["stS"][s, hh], tsem, W=(SH[hh][0],))
        P.load(Gc.ap, I["stm"][s].rearrange("(h o) -> h o", o=1), tsem, W=(Gc,))
        P.pool(lambda h: h.memset(Bc.ap, 0.0), W=(Bc,))
        P.release(m0)

    def state_store(kC, kn, km, kS, s):
        m0 = P.mark()
        co = P.alloc([128, 4, 128], F32)
        no = P.alloc([128, 4], F32)
        pi = nextps()
        for hh in range(4):
            cb_ = CaugH[hh][ccur[hh]]
            P.pe(lambda h: h.transpose(out=PS(pi, 128, 128, hh * 128), in_=cb_.ap[:, 0:128], identity=ident.ap),
                 R=(cb_, ident), W=(ps[pi],))
            P.dve(lambda h: h.tensor_copy(out=no.ap[:, hh:hh + 1], in_=cb_.ap[:, 128:129]), R=(cb_,), W=(no,))
        P.act(lambda h: h.activation(out=co.ap, in_=ps[pi].t.ap().rearrange("p (h c) -> p h c", h=4), func=AF.Copy),
              R=(ps[pi],), W=(co,))
        P.store(O[kC][s].rearrange("h v d -> v h d"), co.ap, R=(co,))
        P.store(O[kn][s].rearrange("h d -> d h"), no.ap, R=(no,), allow_slow_non_contiguous=True)
        mo_ = P.alloc([4, 1], F32)
        P.dve(lambda h: h.tensor_tensor(out=mo_.ap, in0=Bc.ap, in1=Gc.ap, op=ALU.add), R=(Bc, Gc), W=(mo_,))
        P.store(O[km][s].rearrange("(h o) -> h o", o=1), mo_.ap, R=(mo_,))
        for hh in range(4):
            ho = (hh % 2) * 64
            sb_ = SH[hh][scur[hh]]
            P.store(O[kS][s, hh], sb_.ap[ho:ho + 64, :], R=(sb_,))
        P.release(m0)

    def mix0(N, pos0):
        nb = (N + 127) // 128
        nch = N // 64
        m0 = P.mark()
        qTm = P.alloc([128, 4, 512], BF16)
        kTm = P.alloc([128, 4, 512], BF16)
        sigo = P.alloc([128, 4, 512], F32)
        silg = P.alloc([128, 4, 512], F32)
        rqT = P.alloc([128, 2, 512], BF16)
        rqd = P.alloc([128, 2, 512], BF16)
        rkT = P.alloc([128, 2, 512], BF16)
        mv_tm = P.alloc([128, 4, 512], BF16)
        rv_tm = P.alloc([128, 4, 512], BF16)
        rk_tm = P.alloc([128, 4, 256], BF16)
        mixed = P.alloc([128, 8, 512], BF16)
        GR = P.alloc([4, 6, 512], F32)
        wk_tm = P.alloc([128, 4, 4], F32)
        dec_b = P.alloc([128, 4, 8], F32)
        kp_tmL = [P.alloc([128, 4, 128], BF16) for _ in range(2)]
        MWL = [P.alloc([128, 4, 64], F32) for _ in range(2)]
        PTL = [P.alloc([128, 4, 64], BF16) for _ in range(4)]
        CbL = [[P.alloc([128, 256], BF16) for _ in range(3)] for _ in range(2)]
        SbL = [[P.alloc([128, 128], BF16) for _ in range(3)] for _ in range(4)]
        for hh_ in range(4):
            for b_ in SbL[hh_]:
                P.pool(lambda h: h.memset(b_.ap, 0.0), W=(b_,))
        hT = [P.alloc([128, 512], F32) for _ in range(2)]
        P.load(cosb.ap[:, 0:N], I["cosF"][:, pos0:pos0 + N], tsem, W=(cosb,))
        P.load(sinb.ap[:, 0:N], I["sinF"][:, pos0:pos0 + N], tsem, W=(sinb,))

        def fm_chain(w, c0, pi, slot):
            for kc in range(8):
                P.pe(lambda h, kc=kc: h.matmul(PS(pi, 128, N), lhsT=w[:, kc, c0:c0 + 128], rhs=xTb.ap[:, kc, 0:N],
                                               start=(kc == 0), stop=(kc == 7)), R=(slot, xTb), W=(ps[pi],))

        for grp, dst, fn, sc in ((0, qTm, AF.Identity, 1.0), (1, kTm, AF.Identity, 128.0 ** -0.5),
                                 (2, sigo, AF.Sigmoid, 1.0), (5, silg, AF.Silu, 1.0)):
            slot, w = wload("A", grp)
            for hh in range(4):
                pi = nextps()
                fm_chain(w, hh * 128, pi, slot)
                P.act(lambda h, hh=hh, pi=pi, dst=dst, fn=fn, sc=sc: h.activation(
                    out=dst.ap[:, hh, 0:N], in_=PS(pi, 128, N), func=fn, scale=sc), R=(ps[pi],), W=(dst,))
        for grp, dst in ((3, rqT), (4, rkT)):
            slot, w = wload("A", grp)
            wsw = wswap(slot, w, 8, 256)
            for pr in range(2):
                pa = nextps()
                fm_chain(w, pr * 128, pa, slot)
                pb = nextps()
                fm_chain(wsw, pr * 128, pb, slot)
                P.dve(lambda h, pa=pa: h.tensor_tensor(out=T[0].ap[:, 0:N], in0=PS(pa, 128, N), in1=cosb.ap[:, 0:N], op=ALU.mult),
                      R=(ps[pa], cosb), W=(T[0],))
                P.dve(lambda h, pb=pb: h.tensor_tensor(out=T[1].ap[:, 0:N], in0=PS(pb, 128, N), in1=sinb.ap[:, 0:N], op=ALU.mult),
                      R=(ps[pb], sinb), W=(T[1],))
                P.pool(lambda h: h.tensor_tensor(out=T[0].ap[:, 0:N], in0=T[0].ap[:, 0:N], in1=T[1].ap[:, 0:N], op=ALU.add),
                       R=(T[0], T[1]), W=(T[0],))
                P.act(lambda h, dst=dst, pr=pr: h.activation(out=dst.ap[:, pr, 0:N], in_=T[0].ap[:, 0:N], func=AF.Copy),
                      R=(T[0],), W=(dst,))
                if grp == 3:
                    P.dve(lambda h, pr=pr: h.tensor_tensor(
                        out=rqd.ap[:, pr, 0:N].rearrange("p (c t) -> p c t", t=64),
                        in0=T[0].ap[:, 0:N].rearrange("p (c t) -> p c t", t=64),
                        in1=qdec.ap[:, pr:pr + 1, :].broadcast_to([128, nch, 64]), op=ALU.mult),
                        R=(T[0], qdec), W=(rqd,))
        if DBG.get('stop') == 1:
            P.release(m0)
            return
        DBG.get('phase_hook', lambda n, p: None)('m0_vproj', P)
        for grp, dst in ((6, mv_tm), (7, rv_tm)):
            slot, w = wload("A", grp)
            for b in range(nb):
                bs = min(128, N - b * 128)
                pi = nextps()
                for kc in range(8):
                    P.pe(lambda h, kc=kc, b=b, bs=bs, pi=pi, w=w: h.matmul(
                        PS(pi, bs, 512), lhsT=xTb.ap[:, kc, b * 128:b * 128 + bs], rhs=w[:, kc, 0:512],
                        start=(kc == 0), stop=(kc == 7)), R=(slot, xTb), W=(ps[pi],))
                if b % 2 == 0:
                    P.act(lambda h, b=b, bs=bs, pi=pi, dst=dst: h.activation(out=dst.ap[0:bs, b, :], in_=PS(pi, bs, 512), func=AF.Copy),
                          R=(ps[pi],), W=(dst,))
                else:
                    P.dve(lambda h, b=b, bs=bs, pi=pi, dst=dst: h.tensor_copy(out=dst.ap[0:bs, b, :], in_=PS(pi, bs, 512)),
                          R=(ps[pi],), W=(dst,))
        if DBG.get('stop') == 2:
            P.release(m0)
            return
        DBG.get('phase_hook', lambda n, p: None)('m0_gates', P)
        slot, w = wload("A", 8)
        pig = nextps()
        pfg = nextps()
        for kc in range(8):
            P.pe(lambda h, kc=kc: h.matmul(PS(pig, 4, N), lhsT=w[:, kc, 0:4], rhs=xTb.ap[:, kc, 0:N],
                                           start=(kc == 0), stop=(kc == 7)), R=(slot, xTb), W=(ps[pig],))
        for kc in range(8):
            P.pe(lambda h, kc=kc: h.matmul(PS(pfg, 4, N), lhsT=w[:, kc, 4:8], rhs=xTb.ap[:, kc, 0:N],
                                           start=(kc == 0), stop=(kc == 7)), R=(slot, xTb), W=(ps[pfg],))
        gL, gB, gA, gG, gX, gE = [GR.ap[:, i, :] for i in range(6)]
        P.act(lambda h: h.activation(out=gL[:, 0:N], in_=PS(pfg, 4, N), func=AF.Exp, scale=-1.0, bias=nbf.ap[:, 0:1]),
              R=(ps[pfg], nbf), W=(GR,))
        P.act(lambda h: h.activation(out=gL[:, 0:N], in_=gL[:, 0:N], func=AF.Ln, bias=1.0, scale=1.0), R=(GR,), W=(GR,))
        P.dve(lambda h: h.tensor_tensor_scan(out=gB[:, 0:N], data0=ones4.ap[:, 0:N], data1=gL[:, 0:N], initial=Bc.ap[:, 0:1],
                                             op0=ALU.mult, op1=ALU.subtract), R=(GR, ones4, Bc), W=(GR,))
        P.dve(lambda h: h.scalar_tensor_tensor(out=gA[:, 0:N], in0=PS(pig, 4, N), scalar=bif.ap[:, 0:1], in1=gB[:, 0:N],
                                               op0=ALU.add, op1=ALU.subtract), R=(ps[pig], bif, GR), W=(GR,))
        P.dve(lambda h: h.tensor_tensor_scan(out=gG[:, 0:N], data0=ones4.ap[:, 0:N], data1=gA[:, 0:N], initial=Gc.ap[:, 0:1],
                                             op0=ALU.mult, op1=ALU.max), R=(GR, ones4, Gc), W=(GR,))
        g3 = lambda a: a[:, 0:N].rearrange("p (c t) -> p c t", t=64)
        P.act(lambda h: h.activation(out=g3(gX), in_=g3(gG)[:, :, 63:64].broadcast_to([4, nch, 64]), func=AF.Copy),
              R=(GR,), W=(GR,))
        P.dve(lambda h: h.tensor_copy(out=gE[:, 0:1], in_=Gc.ap[:, 0:1]), R=(Gc,), W=(GR,))
        if nch > 1:
            P.dve(lambda h: h.tensor_copy(out=gE[:, 1:nch], in_=g3(gG)[:, 0:nch - 1, 63]), R=(GR,), W=(GR,))
        P.dve(lambda h: h.tensor_tensor(out=gE[:, 0:nch], in0=gE[:, 0:nch], in1=g3(gG)[:, :, 63], op=ALU.subtract),
              R=(GR,), W=(GR,))
        P.act(lambda h: h.activation(out=gE[:, 0:nch], in_=gE[:, 0:nch], func=AF.Exp), R=(GR,), W=(GR,))
        P.dve(lambda h: h.tensor_tensor(out=gA[:, 0:N], in0=gA[:, 0:N], in1=gX[:, 0:N], op=ALU.subtract), R=(GR,), W=(GR,))
        P.act(lambda h: h.activation(out=gA[:, 0:N], in_=gA[:, 0:N], func=AF.Exp), R=(GR,), W=(GR,))
        P.dve(lambda h: h.tensor_tensor(out=gX[:, 0:N], in0=gX[:, 0:N], in1=gB[:, 0:N], op=ALU.add), R=(GR,), W=(GR,))
        P.act(lambda h: h.activation(out=Bc.ap, in_=gB[:, N - 1:N], func=AF.Copy), R=(GR,), W=(Bc,))
        P.act(lambda h: h.activation(out=Gc.ap, in_=gG[:, N - 1:N], func=AF.Copy), R=(GR,), W=(Gc,))
        if DBG.get('stop') == 3:
            P.release(m0)
            return
        pi = nextps()
        for b in range(nb):
            P.pe(lambda h, b=b, pi=pi: h.transpose(out=PS(pi, 128, 4, b * 4), in_=gA[0:4, b * 128:(b + 1) * 128],
                                                   identity=ident.ap[0:4, 0:4]), R=(GR, ident), W=(ps[pi],))
        P.dve(lambda h, pi=pi: h.tensor_copy(out=wk_tm.ap[:, 0:nb, :], in_=PS(pi, 128, 4 * nb).rearrange("p (b f) -> p b f", f=4)),
              R=(ps[pi],), W=(wk_tm,))
        pi = nextps()
        for hh in range(4):
            P.pe(lambda h, hh=hh, pi=pi: h.matmul(PS(pi, 128, nch, hh * 8), lhsT=sel.ap[0:4, hh * 128:(hh + 1) * 128],
                                                  rhs=gE[0:4, 0:nch], start=True, stop=True), R=(sel, GR), W=(ps[pi],))
        P.dve(lambda h, pi=pi: h.tensor_copy(out=dec_b.ap[:, :, 0:nch],
                                             in_=PS(pi, 128, 32).rearrange("p (a c) -> p a c", c=8)[:, :, 0:nch]),
              R=(ps[pi],), W=(dec_b,))
        for pr in range(2):
            pi = nextps()
            for b in range(nb):
                bs = min(128, N - b * 128)
                P.pe(lambda h, b=b, bs=bs, pi=pi, pr=pr: h.transpose(
                    out=PSB(pi)[0:bs, b * 128:(b + 1) * 128], in_=rkT.ap[:, pr, b * 128:b * 128 + bs], identity=identb.ap),
                    R=(rkT, identb), W=(ps[pi],))
            for b in range(nb):
                bs = min(128, N - b * 128)
                P.dve(lambda h, b=b, bs=bs, pi=pi, pr=pr: h.tensor_tensor(
                    out=rk_tm.ap[0:bs, b, pr * 128:(pr + 1) * 128], in0=PSB(pi)[0:bs, b * 128:(b + 1) * 128],
                    in1=kdec.ap[0:bs, pr, :], op=ALU.mult), R=(ps[pi], kdec), W=(rk_tm,))

        if DBG.get('stop') == 4:
            P.release(m0)
            return
        DBG.get('phase_hook', lambda n, p: None)('m0_heads_ml', P)
        def headnorm_out(src, hidx, gcol, gate):
            fm_norm([(src, src.ap[:, 0:N])], N, 128.0, LN_EPS, True)
            P.dve(lambda h: h.scalar_tensor_tensor(out=mixed.ap[:, hidx, 0:N], in0=src.ap[:, 0:N], scalar=gcol,
                                                   in1=gate, op0=ALU.mult, op1=ALU.mult),
                  R=(src, vecp, sigo, silg), W=(mixed,))

        def run_interleaved(gens):
            live = list(gens)
            while live:
                nxt = []
                for g_ in live:
                    try:
                        next(g_)
                        nxt.append(g_)
                    except StopIteration:
                        pass
                live = nxt

        def mlstm_head(hh, sl, PIN, PDN):
            kp_tm, MW, PT = kp_tmL[sl], MWL[sl], PTL[sl]
            pi = nextps()
            for b in range(nb):
                bs = min(128, N - b * 128)
                P.pe(lambda h: h.transpose(out=PSB(pi)[0:bs, b * 128:(b + 1) * 128], in_=kTm.ap[:, hh, b * 128:b * 128 + bs],
                                           identity=identb.ap), R=(kTm, identb), W=(ps[pi],))
            for b in range(nb):
                bs = min(128, N - b * 128)
                P.dve(lambda h: h.tensor_scalar(out=kp_tm.ap[0:bs, b, :], in0=PSB(pi)[0:bs, b * 128:(b + 1) * 128],
                                                scalar1=wk_tm.ap[0:bs, b, hh:hh + 1], scalar2=None, op0=ALU.mult),
                      R=(ps[pi], wk_tm), W=(kp_tm,))
            P.pool(lambda h: h.tensor_tensor(out=MW.ap[:, 0:nb, :], in0=maskml.ap[:, None, :].broadcast_to([128, nb, 64]),
                                             in1=wk_tm.ap[:, 0:nb, hh:hh + 1].broadcast_to([128, nb, 64]), op=ALU.mult),
                   R=(maskml, wk_tm), W=(MW,))
            yield
            psc = nextps()
            for c in range(nch):
                b, hf = c // 2, c % 2
                P.pe(lambda h: h.matmul(PS(psc, 64, 64, b * 64, hf * 64), lhsT=kTm.ap[:, hh, c * 64:(c + 1) * 64],
                                        rhs=qTm.ap[:, hh, c * 64:(c + 1) * 64], start=True, stop=True),
                     R=(kTm, qTm), W=(ps[psc],))
            P.dve(lambda h: h.tensor_tensor(out=PT.ap[:, 0:nb, :], in0=PS(psc, 128, nb * 64).rearrange("p (b t) -> p b t", t=64),
                                            in1=MW.ap[:, 0:nb, :], op=ALU.mult), R=(ps[psc], MW), W=(PT,))
            yield
            for c in range(nch):
                b, hf = c // 2, c % 2
                r0 = hf * 64
                cs = slice(c * 64, (c + 1) * 64)
                dcol = dec_b.ap[:, hh, c:c + 1]
                Cold = CaugH[hh][ccur[hh]]
                Cnew = CaugH[hh][1 - ccur[hh]]
                ccur[hh] = 1 - ccur[hh]
                Cb = CbL[sl][c % 3]
                P.act(lambda h: h.activation(out=Cb.ap, in_=Cold.ap, func=AF.Identity, scale=dcol), R=(Cold, dec_b), W=(Cb,))
                pdc = nextps()
                P.pe(lambda h: h.matmul(PS(pdc, 128, 128), lhsT=kp_tm.ap[r0:r0 + 64, b, :],
                                        rhs=mv_tm.ap[r0:r0 + 64, b, hh * 128:(hh + 1) * 128], start=True, stop=True),
                     R=(kp_tm, mv_tm), W=(ps[pdc],))
                P.pe(lambda h: h.matmul(PS(pdc, 128, 128, 128), lhsT=kp_tm.ap[r0:r0 + 64, b, :], rhs=onesb.ap[r0:r0 + 64, :],
                                        start=True, stop=True), R=(kp_tm, onesb), W=(ps[pdc],))
                P.dve(lambda h: h.scalar_tensor_tensor(out=Cnew.ap, in0=Cold.ap, scalar=dcol, in1=PS(pdc, 128, 256),
                                                       op0=ALU.mult, op1=ALU.add), R=(Cold, dec_b, ps[pdc]), W=(Cnew,))
                P.pe(lambda h: h.matmul(PS(PIN, 128, 64, cs.start), lhsT=Cb.ap[:, 0:128], rhs=qTm.ap[:, hh, cs],
                                        start=True, stop=False), R=(Cb, qTm), W=(ps[PIN],))
                P.pe(lambda h: h.matmul(PS(PIN, 128, 64, cs.start), lhsT=mv_tm.ap[r0:r0 + 64, b, hh * 128:(hh + 1) * 128],
                                        rhs=PT.ap[r0:r0 + 64, b, :], start=False, stop=True), R=(mv_tm, PT), W=(ps[PIN],))
                P.pe(lambda h: h.matmul(PS(PDN, 128, 64, cs.start), lhsT=Cb.ap[:, 128:256], rhs=qTm.ap[:, hh, cs],
                                        start=True, stop=False), R=(Cb, qTm), W=(ps[PDN],))
                P.pe(lambda h: h.matmul(PS(PDN, 128, 64, cs.start), lhsT=onesb.ap[r0:r0 + 64, :], rhs=PT.ap[r0:r0 + 64, b, :],
                                        start=False, stop=True), R=(onesb, PT), W=(ps[PDN],))
                yield
            pe_ = nextps()
            P.pe(lambda h: h.matmul(PS(pe_, 128, N), lhsT=sel.ap[0:4, hh * 128:(hh + 1) * 128], rhs=gX[0:4, 0:N],
                                    start=True, stop=True), R=(sel, GR), W=(ps[pe_],))
            P.act(lambda h: h.activation(out=T[0].ap[:, 0:N], in_=PS(pe_, 128, N), func=AF.Exp, scale=-1.0),
                  R=(ps[pe_],), W=(T[0],))
            P.act(lambda h: h.activation(out=T[1].ap[:, 0:N], in_=PS(PDN, 128, N), func=AF.Abs), R=(ps[PDN],), W=(T[1],))
            P.dve(lambda h: h.tensor_tensor(out=T[1].ap[:, 0:N], in0=T[1].ap[:, 0:N], in1=T[0].ap[:, 0:N], op=ALU.max),
                  R=(T[1], T[0]), W=(T[1],))
            P.act(lambda h: h.activation(out=T[1].ap[:, 0:N], in_=T[1].ap[:, 0:N], func=AF.Ln), R=(T[1],), W=(T[1],))
            P.act(lambda h: h.activation(out=T[1].ap[:, 0:N], in_=T[1].ap[:, 0:N], func=AF.Exp, scale=-1.0), R=(T[1],), W=(T[1],))
            hbuf = hT[sl]
            P.dve(lambda h: h.tensor_tensor(out=hbuf.ap[:, 0:N], in0=PS(PIN, 128, N), in1=T[1].ap[:, 0:N], op=ALU.mult),
                  R=(ps[PIN], T[1]), W=(hbuf,))
            yield
            headnorm_out(hbuf, hh, vecp.ap[:, 0, hh:hh + 1], sigo.ap[:, hh, 0:N])

        for h0 in (0, 2):
            run_interleaved([mlstm_head(h0, 0, 4, 5), mlstm_head(h0 + 1, 1, 6, 7)])

        DBG.get('phase_hook', lambda n, p: None)('m0_heads_ret', P)
        def ret_head(hh, PIN):
            pr, ho = hh // 2, (hh % 2) * 64
            PT = PTL[hh]
            psc = nextps()
            for c in range(nch):
                b, hf = c // 2, c % 2
                P.pe(lambda h: h.matmul(PS(psc, 64, 64, b * 64, hf * 64), lhsT=rkT.ap[ho:ho + 64, pr, c * 64:(c + 1) * 64],
                                        rhs=rqT.ap[ho:ho + 64, pr, c * 64:(c + 1) * 64], start=True, stop=True),
                     R=(rkT, rqT), W=(ps[psc],))
            P.dve(lambda h: h.tensor_tensor(out=PT.ap[:, 0:nb, :], in0=PS(psc, 128, nb * 64).rearrange("p (b t) -> p b t", t=64),
                                            in1=mret.ap[:, hh:hh + 1, :].broadcast_to([128, nb, 64]), op=ALU.mult),
                  R=(ps[psc], mret), W=(PT,))
            yield
            for c in range(nch):
                b, hf = c // 2, c % 2
                r0 = hf * 64
                cs = slice(c * 64, (c + 1) * 64)
                Sold = SH[hh][scur[hh]]
                Snew = SH[hh][1 - scur[hh]]
                scur[hh] = 1 - scur[hh]
                Sb = SbL[hh][c % 3]
                P.act(lambda h: h.activation(out=Sb.ap[ho:ho + 64, :], in_=Sold.ap[ho:ho + 64, :], func=AF.Copy),
                      R=(Sold,), W=(Sb,))
                pdc = nextps()
                P.pe(lambda h: h.matmul(PS(pdc, 64, 128, 0, ho), lhsT=rk_tm.ap[r0:r0 + 64, b, pr * 128 + ho:pr * 128 + ho + 64],
                                        rhs=rv_tm.ap[r0:r0 + 64, b, hh * 128:(hh + 1) * 128], start=True, stop=True),
                     R=(rk_tm, rv_tm), W=(ps[pdc],))
                P.dve(lambda h: h.scalar_tensor_tensor(out=Snew.ap[ho:ho + 64, :], in0=Sold.ap[ho:ho + 64, :],
                                                       scalar=float(GAM[hh] ** 64), in1=PS(pdc, 64, 128, 0, ho),
                                                       op0=ALU.mult, op1=ALU.add), R=(Sold, ps[pdc]), W=(Snew,))
                P.pe(lambda h: h.matmul(PS(PIN, 128, 64, cs.start), lhsT=Sb.ap[:, :], rhs=rqd.ap[:, pr, cs],
                                        start=True, stop=False), R=(Sb, rqd), W=(ps[PIN],))
                P.pe(lambda h: h.matmul(PS(PIN, 128, 64, cs.start), lhsT=rv_tm.ap[r0:r0 + 64, b, hh * 128:(hh + 1) * 128],
                                        rhs=PT.ap[r0:r0 + 64, b, :], start=False, stop=True), R=(rv_tm, PT), W=(ps[PIN],))
                yield
            hbuf = hT[hh % 2]
            P.act(lambda h: h.activation(out=hbuf.ap[:, 0:N], in_=PS(PIN, 128, N), func=AF.Copy), R=(ps[PIN],), W=(hbuf,))
            headnorm_out(hbuf, 4 + hh, vecp.ap[:, 0, 4 + hh:5 + hh], silg.ap[:, hh, 0:N])

        run_interleaved([ret_head(hh, 4 + hh) for hh in range(4)])

        DBG.get('phase_hook', lambda n, p: None)('m0_oproj', P)
        for g in range(2):
            slot, w = wload("OA", g)
            for j in range(4):
                m = g * 4 + j
                pi = nextps()
                for kc in range(8):
                    P.pe(lambda h, kc=kc, j=j, pi=pi, w=w: h.matmul(
                        PS(pi, 128, N), lhsT=w[:, kc, j * 128:(j + 1) * 128], rhs=mixed.ap[:, kc, 0:N],
                        start=(kc == 0), stop=(kc == 7)), R=(slot, mixed), W=(ps[pi],))
                resid_chunk(m, pi, N)
        residual_ln(N, 0, 0)
        P.release(m0)

    krT2 = P.alloc([128, 4096], BF16)
    KT_d = dscr("kt_scr", [8, 128, 4096])
    V_d = dscr("v_scr", [4096 + 128, 1024])
    kvo = [Obj() for _ in range(9)]
    kvsem = [P.new_dma_sem() for _ in range(2)]
    tsem2 = P.new_dma_sem()

    def kv_expand(ckvT, N, kpos0):
        nb = (N + 127) // 128
        reg = kvo[kpos0 // 512]
        m0 = P.mark()
        Kst = P.alloc([128, 8, 512], BF16)
        Vst = P.alloc([128, 4, 1024], BF16)
        slot, w = wload("UKV", 0)
        for hh in range(8):
            pi = nextps()
            for kc in range(2):
                P.pe(lambda h, kc=kc: h.matmul(PS(pi, 128, N), lhsT=w[:, kc, hh * 128:(hh + 1) * 128], rhs=ckvT.ap[:, kc, 0:N],
                                               start=(kc == 0), stop=(kc == 1)), R=(slot, ckvT), W=(ps[pi],))
            if hh % 2 == 0:
                P.act(lambda h: h.activation(out=Kst.ap[:, hh, 0:N], in_=PS(pi, 128, N), func=AF.Copy), R=(ps[pi],), W=(Kst,))
            else:
                P.dve(lambda h: h.tensor_copy(out=Kst.ap[:, hh, 0:N], in_=PS(pi, 128, N)), R=(ps[pi],), W=(Kst,))
        P.store(KT_d[:, :, kpos0:kpos0 + N].rearrange("h d n -> d h n"), Kst.ap[:, :, 0:N], R=(Kst,), W=(reg,))
        slot, w = wload("UKV", 1)
        for b in range(nb):
            bs = min(128, N - b * 128)
            for half in range(2):
                pi = nextps()
                for kc in range(2):
                    P.pe(lambda h, kc=kc: h.matmul(PS(pi, bs, 512), lhsT=ckvT.ap[:, kc, b * 128:b * 128 + bs],
                                                   rhs=w[:, kc, half * 512:(half + 1) * 512],
                                                   start=(kc == 0), stop=(kc == 1)), R=(slot, ckvT), W=(ps[pi],))
                if half == 0:
                    P.act(lambda h: h.activation(out=Vst.ap[0:bs, b, 0:512], in_=PS(pi, bs, 512), func=AF.Copy),
                          R=(ps[pi],), W=(Vst,))
                else:
                    P.dve(lambda h: h.tensor_copy(out=Vst.ap[0:bs, b, 512:1024], in_=PS(pi, bs, 512)), R=(ps[pi],), W=(Vst,))
        if N % 128 == 0:
            P.store(V_d[kpos0:kpos0 + N, :].rearrange("(b p) c -> p b c", p=128), Vst.ap[:, 0:nb, :], R=(Vst,), W=(reg,))
        else:
            P.store(V_d[kpos0:kpos0 + N, :], Vst.ap[0:N, 0, :], R=(Vst,), W=(reg,))
        P.release(m0)

    def tm_to_T(ckv_b, kr_b, ckvT, b, bs, kpos):
        pi = nextps()
        for kc in range(2):
            P.pe(lambda h, kc=kc: h.transpose(out=PSB(pi)[:, kc * 128:kc * 128 + bs], in_=ckv_b.ap[0:bs, kc * 128:(kc + 1) * 128],
                                              identity=identb.ap[0:bs, 0:bs]), R=(ckv_b, identb), W=(ps[pi],))
        P.pe(lambda h: h.transpose(out=PSB(pi)[:, 256:256 + bs], in_=kr_b.ap[0:bs, :], identity=identb.ap[0:bs, 0:bs]),
             R=(kr_b, identb), W=(ps[pi],))
        P.act(lambda h: h.activation(out=ckvT.ap[:, :, b * 128:b * 128 + bs],
                                     in_=PSB(pi)[:, 0:256].rearrange("p (k c) -> p k c", k=2)[:, :, 0:bs], func=AF.Copy),
              R=(ps[pi],), W=(ckvT,))
        P.dve(lambda h: h.tensor_copy(out=krT2.ap[:, kpos:kpos + bs], in_=PSB(pi)[:, 256:256 + bs]), R=(ps[pi],), W=(krT2,))

    def kv_prefill(s):
        for c in range(PAST // 512):
            m0 = P.mark()
            cin = P.alloc([128, 4, 256], F32)
            kin = P.alloc([128, 4, 64], F32)
            P.load(cin.ap, I["cckv"][s, c * 512:(c + 1) * 512, :].rearrange("(b p) f -> p b f", p=128), tsem2, W=(cin,))
            P.load(kin.ap, I["ckr"][s, c * 512:(c + 1) * 512, :].rearrange("(b p) f -> p b f", p=128), tsem2, W=(kin,))
            ckvT = P.alloc([128, 2, 512], BF16)
            for b in range(4):
                ckv_b = P.alloc([128, 256], BF16)
                kr_b = P.alloc([128, 128], BF16)
                P.act(lambda h: h.activation(out=ckv_b.ap, in_=cin.ap[:, b, :], func=AF.Copy), R=(cin,), W=(ckv_b,))
                P.dve(lambda h: h.tensor_copy(out=kr_b.ap.rearrange("p (a c) -> p a c", a=2),
                                              in_=kin.ap[:, b:b + 1, :].broadcast_to([128, 2, 64])), R=(kin,), W=(kr_b,))
                tm_to_T(ckv_b, kr_b, ckvT, b, 128, c * 512 + b * 128)
            kv_expand(ckvT, 512, c * 512)
            P.release(m0)

    def mix1(N, pos0, kpos0, is_prompt, okv, okr, s, orow0):
        nb = (N + 127) // 128
        m0 = P.mark()
        qnT = P.alloc([128, 8, 512], BF16)
        qrT = P.alloc([128, 8, 512], BF16)
        P.pool(lambda h: h.memset(qrT.ap, 0.0), W=(qrT,))
        oT = P.alloc([128, 8, 512], BF16)
        m1 = P.mark()
        cqT = P.alloc([128, 4, 512], F32)
        cqTb = P.alloc([128, 4, 512], BF16)
        ckvT = P.alloc([128, 2, 512], BF16)
        ctm = P.alloc([128, 4, 32], F32)
        stm_ = P.alloc([128, 4, 32], F32)
        P.load(cosb.ap[:, 0:N], I["cosF"][:, pos0:pos0 + N], tsem, W=(cosb,))
        P.load(sinb.ap[:, 0:N], I["sinF"][:, pos0:pos0 + N], tsem, W=(sinb,))
        if N % 128 == 0:
            P.load(ctm.ap[:, 0:nb, :], I["cosT"][pos0:pos0 + N, :].rearrange("(b p) j -> p b j", p=128), tsem2, W=(ctm,))
            P.load(stm_.ap[:, 0:nb, :], I["sinT"][pos0:pos0 + N, :].rearrange("(b p) j -> p b j", p=128), tsem2, W=(stm_,))
        else:
            P.load(ctm.ap[0:N, 0, :], I["cosT"][pos0:pos0 + N, :], tsem2, W=(ctm,))
            P.load(stm_.ap[0:N, 0, :], I["sinT"][pos0:pos0 + N, :], tsem2, W=(stm_,))
        slot, w = wload("C", 0)
        for m in range(4):
            pi = nextps()
            for kc in range(8):
                P.pe(lambda h, kc=kc: h.matmul(PS(pi, 128, N), lhsT=w[:, kc, m * 128:(m + 1) * 128], rhs=xTb.ap[:, kc, 0:N],
                                               start=(kc == 0), stop=(kc == 7)), R=(slot, xTb), W=(ps[pi],))
            P.act(lambda h: h.activation(out=cqT.ap[:, m, 0:N], in_=PS(pi, 128, N), func=AF.Copy), R=(ps[pi],), W=(cqT,))
        fm_norm([(cqT, cqT.ap[:, m, 0:N]) for m in range(4)], N, 512.0, RMS_EPS, False)
        for m in range(4):
            P.act(lambda h: h.activation(out=cqTb.ap[:, m, 0:N], in_=cqT.ap[:, m, 0:N], func=AF.Identity,
                                         scale=vecp.ap[:, 1, m:m + 1]), R=(cqT, vecp), W=(cqTb,))
        DBG.get('phase_hook', lambda n, p: None)('m1_ckv', P)
        slot, w = wload("C", 1)
        for b in range(nb):
            bs = min(128, N - b * 128)
            mb = P.mark()
            ckvo = P.alloc([128, 256], F32)
            kro = P.alloc([128, 64], F32)
            ckv_b = P.alloc([128, 256], BF16)
            kr_b = P.alloc([128, 128], BF16)
            sm = P.alloc([128, 8], F32)
            rt = P.alloc([128, 4, 32], F32)
            pi = nextps()
            for kc in range(8):
                P.pe(lambda h, kc=kc: h.matmul(PS(pi, bs, 320), lhsT=xTb.ap[:, kc, b * 128:b * 128 + bs], rhs=w[:, kc, 0:320],
                                               start=(kc == 0), stop=(kc == 7)), R=(slot, xTb), W=(ps[pi],))
            P.act(lambda h: h.activation(out=ckvo.ap[0:bs, :], in_=PS(pi, bs, 256), func=AF.Square, accum_out=sm.ap[0:bs, 0:1]),
                  R=(ps[pi],), W=(ckvo, sm))
            P.act(lambda h: h.activation(out=sm.ap[0:bs, 1:2], in_=sm.ap[0:bs, 0:1], func=AF.Ln, scale=1.0 / 256,
                                         bias=epsb[RMS_EPS].ap[0:bs, 0:1]), R=(sm, epsb[RMS_EPS]), W=(sm,))
            P.act(lambda h: h.activation(out=sm.ap[0:bs, 2:3], in_=sm.ap[0:bs, 1:2], func=AF.Exp, scale=-0.5), R=(sm,), W=(sm,))
            P.dve(lambda h: h.scalar_tensor_tensor(out=ckvo.ap[0:bs, :], in0=PS(pi, bs, 256), scalar=sm.ap[0:bs, 2:3],
                                                   in1=gkvb.ap[0:bs, :], op0=ALU.mult, op1=ALU.mult),
                  R=(ps[pi], sm, gkvb), W=(ckvo,))
            P.store(O[okv][s, orow0 + b * 128:orow0 + b * 128 + bs, :], ckvo.ap[0:bs, :], R=(ckvo,))
            P.act(lambda h: h.activation(out=ckv_b.ap[0:bs, :], in_=ckvo.ap[0:bs, :], func=AF.Copy), R=(ckvo,), W=(ckv_b,))
            x1 = PS(pi, bs, 32, 256)
            x2 = PS(pi, bs, 32, 288)
            cs_, sn_ = ctm.ap[0:bs, b, :], stm_.ap[0:bs, b, :]
            for j, (xa, tb_) in enumerate(((x1, cs_), (x2, sn_), (x2, cs_), (x1, sn_))):
                P.dve(lambda h, j=j, xa=xa, tb_=tb_: h.tensor_tensor(out=rt.ap[0:bs, j, :], in0=xa, in1=tb_, op=ALU.mult),
                      R=(ps[pi], ctm, stm_), W=(rt,))
            P.dve(lambda h: h.tensor_tensor(out=kro.ap[0:bs, 0:32], in0=rt.ap[0:bs, 0, :], in1=rt.ap[0:bs, 1, :], op=ALU.subtract),
                  R=(rt,), W=(kro,))
            P.dve(lambda h: h.tensor_tensor(out=kro.ap[0:bs, 32:64], in0=rt.ap[0:bs, 2, :], in1=rt.ap[0:bs, 3, :], op=ALU.add),
                  R=(rt,), W=(kro,))
            P.store(O[okr][s, orow0 + b * 128:orow0 + b * 128 + bs, :], kro.ap[0:bs, :], R=(kro,))
            P.act(lambda h: h.activation(out=kr_b.ap[0:bs, :].rearrange("p (a c) -> p a c", a=2),
                                         in_=kro.ap[0:bs, None, :].broadcast_to([bs, 2, 64]), func=AF.Copy), R=(kro,), W=(kr_b,))
            tm_to_T(ckv_b, kr_b, ckvT, b, bs, kpos0 + b * 128)
            P.release(mb)
        DBG.get('phase_hook', lambda n, p: None)('m1_kvexp', P)
        kv_expand(ckvT, N, kpos0)
        slot, w = wload("UQ", 0)
        for hh in range(8):
            pi = nextps()
            for kc in range(4):
                P.pe(lambda h, kc=kc: h.matmul(PS(pi, 128, N), lhsT=w[:, kc, hh * 128:(hh + 1) * 128], rhs=cqTb.ap[:, kc, 0:N],
                                               start=(kc == 0), stop=(kc == 3)), R=(slot, cqTb), W=(ps[pi],))
            if hh % 2 == 0:
                P.act(lambda h: h.activation(out=qnT.ap[:, hh, 0:N], in_=PS(pi, 128, N), func=AF.Copy), R=(ps[pi],), W=(qnT,))
            else:
                P.dve(lambda h: h.tensor_copy(out=qnT.ap[:, hh, 0:N], in_=PS(pi, 128, N)), R=(ps[pi],), W=(qnT,))
        slot, w = wload("UQ", 1)
        wsw = wswap(slot, w, 4, 512)
        for pr in range(4):
            pa = nextps()
            for kc in range(4):
                P.pe(lambda h, kc=kc: h.matmul(PS(pa, 128, N), lhsT=w[:, kc, pr * 128:(pr + 1) * 128], rhs=cqTb.ap[:, kc, 0:N],
                                               start=(kc == 0), stop=(kc == 3)), R=(slot, cqTb), W=(ps[pa],))
            pb = nextps()
            for kc in range(4):
                P.pe(lambda h, kc=kc: h.matmul(PS(pb, 128, N), lhsT=wsw[:, kc, pr * 128:(pr + 1) * 128],
                                               rhs=cqTb.ap[:, kc, 0:N], start=(kc == 0), stop=(kc == 3)),
                     R=(slot, cqTb), W=(ps[pb],))
            P.dve(lambda h: h.tensor_tensor(out=T[0].ap[:, 0:N], in0=PS(pa, 128, N), in1=cosb.ap[:, 0:N], op=ALU.mult),
                  R=(ps[pa], cosb), W=(T[0],))
            P.dve(lambda h: h.tensor_tensor(out=T[1].ap[:, 0:N], in0=PS(pb, 128, N), in1=sinb.ap[:, 0:N], op=ALU.mult),
                  R=(ps[pb], sinb), W=(T[1],))
            P.pool(lambda h: h.tensor_tensor(out=qrT.ap[0:64, 2 * pr, 0:N], in0=T[0].ap[0:64, 0:N], in1=T[1].ap[0:64, 0:N],
                                             op=ALU.add), R=(T[0], T[1]), W=(qrT,))
            P.pool(lambda h: h.tensor_tensor(out=qrT.ap[64:128, 2 * pr + 1, 0:N], in0=T[0].ap[64:128, 0:N],
                                             in1=T[1].ap[64:128, 0:N], op=ALU.add), R=(T[0], T[1]), W=(qrT,))
        P.release(m1)
        DBG.get('phase_hook', lambda n, p: None)('m1_attn', P)
        stageL = [P.alloc([128, 4096], F32) for _ in range(2)]
        PbL = [P.alloc([128, 4096], BF16) for _ in range(2)]
        KTh = P.alloc([128, 4096], BF16)
        VhL = [P.alloc([128, 32, 128], BF16) for _ in range(2)]
        PTs = [P.alloc([128, 4, 128], BF16) for _ in range(4)]
        o_tm = P.alloc([128, 4, 1024], BF16)
        sm2L = [P.alloc([128, 16], F32) for _ in range(2)]
        qrr = [0]
        nk = kpos0 + N
        nreg = (nk + 511) // 512
        kregs = tuple(kvo[i] for i in range(nreg))
        ptrr = [0]
        porr = [0]
        iters = [(hh, qb) for hh in range(8) for qb in range(nb)]

        def stage_a(it):
            hh, qb = iters[it]
            pr, ho = hh // 2, (hh % 2) * 64
            Vh = VhL[hh % 2]
            if qb == 0:
                P.load(KTh.ap[:, 0:nk], KT_d[hh, :, 0:nk], kvsem[0], R=kregs, W=(KTh,))
                nfull = nk // 128
                P.load(Vh.ap[:, 0:nfull, :], V_d[0:nfull * 128, hh * 128:(hh + 1) * 128].rearrange("(j p) c -> p j c", p=128),
                       kvsem[1], R=kregs, W=(Vh,))
                if nk % 128:
                    rem = nk % 128
                    P.load(Vh.ap[0:rem, nfull, :], V_d[nfull * 128:nk, hh * 128:(hh + 1) * 128], kvsem[1], R=kregs, W=(Vh,))
            bq = min(128, N - qb * 128)
            qs = slice(qb * 128, qb * 128 + bq)
            nvis = kpos0 + (qb + 1) * 128 if is_prompt else nk
            stage, Pb, sm2 = stageL[it % 2], PbL[it % 2], sm2L[it % 2]
            ng = (nvis + 511) // 512
            for g in range(ng):
                gs = min(512, nvis - g * 512)
                pi = nextps()
                P.pe(lambda h: h.matmul(PS(pi, bq, gs), lhsT=qnT.ap[:, hh, qs], rhs=KTh.ap[:, g * 512:g * 512 + gs],
                                        start=True, stop=False), R=(qnT, KTh), W=(ps[pi],))
                P.pe(lambda h: h.matmul(PS(pi, bq, gs), lhsT=qrT.ap[:, hh, qs],
                                        rhs=krT2.ap[:, g * 512:g * 512 + gs], start=False, stop=True),
                     R=(qrT, krT2), W=(ps[pi],))
                if g == ng - 1:
                    P.act(lambda h: h.activation(out=stage.ap[0:bq, g * 512:g * 512 + gs], in_=PS(pi, bq, gs), func=AF.Copy),
                          R=(ps[pi],), W=(stage,))
                    if is_prompt:
                        P.pool(lambda h: h.memset(stage.ap[0:64, nvis - 64:nvis], NEG), W=(stage,))
                else:
                    P.dve(lambda h: h.tensor_scalar(out=stage.ap[0:bq, g * 512:g * 512 + gs], in0=PS(pi, bq, gs),
                                                    scalar1=1.0, scalar2=None, op0=ALU.mult, op1=ALU.max,
                                                    accum_out=sm2.ap[0:bq, 4 + g:5 + g]), R=(ps[pi],), W=(stage, sm2))

        def stage_a2(it):
            hh, qb = iters[it]
            bq = min(128, N - qb * 128)
            nvis = kpos0 + (qb + 1) * 128 if is_prompt else nk
            stage, Pb, sm2 = stageL[it % 2], PbL[it % 2], sm2L[it % 2]
            ng = (nvis + 511) // 512
            g = ng - 1
            gs = min(512, nvis - g * 512)
            P.dve(lambda h: h.reduce_max(out=sm2.ap[0:bq, 4 + g:5 + g], in_=stage.ap[0:bq, g * 512:g * 512 + gs],
                                         axis=AX.X), R=(stage,), W=(sm2,))
            P.dve(lambda h: h.reduce_max(out=sm2.ap[0:bq, 0:1], in_=sm2.ap[0:bq, 4:4 + ng], axis=AX.X), R=(sm2,), W=(sm2,))
            P.dve(lambda h: h.tensor_scalar(out=sm2.ap[0:bq, 1:2], in0=sm2.ap[0:bq, 0:1], scalar1=-MLA_SCALE, scalar2=None,
                                            op0=ALU.mult), R=(sm2,), W=(sm2,))
            P.act(lambda h: h.activation(out=Pb.ap[0:bq, 0:nvis], in_=stage.ap[0:bq, 0:nvis], func=AF.Exp, scale=MLA_SCALE,
                                         bias=sm2.ap[0:bq, 1:2], accum_out=sm2.ap[0:bq, 2:3]), R=(stage, sm2), W=(Pb, sm2))
            P.dve(lambda h: h.reciprocal(out=sm2.ap[0:bq, 3:4], in_=sm2.ap[0:bq, 2:3]), R=(sm2,), W=(sm2,))

        def stage_b(it):
            hh, qb = iters[it]
            Vh = VhL[hh % 2]
            bq = min(128, N - qb * 128)
            nvis = kpos0 + (qb + 1) * 128 if is_prompt else nk
            Pb, sm2 = PbL[it % 2], sm2L[it % 2]
            nkb = (nvis + 127) // 128
            po = 4
            groups = [(k0, min(nkb, k0 + 4)) for k0 in range(0, nkb, 4)]
            slots = []

            def emit_T(gi):
                k0, k1 = groups[gi]
                pt = 5 + ptrr[0] % 3
                pts = PTs[ptrr[0] % len(PTs)]
                ptrr[0] += 1
                for kb in range(k0, k1):
                    kbs = min(128, nvis - kb * 128)
                    j = kb - k0
                    P.pe(lambda h: h.transpose(out=PSB(pt)[0:kbs, j * 128:j * 128 + bq], in_=Pb.ap[0:bq, kb * 128:kb * 128 + kbs],
                                               identity=identb.ap[0:bq, 0:bq]), R=(Pb, identb), W=(ps[pt],))
                nj = k1 - k0
                src = PSB(pt)[:, 0:nj * 128].rearrange("p (j c) -> p j c", j=nj)[:, :, 0:bq]
                if gi % 2 == 0:
                    P.dve(lambda h: h.tensor_copy(out=pts.ap[:, 0:nj, 0:bq], in_=src), R=(ps[pt],), W=(pts,))
                else:
                    P.act(lambda h: h.activation(out=pts.ap[:, 0:nj, 0:bq], in_=src, func=AF.Copy), R=(ps[pt],), W=(pts,))
                slots.append(pts)

            def emit_PV(gi):
                k0, k1 = groups[gi]
                pts = slots[gi]
                for kb in range(k0, k1):
                    kbs = min(128, nvis - kb * 128)
                    j = kb - k0
                    P.pe(lambda h: h.matmul(PS(po, bq, 128), lhsT=pts.ap[0:kbs, j, 0:bq], rhs=Vh.ap[0:kbs, kb, :],
                                            start=(kb == 0), stop=(kb == nkb - 1)), R=(pts, Vh), W=(ps[po],))

            LA = 2
            for gi in range(min(LA, len(groups))):
                emit_T(gi)
            for gi in range(len(groups)):
                if gi + LA < len(groups):
                    emit_T(gi + LA)
                emit_PV(gi)
            P.act(lambda h: h.activation(out=o_tm.ap[0:bq, qb, hh * 128:(hh + 1) * 128], in_=PS(po, bq, 128), func=AF.Identity,
                                         scale=sm2.ap[0:bq, 3:4]), R=(ps[po], sm2), W=(o_tm,))

        stage_a(0)
        stage_a2(0)
        for it in range(len(iters)):
            if it + 1 < len(iters):
                stage_a(it + 1)
            stage_b(it)
            if it + 1 < len(iters):
                stage_a2(it + 1)
        DBG.get('phase_hook', lambda n, p: None)('m1_oT', P)
        for qb in range(nb):
            bq = min(128, N - qb * 128)
            for half in range(2):
                pt = 5 + ptrr[0] % 3
                ptrr[0] += 1
                for j in range(4):
                    m = half * 4 + j
                    P.pe(lambda h, j=j, m=m: h.transpose(out=PSB(pt)[:, j * 128:j * 128 + bq],
                                                         in_=o_tm.ap[0:bq, qb, m * 128:(m + 1) * 128],
                                                         identity=identb.ap[0:bq, 0:bq]), R=(o_tm, identb), W=(ps[pt],))
                src = PSB(pt)[:, 0:512].rearrange("p (j c) -> p j c", j=4)[:, :, 0:bq]
                if half == 0:
                    P.act(lambda h, src=src: h.activation(out=oT.ap[:, 0:4, qb * 128:qb * 128 + bq], in_=src, func=AF.Copy),
                          R=(ps[pt],), W=(oT,))
                else:
                    P.dve(lambda h, src=src: h.tensor_copy(out=oT.ap[:, 4:8, qb * 128:qb * 128 + bq], in_=src),
                          R=(ps[pt],), W=(oT,))
        for g in range(2):
            slot, w = wload("OC", g)
            for j in range(4):
                m = g * 4 + j
                pi = nextps()
                for kc in range(8):
                    P.pe(lambda h, kc=kc: h.matmul(PS(pi, 128, N), lhsT=w[:, kc, j * 128:(j + 1) * 128], rhs=oT.ap[:, kc, 0:N],
                                                   start=(kc == 0), stop=(kc == 7)), R=(slot, oT), W=(ps[pi],))
                resid_chunk(m, pi, N)
        residual_ln(N, 1, 0)
        P.release(m0)

    for s in range(n_pseq):
        if "mix0" in stages:
            state_zero()
        for t in range(n_ptiles):
            N = 512
            _ph = DBG.get("phase_hook", lambda n, p: None)
            _ph("load", P)
            if (t, "ld") not in DBG.get("skip", ()):
                load_x(I["xp"][s, t * N:(t + 1) * N, :], N)
            _ph("mix0", P)
            if "mix0" in stages:
                mix0(N, t * N)
                if t == n_ptiles - 1:
                    state_store("pC", "pn", "pm", "pS", s)
            _ph("ffn0", P)
            if "ffn0" in stages:
                ffn(N, 0)
            _ph("mix1", P)
            if "mix1" in stages:
                mix1(N, t * N, t * N, True, "pckv", "pkr", s, t * N)
            _ph("ffn1", P)
            if "ffn1" in stages:
                ffn(N, 1)
            _ph("store", P)
            if (t, "st") not in DBG.get("skip", ()):
                store_y(O["yp"][s, t * N:(t + 1) * N, :], N)
    DBG.get("phase_hook", lambda n, p: None)("sample", P)
    for s in range(n_sseq):
        N = DEC_SEQ
        if DBG.get("s_load", True):
            load_x(I["xs"][s, :, :], N)
        if "mix0" in stages:
            state_load(s)
            mix0(N, SEQ)
            state_store("sC", "sn", "sm", "sS", s)
        if "ffn0" in stages:
            ffn(N, 0)
        if "mix1" in stages:
            kv_prefill(s)
            mix1(N, SEQ, PAST, False, "sckv", "skr", s, 0)
        if "ffn1" in stages:
            ffn(N, 1)
        if DBG.get("s_store", True):
            store_y(O["ys"][s, :, :], N)

    P.finish(stack)
    stack.close()
    return nc, P


def _consts():
    c = {}
    c["ident"] = np.eye(128, dtype=np.float32)
    half = 32
    inv = (10000.0 ** (-np.arange(half, dtype=np.float32) / half)).astype(np.float32)
    pos = np.concatenate([np.arange(SEQ), PAST + np.arange(DEC_SEQ)]).astype(np.float32)
    ang = (pos[:, None] * inv[None, :]).astype(np.float32)
    cos = np.cos(ang).astype(np.float32)
    sin = np.sin(ang).astype(np.float32)
    c["cosT"] = cos
    c["sinT"] = sin
    p = np.arange(128)
    j = (p % 64) % 32
    sign = np.where((p % 64) < 32, -1.0, 1.0).astype(np.float32)
    c["cosF"] = np.ascontiguousarray(cos[:, j].T)
    c["sinF"] = np.ascontiguousarray((sin[:, j] * sign[None, :]).T)
    s_idx = (p % 64)[:, None]
    t_idx = np.arange(64)[None, :]
    c["mask_ml"] = (s_idx <= t_idx).astype(np.float32)
    gam = (1.0 - 2.0 ** (-5.0 - np.arange(4, dtype=np.float64)))
    mret = np.zeros((128, 4, 64), np.float32)
    for h in range(4):
        rel = t_idx - s_idx
        mret[:, h, :] = np.where(rel >= 0, gam[h] ** np.maximum(rel, 0), 0.0) * (64.0 ** -0.5)
    c["mret"] = mret
    kdec = np.zeros((128, 2, 128), np.float32)
    for pr in range(2):
        for jj in range(128):
            h = 2 * pr + jj // 64
            kdec[:, pr, jj] = gam[h] ** (63 - (p % 64)) * (64.0 ** -0.5)
    c["kdec"] = kdec
    qdec = np.zeros((128, 2, 64), np.float32)
    for pr in range(2):
        for pp in range(128):
            h = 2 * pr + pp // 64
            qdec[pp, pr, :] = gam[h] ** (np.arange(64) + 1.0)
    c["qdec"] = qdec
    sel = np.zeros((4, 4, 128), np.float32)
    for h in range(4):
        sel[h, h, :] = 1.0
    c["sel"] = sel
    dm = np.zeros((128, 128), np.float32)
    dm[0:64, 64:128] = NEG
    c["dmask"] = dm
    return c


_CACHE = {}


def kernel(**inp):
    f = lambda a: np.ascontiguousarray(np.asarray(a, dtype=np.float32))
    key = "full"
    if key not in _CACHE:
        _CACHE[key] = build()
    nc, _ = _CACHE[key]
    cst = _consts()
    ln = np.concatenate([f(inp["ln_mix_g"]), f(inp["ln_mix_b"]), f(inp["ln_ffn_g"]), f(inp["ln_ffn_b"])], axis=0)
    shared = {
        "w_in_a": f(inp["w_in_a"]), "b_if": f(inp["b_if_a"]).reshape(8), "g_ml": f(inp["g_ml"]).reshape(512),
        "g_ret": f(inp["g_ret"]).reshape(512), "w_out_a": f(inp["w_out_a"]), "w_in_c": f(inp["w_in_c"]),
        "g_q": f(inp["g_q"]).reshape(512), "g_kv": f(inp["g_kv"]).reshape(256), "w_uq": f(inp["w_uq"]),
        "w_ukv": f(inp["w_ukv"]), "w_out_c": f(inp["w_out_c"]), "ln": ln,
        "w_gu": f(inp["w_gu"]), "w_down": f(inp["w_down"]),
    }
    shared.update(cst)
    in_maps = []
    for c in range(NCORES):
        sl = slice(2 * c, 2 * c + 2)
        m = dict(shared)
        m["xp"] = f(inp["x_prompt"][sl])
        m["xs"] = f(inp["x_sample"][sl])
        m["stC"] = f(inp["state_mlstm_C"][0, sl])
        m["stn"] = f(inp["state_mlstm_n"][0, sl])
        m["stm"] = f(inp["state_mlstm_m"][0, sl])
        m["stS"] = f(inp["state_ret_S"][0, sl])
        m["cckv"] = f(inp["cache_ckv"][0, sl])
        m["ckr"] = f(inp["cache_krope"][0, sl])
        in_maps.append(m)
    res = run_bass_kernel_spmd(nc, in_maps, core_ids=list(range(NCORES)))
    R = res.results
    cat = lambda k: np.concatenate([np.asarray(r[k], dtype=np.float32) for r in R], axis=0)
    outs = (cat("yp"), cat("ys"),
            cat("pC")[None], cat("pn")[None], cat("pm")[None], cat("pS")[None], cat("pckv")[None], cat("pkr")[None],
            cat("sC")[None], cat("sn")[None], cat("sm")[None], cat("sS")[None], cat("sckv")[None], cat("skr")[None])
    return outs
```

```python
import math
import numpy as np
import concourse.bass as bass
import concourse.mybir as mybir
from concourse.bass_utils import run_bass_kernel_spmd

F32 = mybir.dt.float32
BF16 = mybir.dt.bfloat16
AF = mybir.ActivationFunctionType
ALU = mybir.AluOpType
AX = mybir.AxisListType

NCORES = 8
D = 1024
SEQ = 4096
DEC_SEQ = 64
PAST = 2048
DFF = 2816
PROJ_A = 3592
ALPHA = 4.0 ** 0.25
LN_EPS = 1e-5
RMS_EPS = 1e-6
MLA_SCALE = 192.0 ** -0.5
NPOS = SEQ + DEC_SEQ
NEG = -1.0e30
DBG = {}


def _freeze(fn):
    if getattr(fn, "__closure__", None) is None:
        return fn
    import types
    cells = []
    for c in fn.__closure__:
        try:
            cells.append(types.CellType(c.cell_contents))
        except ValueError:
            cells.append(c)
    return types.FunctionType(fn.__code__, fn.__globals__, fn.__name__, fn.__defaults__, tuple(cells))


class Res:
    __slots__ = ("w", "r")

    def __init__(self):
        self.w = None
        self.r = {}


class Obj:
    def __init__(self, n=1):
        self.res = [Res() for _ in range(n)]


class Buf:
    def __init__(self, ap, res):
        self.ap = ap
        self.res = res

    def __getitem__(self, idx):
        return self.ap[idx]


class Eng:
    def __init__(self, name, key):
        self.name = name
        self.key = key
        self.cnt = 0
        self.known = {}
        self.ops = []


class Prog:
    BLK = 256

    def __init__(self, nc, arena_bytes):
        self.nc = nc
        self.engs = {n: Eng(n, i) for i, n in enumerate(["pe", "act", "dve", "pool", "sp"])}
        self.nsem = 5
        self.dma_cnt = {}
        self.dma_last = {}
        self.arena_bytes = arena_bytes
        self.arena = nc.alloc_sbuf_tensor("arena", [128, arena_bytes // 2], BF16)
        self.ares = [Res() for _ in range(arena_bytes // self.BLK)]
        self.top = 0
        self.psum = []
        for i in range(8):
            t = nc.alloc_psum_tensor("ps%d" % i, [128, 512], F32)
            o = Obj()
            o.t = t
            o.excl = True
            self.psum.append(o)
        self.store_sems = []
        self.store_rr = 0
        self.store_eng = "pool"

    def new_dma_sem(self):
        k = self.nsem
        self.nsem += 1
        self.dma_cnt[k] = 0
        return k

    def alloc(self, shape, dtype, at=None):
        esz = 4 if dtype == F32 else 2
        free = 1
        for s in shape[1:]:
            free *= s
        nbytes = free * esz
        nbytes_r = (nbytes + self.BLK - 1) // self.BLK * self.BLK
        if at is not None:
            off = at
            assert off % self.BLK == 0 and off + nbytes_r <= self.arena_bytes
        else:
            off = self.top
            self.top += nbytes_r
            self.peak = max(getattr(self, "peak", 0), self.top)
            assert self.top <= self.arena_bytes, "arena overflow %d" % self.top
        ap = self.arena[0:shape[0], off // 2:(off + nbytes) // 2]
        if dtype == F32:
            ap = ap.bitcast(F32)
        if len(shape) == 3:
            ap = ap.rearrange("p (a b) -> p a b", a=shape[1])
        elif len(shape) == 4:
            ap = ap.rearrange("p (a b c) -> p a b c", a=shape[1], b=shape[2])
        res = self.ares[off // self.BLK:(off + nbytes_r) // self.BLK]
        return Buf(ap, res)

    def mark(self):
        return self.top

    def release(self, m):
        self.top = m

    def emit(self, ename, fn, R=(), W=(), dma_sem=None, chain=False):
        eng = self.engs[ename]
        need = {}
        if any(getattr(b, "excl", False) for b in R):
            W = tuple(W) + tuple(b for b in R if getattr(b, "excl", False))
            R = tuple(b for b in R if not getattr(b, "excl", False))

        def req(s, v):
            if ename == "pe" and s == eng.key:
                return
            if eng.known.get(s, 0) >= v:
                return
            if need.get(s, 0) < v:
                need[s] = v

        for b in R:
            for r in b.res:
                if r.w is not None:
                    req(*r.w)
        for b in W:
            for r in b.res:
                if r.w is not None:
                    req(*r.w)
                for s, v in r.r.items():
                    req(s, v)
        if dma_sem is not None:
            lt = self.dma_last.get(dma_sem)
            if lt is not None and not chain:
                req(*lt)
            self.dma_cnt[dma_sem] += 16
            tok = (dma_sem, self.dma_cnt[dma_sem])
            self.dma_last[dma_sem] = tok
        else:
            eng.cnt += 1
            tok = (eng.key, eng.cnt)
        for s, v in need.items():
            eng.known[s] = v
        for b in R:
            for r in b.res:
                if r.r.get(tok[0], 0) < tok[1]:
                    r.r[tok[0]] = tok[1]
        for b in W:
            for r in b.res:
                r.w = tok
                r.r = {}
        eng.ops.append((_freeze(fn), sorted(need.items()), tok, dma_sem is not None))
        return tok

    def pe(self, fn, R=(), W=()):
        return self.emit("pe", fn, R, W)

    def act(self, fn, R=(), W=()):
        return self.emit("act", fn, R, W)

    def dve(self, fn, R=(), W=()):
        return self.emit("dve", fn, R, W)

    def pool(self, fn, R=(), W=()):
        return self.emit("pool", fn, R, W)

    def load(self, out_ap, in_ap, sem, R=(), W=(), **kw):
        return self.emit("sp", lambda h: h.dma_start(out=out_ap, in_=in_ap, **kw), R, W, dma_sem=sem)

    def store(self, out_ap, in_ap, R=(), W=(), **kw):
        sem = self.store_sems[self.store_rr % len(self.store_sems)]
        self.store_rr += 1
        return self.emit(self.store_eng, lambda h: h.dma_start(out=out_ap, in_=in_ap, **kw), R, W, dma_sem=sem)

    def finish(self, stack):
        nc = self.nc
        pool = self.engs["pool"]
        fin = []
        for s, v in self.dma_cnt.items():
            if v > 0 and pool.known.get(s, 0) < v:
                fin.append((s, v))
        sems = [stack.enter_context(nc.semaphore("s%d" % i)) for i in range(self.nsem)]
        block = stack.enter_context(nc.Block())

        def replay(ename, h, extra=()):
            for fn, waits, tok, is_dma in self.engs[ename].ops:
                for s, v in waits:
                    h.wait_ge(sems[s], v)
                fn(h).then_inc(sems[tok[0]], 16 if is_dma else 1)
            for s, v in extra:
                h.wait_ge(sems[s], v)

        @block.tensor
        def _(h):
            replay("pe", h)

        @block.scalar
        def _(h):
            replay("act", h)

        @block.vector
        def _(h):
            replay("dve", h)

        @block.gpsimd
        def _(h):
            replay("pool", h, fin)

        @block.sync
        def _(h):
            replay("sp", h)


def _wspec():
    S = {}
    a = []
    a.append([(0, 512)])
    a.append([(512, 512)])
    a.append([(1536, 512)])
    rq0, rk0 = 2056, 2312

    def swp(base):
        p = []
        for h in range(4):
            p.append((base + 64 * h + 32, 32))
            p.append((base + 64 * h, 32))
        return p
    a.append([(rq0, 256)])
    a.append([(rk0, 256)])
    a.append([(3080, 512)])
    a.append([(1024, 512)])
    a.append([(2568, 512)])
    a.append([(2048, 8)])
    S["A"] = ("w_in_a", 0, 1024, a)
    S["OA"] = ("w_out_a", 0, 1024, [[(0, 512)], [(512, 512)]])
    for l in range(2):
        g = []
        for grp in range(11):
            m0 = 2 * grp
            g.append([(m0 * 128, 256), (DFF + m0 * 128, 256)])
        S["GU%d" % l] = ("w_gu", l, 1024, g)
        S["DN%d" % l] = ("w_down", l, DFF, [[(m * 128, 128)] for m in range(8)])
    S["C"] = ("w_in_c", 0, 1024, [[(0, 512)], [(512, 320)]])
    uq0 = [(h * 192, 128) for h in range(8)]
    uq1 = [(h * 192 + 128, 64) for h in range(8)]
    S["UQ"] = ("w_uq", 0, 512, [uq0, uq1])
    S["UKV"] = ("w_ukv", 0, 256, [[(h * 256, 128) for h in range(8)], [(h * 256 + 128, 128) for h in range(8)]])
    S["OC"] = ("w_out_c", 0, 1024, [[(0, 512)], [(512, 512)]])
    return S


def _merge(pieces):
    out = []
    for c0, n in pieces:
        if out and out[-1][0] + out[-1][1] == c0:
            out[-1] = (out[-1][0], out[-1][1] + n)
        else:
            out.append((c0, n))
    return out


def build(n_ptiles=8, n_pseq=2, n_sseq=2, stages=("mix0", "ffn0", "mix1", "ffn1")):
    from contextlib import ExitStack
    nc = bass.Bass("TRN2", target_bir_lowering=False)
    stack = ExitStack()

    def din(name, shape, dt=F32):
        return nc.dram_tensor(name, list(shape), dt, kind="ExternalInput").ap()

    def dout(name, shape, dt=F32):
        return nc.dram_tensor(name, list(shape), dt, kind="ExternalOutput").ap()

    def dscr(name, shape, dt=BF16):
        return nc.dram_tensor(name, list(shape), dt, kind="Internal").ap()

    I = {}
    I["xp"] = din("xp", [2, SEQ, D])
    I["xs"] = din("xs", [2, DEC_SEQ, D])
    I["stC"] = din("stC", [2, 4, 128, 128])
    I["stn"] = din("stn", [2, 4, 128])
    I["stm"] = din("stm", [2, 4])
    I["stS"] = din("stS", [2, 4, 64, 128])
    I["cckv"] = din("cckv", [2, PAST, 256])
    I["ckr"] = din("ckr", [2, PAST, 64])
    I["w_in_a"] = din("w_in_a", [1, D, PROJ_A])
    I["b_if"] = din("b_if", [8])
    I["g_ml"] = din("g_ml", [512])
    I["g_ret"] = din("g_ret", [512])
    I["w_out_a"] = din("w_out_a", [1, D, D])
    I["w_in_c"] = din("w_in_c", [1, D, 832])
    I["g_q"] = din("g_q", [512])
    I["g_kv"] = din("g_kv", [256])
    I["w_uq"] = din("w_uq", [1, 512, 1536])
    I["w_ukv"] = din("w_ukv", [1, 256, 2048])
    I["w_out_c"] = din("w_out_c", [1, D, D])
    I["ln"] = din("ln", [8, D])
    I["w_gu"] = din("w_gu", [2, D, 2 * DFF])
    I["w_down"] = din("w_down", [2, DFF, D])
    I["ident"] = din("ident", [128, 128])
    I["cosF"] = din("cosF", [128, NPOS])
    I["sinF"] = din("sinF", [128, NPOS])
    I["cosT"] = din("cosT", [NPOS, 32])
    I["sinT"] = din("sinT", [NPOS, 32])
    I["mask_ml"] = din("mask_ml", [128, 64])
    I["mret"] = din("mret", [128, 4, 64])
    I["kdec"] = din("kdec", [128, 2, 128])
    I["qdec"] = din("qdec", [128, 2, 64])
    I["sel"] = din("sel", [4, 4, 128])
    I["dmask"] = din("dmask", [128, 128])

    O = {}
    O["yp"] = dout("yp", [2, SEQ, D])
    O["ys"] = dout("ys", [2, DEC_SEQ, D])
    O["pC"] = dout("pC", [2, 4, 128, 128])
    O["pn"] = dout("pn", [2, 4, 128])
    O["pm"] = dout("pm", [2, 4])
    O["pS"] = dout("pS", [2, 4, 64, 128])
    O["pckv"] = dout("pckv", [2, SEQ, 256])
    O["pkr"] = dout("pkr", [2, SEQ, 64])
    O["sC"] = dout("sC", [2, 4, 128, 128])
    O["sn"] = dout("sn", [2, 4, 128])
    O["sm"] = dout("sm", [2, 4])
    O["sS"] = dout("sS", [2, 4, 64, 128])
    O["sckv"] = dout("sckv", [2, DEC_SEQ, 256])
    O["skr"] = dout("skr", [2, DEC_SEQ, 64])

    P = Prog(nc, arena_bytes=207 * 1024)
    P.store_sems = [P.new_dma_sem() for _ in range(8)]
    ps = P.psum

    def PS(i, parts=128, cols=512, c0=0, p0=0):
        return ps[i].t.ap()[p0:p0 + parts, c0:c0 + cols]

    def PSB(i):
        return ps[i].t.ap().bitcast(BF16)

    spec = _wspec()
    WS = {}
    need = set()
    if "mix0" in stages:
        need |= {"A", "OA"}
    if "ffn0" in stages:
        need |= {"GU0", "DN0"}
    if "mix1" in stages:
        need |= {"C", "UQ", "UKV", "OC"}
    if "ffn1" in stages:
        need |= {"GU1", "DN1"}
    for name in ("A", "OA", "GU0", "DN0", "C", "UKV", "UQ", "OC", "GU1", "DN1"):
        (src, li, K, groups) = spec[name]
        if name not in need:
            continue
        M = I[src].shape[2]
        scr = dscr("ws_%s" % name, [K, M])
        o = Obj()
        for c0 in range(0, M, 2048):
            c1 = min(M, c0 + 2048)
            P.store(scr[:, c0:c1], I[src][li, :, c0:c1], R=(), W=(o,))
        WS[name] = (scr, o, K // 128, [_merge(g) for g in groups])

    sem_c = P.new_dma_sem()
    ident = P.alloc([128, 128], F32)
    P.load(ident.ap, I["ident"], sem_c, W=(ident,))
    identb = P.alloc([128, 128], BF16)
    P.dve(lambda h: h.tensor_copy(out=identb.ap, in_=ident.ap), R=(ident,), W=(identb,))
    onesb = P.alloc([128, 128], BF16)
    P.pool(lambda h: h.memset(onesb.ap, 1.0), W=(onesb,))
    lnp = P.alloc([128, 8, 8], F32)
    lnrow = P.alloc([8, 1024], F32, at=P.arena_bytes - 4096)
    P.load(lnrow.ap, I["ln"], sem_c, W=(lnrow,))
    for m in range(8):
        P.pe(lambda h, m=m: h.transpose(out=ps[0].t.ap()[:, m * 8:(m + 1) * 8], in_=lnrow.ap[0:8, m * 128:(m + 1) * 128],
                                        identity=ident.ap[0:8, 0:8]), R=(lnrow, ident), W=(ps[0],))
    P.dve(lambda h: h.tensor_copy(out=lnp.ap.rearrange("p w m -> p m w"), in_=ps[0].t.ap()[:, 0:64].rearrange("p (m w) -> p m w", m=8)),
          R=(ps[0],), W=(lnp,))

    NW = 3
    WSLOT = 4096
    wring = [P.alloc([128, WSLOT], BF16) for _ in range(NW)]
    wsem = [P.new_dma_sem() for _ in range(NW)]
    wrr = [0]

    def wload(name, gi):
        scr, o, KC, groups = WS[name]
        pieces = groups[gi]
        i = wrr[0] % NW
        wrr[0] += 1
        slot = wring[i]
        n = pieces[0][1]
        cnt = len(pieces)
        ncols = cnt * n
        assert KC * ncols <= WSLOT
        if cnt == 1:
            c0 = pieces[0][0]
            srcap = scr[:, c0:c0 + n].rearrange("(kc p) c -> p kc c", p=128)
            dstap = slot.ap[:, 0:KC * n].rearrange("p (k c) -> p k c", k=KC)
            v = slot.ap[:, 0:KC * n].rearrange("p (k c) -> p k c", k=KC)
            P.load(dstap, srcap, wsem[i], R=(o,), W=(slot,))
            return slot, v
        assert all(p_[1] == n for p_ in pieces)
        v = slot.ap[:, 0:KC * ncols].rearrange("p (k c) -> p k c", k=KC)
        for j, (c0, _) in enumerate(pieces):
            srcap = scr[:, c0:c0 + n].rearrange("(kc p) c -> p kc c", p=128)
            dstap = v[:, :, j * n:(j + 1) * n]
            if j == 0:
                P.load(dstap, srcap, wsem[i], R=(o,), W=(slot,))
            else:
                P.emit("sp", lambda h: h.dma_start(out=dstap, in_=srcap), (), (), dma_sem=wsem[i], chain=True)
        ftok = (wsem[i], P.dma_cnt[wsem[i]])
        for r in slot.res:
            r.w = ftok
        return slot, v

    def wswap(slot, w, KC, ncols):
        wsw = slot.ap[:, KC * ncols:2 * KC * ncols].rearrange("p (k c) -> p k c", k=KC)
        src = w.rearrange("p k (h t j) -> p k h t j", t=2, j=32)
        dst = wsw.rearrange("p k (h t j) -> p k h t j", t=2, j=32)
        P.dve(lambda h: h.tensor_copy(out=dst[:, :, :, 0, :], in_=src[:, :, :, 1, :]), R=(slot,), W=(slot,))
        P.act(lambda h: h.activation(out=dst[:, :, :, 1, :], in_=src[:, :, :, 0, :], func=AF.Copy), R=(slot,), W=(slot,))
        return wsw

    xT = P.alloc([128, 8, 512], F32)
    xTb = P.alloc([128, 8, 512], BF16)
    def subs(buf, n):
        k = len(buf.res) // n
        assert k * n == len(buf.res)
        return [Buf(buf.ap[:, i], buf.res[i * k:(i + 1) * k]) for i in range(n)]
    xTc = subs(xT, 8)
    xTbc = subs(xTb, 8)
    T = [P.alloc([128, 512], F32) for _ in range(4)]
    TB = [P.alloc([128, 512], BF16) for _ in range(4)]
    xsem = [P.new_dma_sem() for _ in range(2)]
    base_mark = P.mark()

    psrr = [0]

    def nextps(lo=0, hi=4):
        i = lo + psrr[0] % (hi - lo)
        psrr[0] += 1
        return i

    def load_x(src_rows, N):
        nb = (N + 127) // 128
        m0 = P.mark()
        for b in range(nb):
            bs = min(128, N - b * 128)
            xin = P.alloc([128, 1024], F32, at=P.arena_bytes - (b + 1) * 4096)
            P.load(xin.ap[0:bs, :], src_rows[b * 128:b * 128 + bs, :], xsem[b % 2], W=(xin,))
            for half in range(2):
                pi = nextps()
                for j in range(4):
                    m = half * 4 + j
                    P.pe(lambda h, m=m, j=j, pi=pi, xin=xin, bs=bs: h.transpose(
                        out=PS(pi, 128, 128, j * 128), in_=xin.ap[:, m * 128:(m + 1) * 128],
                        identity=ident.ap), R=(xin, ident), W=(ps[pi],))
                src = ps[pi].t.ap().rearrange("p (j c) -> p j c", j=4)[:, :, 0:bs]
                P.act(lambda h, src=src, half=half, b=b, bs=bs: h.activation(
                    out=xT.ap[:, half * 4:half * 4 + 4, b * 128:b * 128 + bs], in_=src, func=AF.Copy),
                    R=(ps[pi],), W=(xT,))
                P.dve(lambda h, src=src, half=half, b=b, bs=bs: h.tensor_copy(
                    out=xTb.ap[:, half * 4:half * 4 + 4, b * 128:b * 128 + bs], in_=src),
                    R=(ps[pi],), W=(xTb,))
            P.release(P.mark())
        if not DBG.get("norel"):
            P.release(m0)

    def store_y(dst_rows, N):
        nb = (N + 127) // 128
        m0 = P.mark()
        for b in range(nb):
            bs = min(128, N - b * 128)
            yo = P.alloc([128, 1024], F32, at=P.arena_bytes - 16384 - (b + 1) * 4096)
            for half in range(2):
                pi = nextps()
                for j in range(4):
                    m = half * 4 + j
                    P.pe(lambda h, m=m, j=j, pi=pi, b=b, bs=bs: h.transpose(
                        out=PS(pi, 128, 128, j * 128), in_=xT.ap[:, m, b * 128:(b + 1) * 128],
                        identity=ident.ap), R=(xT, ident), W=(ps[pi],))
                if half == 0:
                    P.act(lambda h, pi=pi, yo=yo, bs=bs: h.activation(
                        out=yo.ap[0:bs, 0:512], in_=PS(pi, bs, 512), func=AF.Copy), R=(ps[pi],), W=(yo,))
                else:
                    P.dve(lambda h, pi=pi, yo=yo, bs=bs: h.tensor_copy(
                        out=yo.ap[0:bs, 512:1024], in_=PS(pi, bs, 512)), R=(ps[pi],), W=(yo,))
            P.store(dst_rows[b * 128:b * 128 + bs, :], yo.ap[0:bs, :], R=(yo,))
        if not DBG.get("norel"):
            P.release(m0)

    def fm_norm(chunks, N, F, eps, center):
        n = len(chunks)
        p2 = nextps()
        p1 = nextps() if center else None
        for i, (b, ap) in enumerate(chunks):
            sq = TB[i % 2]
            P.act(lambda h: h.activation(out=sq.ap[:, 0:N], in_=ap, func=AF.Square), R=(b,), W=(sq,))
            P.pe(lambda h: h.matmul(PS(p2, 128, N), lhsT=onesb.ap, rhs=sq.ap[:, 0:N], start=(i == 0), stop=(i == n - 1)),
                 R=(sq, onesb), W=(ps[p2],))
            if center:
                zb = TB[2 + i % 2]
                P.dve(lambda h: h.tensor_copy(out=zb.ap[:, 0:N], in_=ap), R=(b,), W=(zb,))
                P.pe(lambda h: h.matmul(PS(p1, 128, N), lhsT=onesb.ap, rhs=zb.ap[:, 0:N], start=(i == 0), stop=(i == n - 1)),
                     R=(zb, onesb), W=(ps[p1],))
        mean, var = T[3], T[2]
        if center:
            P.act(lambda h: h.activation(out=mean.ap[:, 0:N], in_=PS(p1, 128, N), func=AF.Identity, scale=1.0 / F),
                  R=(ps[p1],), W=(mean,))
            P.act(lambda h: h.activation(out=var.ap[:, 0:N], in_=mean.ap[:, 0:N], func=AF.Square), R=(mean,), W=(var,))
            P.dve(lambda h: h.scalar_tensor_tensor(out=var.ap[:, 0:N], in0=PS(p2, 128, N), scalar=1.0 / F, in1=var.ap[:, 0:N],
                                                   op0=ALU.mult, op1=ALU.subtract), R=(ps[p2], var), W=(var,))
            P.dve(lambda h: h.tensor_scalar(out=var.ap[:, 0:N], in0=var.ap[:, 0:N], scalar1=0.0, scalar2=float(eps),
                                            op0=ALU.max, op1=ALU.add), R=(var,), W=(var,))
        else:
            P.dve(lambda h: h.tensor_scalar(out=var.ap[:, 0:N], in0=PS(p2, 128, N), scalar1=1.0 / F, scalar2=float(eps),
                                            op0=ALU.mult, op1=ALU.add), R=(ps[p2],), W=(var,))
        P.act(lambda h: h.activation(out=var.ap[:, 0:N], in_=var.ap[:, 0:N], func=AF.Ln), R=(var,), W=(var,))
        P.act(lambda h: h.activation(out=var.ap[:, 0:N], in_=var.ap[:, 0:N], func=AF.Exp, scale=-0.5), R=(var,), W=(var,))
        for i, (b, ap) in enumerate(chunks):
            if center:
                P.dve(lambda h: h.tensor_tensor(out=ap, in0=ap, in1=mean.ap[:, 0:N], op=ALU.subtract), R=(b, mean), W=(b,))
            P.dve(lambda h: h.tensor_tensor(out=ap, in0=ap, in1=var.ap[:, 0:N], op=ALU.mult), R=(b, var), W=(b,))

    epsb = {}
    for e in (LN_EPS, RMS_EPS):
        eb = P.alloc([128, 1], F32)
        P.pool(lambda h, eb=eb, e=e: h.memset(eb.ap, e), W=(eb,))
        epsb[e] = eb
    base_mark = P.mark()

    def ln_affine(N, gi, bi):
        for m in range(8):
            P.act(lambda h: h.activation(out=xTb.ap[:, m, 0:N], in_=xT.ap[:, m, 0:N], func=AF.Identity,
                                         scale=lnp.ap[:, gi, m:m + 1], bias=lnp.ap[:, bi, m:m + 1]),
                  R=(xTc[m], lnp), W=(xTbc[m],))
            P.act(lambda h: h.activation(out=xT.ap[:, m, 0:N], in_=xT.ap[:, m, 0:N], func=AF.Identity,
                                         scale=lnp.ap[:, gi, m:m + 1], bias=lnp.ap[:, bi, m:m + 1]),
                  R=(xTc[m], lnp), W=(xTc[m],))

    ln_pending = []
    LN1, LN2 = 4, 5

    def resid_chunk(m, pi, N):
        while ln_pending:
            ln_pending.pop(0)()
        zb, sq = TB[2 + m % 2], TB[m % 2]
        P.dve(lambda h: h.scalar_tensor_tensor(out=xT.ap[:, m, 0:N], in0=xT.ap[:, m, 0:N], scalar=ALPHA, in1=PS(pi, 128, N),
                                               op0=ALU.mult, op1=ALU.add), R=(xTc[m], ps[pi]), W=(xTc[m],))
        P.dve(lambda h: h.tensor_copy(out=zb.ap[:, 0:N], in_=xT.ap[:, m, 0:N]), R=(xTc[m],), W=(zb,))
        P.act(lambda h: h.activation(out=sq.ap[:, 0:N], in_=xT.ap[:, m, 0:N], func=AF.Square), R=(xTc[m],), W=(sq,))
        def stat_mm(m=m, zb=zb, sq=sq, N=N):
            P.pe(lambda h: h.matmul(PS(LN1, 128, N), lhsT=onesb.ap, rhs=zb.ap[:, 0:N], start=(m == 0), stop=(m == 7)),
                 R=(zb, onesb), W=(ps[LN1],))
            P.pe(lambda h: h.matmul(PS(LN2, 128, N), lhsT=onesb.ap, rhs=sq.ap[:, 0:N], start=(m == 0), stop=(m == 7)),
                 R=(sq, onesb), W=(ps[LN2],))
        ln_pending.append(stat_mm)

    def residual_ln(N, lay, which):
        while ln_pending:
            ln_pending.pop(0)()
        F = float(D)
        gi = (0 if which == 0 else 4) + lay
        bi = gi + 2
        mean, var = T[3], T[2]
        P.act(lambda h: h.activation(out=mean.ap[:, 0:N], in_=PS(LN1, 128, N), func=AF.Identity, scale=1.0 / F),
              R=(ps[LN1],), W=(mean,))
        P.act(lambda h: h.activation(out=var.ap[:, 0:N], in_=mean.ap[:, 0:N], func=AF.Square), R=(mean,), W=(var,))
        P.dve(lambda h: h.scalar_tensor_tensor(out=var.ap[:, 0:N], in0=PS(LN2, 128, N), scalar=1.0 / F, in1=var.ap[:, 0:N],
                                               op0=ALU.mult, op1=ALU.subtract), R=(ps[LN2], var), W=(var,))
        P.dve(lambda h: h.tensor_scalar(out=var.ap[:, 0:N], in0=var.ap[:, 0:N], scalar1=0.0, scalar2=float(LN_EPS),
                                        op0=ALU.max, op1=ALU.add), R=(var,), W=(var,))
        P.act(lambda h: h.activation(out=var.ap[:, 0:N], in_=var.ap[:, 0:N], func=AF.Ln), R=(var,), W=(var,))
        P.act(lambda h: h.activation(out=var.ap[:, 0:N], in_=var.ap[:, 0:N], func=AF.Exp, scale=-0.5), R=(var,), W=(var,))
        for m in range(8):
            P.dve(lambda h: h.tensor_tensor(out=xT.ap[:, m, 0:N], in0=xT.ap[:, m, 0:N], in1=mean.ap[:, 0:N], op=ALU.subtract),
                  R=(xTc[m], mean), W=(xTc[m],))
            P.dve(lambda h: h.tensor_tensor(out=xT.ap[:, m, 0:N], in0=xT.ap[:, m, 0:N], in1=var.ap[:, 0:N], op=ALU.mult),
                  R=(xTc[m], var), W=(xTc[m],))
            P.act(lambda h: h.activation(out=xTb.ap[:, m, 0:N], in_=xT.ap[:, m, 0:N], func=AF.Identity,
                                         scale=lnp.ap[:, gi, m:m + 1], bias=lnp.ap[:, bi, m:m + 1]),
                  R=(xTc[m], lnp), W=(xTbc[m],))
        for m in range(8):
            P.act(lambda h: h.activation(out=xT.ap[:, m, 0:N], in_=xT.ap[:, m, 0:N], func=AF.Identity,
                                         scale=lnp.ap[:, gi, m:m + 1], bias=lnp.ap[:, bi, m:m + 1]),
                  R=(xTc[m], lnp), W=(xTc[m],))

    def ffn(N, lay):
        m0 = P.mark()
        aT = P.alloc([128, 22, 512], BF16)
        gu = "GU%d" % lay
        for grp in range(11):
            slot, w = wload(gu, grp)
            for j in range(2):
                m = 2 * grp + j
                pg = nextps()
                pu = nextps()
                for kc in range(8):
                    P.pe(lambda h, kc=kc, j=j, w=w, pg=pg: h.matmul(
                        PS(pg, 128, N), lhsT=w[:, kc, j * 128:(j + 1) * 128], rhs=xTb.ap[:, kc, 0:N],
                        start=(kc == 0), stop=(kc == 7)), R=(slot, xTb), W=(ps[pg],))
                for kc in range(8):
                    P.pe(lambda h, kc=kc, j=j, w=w, pu=pu: h.matmul(
                        PS(pu, 128, N), lhsT=w[:, kc, 256 + j * 128:256 + (j + 1) * 128], rhs=xTb.ap[:, kc, 0:N],
                        start=(kc == 0), stop=(kc == 7)), R=(slot, xTb), W=(ps[pu],))
                sg = T[m % 2]
                P.act(lambda h, sg=sg, pg=pg: h.activation(out=sg.ap[:, 0:N], in_=PS(pg, 128, N), func=AF.Silu),
                      R=(ps[pg],), W=(sg,))
                P.dve(lambda h, sg=sg, pu=pu, m=m: h.tensor_tensor(
                    out=aT.ap[:, m, 0:N], in0=PS(pu, 128, N), in1=sg.ap[:, 0:N], op=ALU.mult),
                    R=(ps[pu], sg), W=(aT,))
        dn = "DN%d" % lay
        for m in range(8):
            slot, w = wload(dn, m)
            pi = nextps()
            for kc in range(22):
                P.pe(lambda h, kc=kc, w=w, pi=pi: h.matmul(
                    PS(pi, 128, N), lhsT=w[:, kc, :], rhs=aT.ap[:, kc, 0:N],
                    start=(kc == 0), stop=(kc == 21)), R=(slot, aT), W=(ps[pi],))
            resid_chunk(m, pi, N)
        residual_ln(N, lay, 1)
        P.release(m0)

    vrow = P.alloc([2, 1024], F32, at=P.arena_bytes - 8192)
    P.load(vrow.ap[0:1, 0:512], I["g_ml"].rearrange("(o n) -> o n", o=1), sem_c, W=(vrow,))
    P.load(vrow.ap[0:1, 512:1024], I["g_ret"].rearrange("(o n) -> o n", o=1), sem_c, W=(vrow,))
    P.load(vrow.ap[1:2, 0:512], I["g_q"].rearrange("(o n) -> o n", o=1), sem_c, W=(vrow,))
    P.load(vrow.ap[1:2, 512:1024], I["g_q"].rearrange("(o n) -> o n", o=1), sem_c, W=(vrow,))
    vecp = P.alloc([128, 2, 8], F32)
    for m in range(8):
        P.pe(lambda h, m=m: h.transpose(out=ps[1].t.ap()[:, m * 2:(m + 1) * 2], in_=vrow.ap[0:2, m * 128:(m + 1) * 128],
                                        identity=ident.ap[0:2, 0:2]), R=(vrow, ident), W=(ps[1],))
    P.dve(lambda h: h.tensor_copy(out=vecp.ap.rearrange("p w m -> p m w"),
                                  in_=ps[1].t.ap()[:, 0:16].rearrange("p (m w) -> p m w", m=8)), R=(ps[1],), W=(vecp,))
    bif = P.alloc([4, 2], F32)
    P.load(bif.ap, I["b_if"].rearrange("(t h) -> h t", t=2), sem_c, W=(bif,), allow_slow_non_contiguous=True)
    nbf = P.alloc([4, 1], F32)
    P.dve(lambda h: h.tensor_scalar(out=nbf.ap, in0=bif.ap[:, 1:2], scalar1=-1.0, scalar2=None, op0=ALU.mult),
          R=(bif,), W=(nbf,))
    ones4 = P.alloc([4, 512], F32)
    P.pool(lambda h: h.memset(ones4.ap, 1.0), W=(ones4,))
    sel = P.alloc([4, 512], F32)
    P.load(sel.ap, I["sel"].rearrange("k h c -> k (h c)"), sem_c, W=(sel,))
    maskml = P.alloc([128, 64], F32)
    P.load(maskml.ap, I["mask_ml"], sem_c, W=(maskml,))
    mret = P.alloc([128, 4, 64], F32)
    P.load(mret.ap, I["mret"], sem_c, W=(mret,))
    kdec = P.alloc([128, 2, 128], F32)
    P.load(kdec.ap, I["kdec"], sem_c, W=(kdec,))
    qdec = P.alloc([128, 2, 64], F32)
    P.load(qdec.ap, I["qdec"], sem_c, W=(qdec,))
    gkvb = P.alloc([128, 256], F32)
    P.load(gkvb.ap, I["g_kv"].partition_broadcast(128), sem_c, W=(gkvb,))
    dmask = P.alloc([128, 128], F32)
    P.load(dmask.ap, I["dmask"], sem_c, W=(dmask,))
    CaugH = [[P.alloc([128, 256], F32) for _ in range(2)] for _ in range(4)]
    SH = [[P.alloc([128, 128], F32) for _ in range(2)] for _ in range(4)]
    ccur = [0, 0, 0, 0]
    scur = [0, 0, 0, 0]
    Bc = P.alloc([4, 1], F32)
    Gc = P.alloc([4, 1], F32)
    cosb = P.alloc([128, 512], F32)
    sinb = P.alloc([128, 512], F32)
    tsem = P.new_dma_sem()
    GAM = [1.0 - 2.0 ** (-5.0 - h) for h in range(4)]

    def state_zero():
        for hh in range(4):
            ccur[hh] = 0
            scur[hh] = 0
            P.pool(lambda h: h.memset(CaugH[hh][0].ap, 0.0), W=(CaugH[hh][0],))
            P.pool(lambda h: h.memset(SH[hh][0].ap, 0.0), W=(SH[hh][0],))
        P.pool(lambda h: h.memset(Bc.ap, 0.0), W=(Bc,))
        P.pool(lambda h: h.memset(Gc.ap, 0.0), W=(Gc,))

    def state_load(s):
        m0 = P.mark()
        cin = P.alloc([128, 4, 128], F32)
        P.load(cin.ap, I["stC"][s].rearrange("h v d -> v h d"), tsem, W=(cin,))
        nrow = P.alloc([4, 128], F32)
        P.load(nrow.ap, I["stn"][s], tsem, W=(nrow,))
        pi = nextps()
        for hh in range(4):
            P.pe(lambda h: h.transpose(out=PS(pi, 128, 128, hh * 128), in_=cin.ap[:, hh, :], identity=ident.ap),
                 R=(cin, ident), W=(ps[pi],))
        pj = nextps()
        P.pe(lambda h: h.transpose(out=PS(pj, 128, 4), in_=nrow.ap[0:4, :], identity=ident.ap[0:4, 0:4]),
             R=(nrow, ident), W=(ps[pj],))
        ncol = P.alloc([128, 4], F32)
        P.act(lambda h: h.activation(out=ncol.ap, in_=PS(pj, 128, 4), func=AF.Copy), R=(ps[pj],), W=(ncol,))
        for hh in range(4):
            ccur[hh] = 0
            scur[hh] = 0
            ho = (hh % 2) * 64
            P.act(lambda h: h.activation(out=CaugH[hh][0].ap[:, 0:128], in_=PS(pi, 128, 128, hh * 128), func=AF.Copy),
                  R=(ps[pi],), W=(CaugH[hh][0],))
            P.dve(lambda h: h.tensor_copy(out=CaugH[hh][0].ap[:, 128:256], in_=ncol.ap[:, hh:hh + 1].broadcast_to([128, 128])),
                  R=(ncol,), W=(CaugH[hh][0],))
            P.load(SH[hh][0].ap[ho:ho + 64, :], I["stS"][s, hh], tsem, W=(SH[hh][0],))
        P.load(Gc.ap, I["stm"][s].rearrange("(h o) -> h o", o=1), tsem, W=(Gc,))
        P.pool(lambda h: h.memset(Bc.ap, 0.0), W=(Bc,))
        P.release(m0)

    def state_store(kC, kn, km, kS, s):
        m0 = P.mark()
        co = P.alloc([128, 4, 128], F32)
        no = P.alloc([128, 4], F32)
        pi = nextps()
        for hh in range(4):
            cb_ = CaugH[hh][ccur[hh]]
            P.pe(lambda h: h.transpose(out=PS(pi, 128, 128, hh * 128), in_=cb_.ap[:, 0:128], identity=ident.ap),
                 R=(cb_, ident), W=(ps[pi],))
            P.dve(lambda h: h.tensor_copy(out=no.ap[:, hh:hh + 1], in_=cb_.ap[:, 128:129]), R=(cb_,), W=(no,))
        P.act(lambda h: h.activation(out=co.ap, in_=ps[pi].t.ap().rearrange("p (h c) -> p h c", h=4), func=AF.Copy),
              R=(ps[pi],), W=(co,))
        P.store(O[kC][s].rearrange("h v d -> v h d"), co.ap, R=(co,))
        P.store(O[kn][s].rearrange("h d -> d h"), no.ap, R=(no,), allow_slow_non_contiguous=True)
        mo_ = P.alloc([4, 1], F32)
        P.dve(lambda h: h.tensor_tensor(out=mo_.ap, in0=Bc.ap, in1=Gc.ap, op=ALU.add), R=(Bc, Gc), W=(mo_,))
        P.store(O[km][s].rearrange("(h o) -> h o", o=1), mo_.ap, R=(mo_,))
        for hh in range(4):
            ho = (hh % 2) * 64
            sb_ = SH[hh][scur[hh]]
            P.store(O[kS][s, hh], sb_.ap[ho:ho + 64, :], R=(sb_,))
        P.release(m0)

    def mix0(N, pos0):
        nb = (N + 127) // 128
        nch = N // 64
        m0 = P.mark()
        qTm = P.alloc([128, 4, 512], BF16)
        kTm = P.alloc([128, 4, 512], BF16)
        sigo = P.alloc([128, 4, 512], F32)
        silg = P.alloc([128, 4, 512], F32)
        rqT = P.alloc([128, 2, 512], BF16)
        rqd = P.alloc([128, 2, 512], BF16)
        rkT = P.alloc([128, 2, 512], BF16)
        mv_tm = P.alloc([128, 4, 512], BF16)
        rv_tm = P.alloc([128, 4, 512], BF16)
        rk_tm = P.alloc([128, 4, 256], BF16)
        mixed = P.alloc([128, 8, 512], BF16)
        GR = P.alloc([4, 6, 512], F32)
        wk_tm = P.alloc([128, 4, 4], F32)
        dec_b = P.alloc([128, 4, 8], F32)
        kp_tmL = [P.alloc([128, 4, 128], BF16) for _ in range(2)]
        MWL = [P.alloc([128, 4, 64], F32) for _ in range(2)]
        PTL = [P.alloc([128, 4, 64], BF16) for _ in range(4)]
        CbL = [[P.alloc([128, 256], BF16) for _ in range(3)] for _ in range(2)]
        SbL = [[P.alloc([128, 128], BF16) for _ in range(3)] for _ in range(4)]
        for hh_ in range(4):
            for b_ in SbL[hh_]:
                P.pool(lambda h: h.memset(b_.ap, 0.0), W=(b_,))
        hT = [P.alloc([128, 512], F32) for _ in range(2)]
        P.load(cosb.ap[:, 0:N], I["cosF"][:, pos0:pos0 + N], tsem, W=(cosb,))
        P.load(sinb.ap[:, 0:N], I["sinF"][:, pos0:pos0 + N], tsem, W=(sinb,))

        def fm_chain(w, c0, pi, slot):
            for kc in range(8):
                P.pe(lambda h, kc=kc: h.matmul(PS(pi, 128, N), lhsT=w[:, kc, c0:c0 + 128], rhs=xTb.ap[:, kc, 0:N],
                                               start=(kc == 0), stop=(kc == 7)), R=(slot, xTb), W=(ps[pi],))

        for grp, dst, fn, sc in ((0, qTm, AF.Identity, 1.0), (1, kTm, AF.Identity, 128.0 ** -0.5),
                                 (2, sigo, AF.Sigmoid, 1.0), (5, silg, AF.Silu, 1.0)):
            slot, w = wload("A", grp)
            for hh in range(4):
                pi = nextps()
                fm_chain(w, hh * 128, pi, slot)
                P.act(lambda h, hh=hh, pi=pi, dst=dst, fn=fn, sc=sc: h.activation(
                    out=dst.ap[:, hh, 0:N], in_=PS(pi, 128, N), func=fn, scale=sc), R=(ps[pi],), W=(dst,))
        for grp, dst in ((3, rqT), (4, rkT)):
            slot, w = wload("A", grp)
            wsw = wswap(slot, w, 8, 256)
            for pr in range(2):
                pa = nextps()
                fm_chain(w, pr * 128, pa, slot)
                pb = nextps()
                fm_chain(wsw, pr * 128, pb, slot)
                P.dve(lambda h, pa=pa: h.tensor_tensor(out=T[0].ap[:, 0:N], in0=PS(pa, 128, N), in1=cosb.ap[:, 0:N], op=ALU.mult),
                      R=(ps[pa], cosb), W=(T[0],))
                P.dve(lambda h, pb=pb: h.tensor_tensor(out=T[1].ap[:, 0:N], in0=PS(pb, 128, N), in1=sinb.ap[:, 0:N], op=ALU.mult),
                      R=(ps[pb], sinb), W=(T[1],))
                P.pool(lambda h: h.tensor_tensor(out=T[0].ap[:, 0:N], in0=T[0].ap[:, 0:N], in1=T[1].ap[:, 0:N], op=ALU.add),
                       R=(T[0], T[1]), W=(T[0],))
                P.act(lambda h, dst=dst, pr=pr: h.activation(out=dst.ap[:, pr, 0:N], in_=T[0].ap[:, 0:N], func=AF.Copy),
                      R=(T[0],), W=(dst,))
                if grp == 3:
                    P.dve(lambda h, pr=pr: h.tensor_tensor(
                        out=rqd.ap[:, pr, 0:N].rearrange("p (c t) -> p c t", t=64),
                        in0=T[0].ap[:, 0:N].rearrange("p (c t) -> p c t", t=64),
                        in1=qdec.ap[:, pr:pr + 1, :].broadcast_to([128, nch, 64]), op=ALU.mult),
                        R=(T[0], qdec), W=(rqd,))
        if DBG.get('stop') == 1:
            P.release(m0)
            return
        DBG.get('phase_hook', lambda n, p: None)('m0_vproj', P)
        for grp, dst in ((6, mv_tm), (7, rv_tm)):
            slot, w = wload("A", grp)
            for b in range(nb):
                bs = min(128, N - b * 128)
                pi = nextps()
                for kc in range(8):
                    P.pe(lambda h, kc=kc, b=b, bs=bs, pi=pi, w=w: h.matmul(
                        PS(pi, bs, 512), lhsT=xTb.ap[:, kc, b * 128:b * 128 + bs], rhs=w[:, kc, 0:512],
                        start=(kc == 0), stop=(kc == 7)), R=(slot, xTb), W=(ps[pi],))
                if b % 2 == 0:
                    P.act(lambda h, b=b, bs=bs, pi=pi, dst=dst: h.activation(out=dst.ap[0:bs, b, :], in_=PS(pi, bs, 512), func=AF.Copy),
                          R=(ps[pi],), W=(dst,))
                else:
                    P.dve(lambda h, b=b, bs=bs, pi=pi, dst=dst: h.tensor_copy(out=dst.ap[0:bs, b, :], in_=PS(pi, bs, 512)),
                          R=(ps[pi],), W=(dst,))
        if DBG.get('stop') == 2:
            P.release(m0)
            return
        DBG.get('phase_hook', lambda n, p: None)('m0_gates', P)
        slot, w = wload("A", 8)
        pig = nextps()
        pfg = nextps()
        for kc in range(8):
            P.pe(lambda h, kc=kc: h.matmul(PS(pig, 4, N), lhsT=w[:, kc, 0:4], rhs=xTb.ap[:, kc, 0:N],
                                           start=(kc == 0), stop=(kc == 7)), R=(slot, xTb), W=(ps[pig],))
        for kc in range(8):
            P.pe(lambda h, kc=kc: h.matmul(PS(pfg, 4, N), lhsT=w[:, kc, 4:8], rhs=xTb.ap[:, kc, 0:N],
                                           start=(kc == 0), stop=(kc == 7)), R=(slot, xTb), W=(ps[pfg],))
        gL, gB, gA, gG, gX, gE = [GR.ap[:, i, :] for i in range(6)]
        P.act(lambda h: h.activation(out=gL[:, 0:N], in_=PS(pfg, 4, N), func=AF.Exp, scale=-1.0, bias=nbf.ap[:, 0:1]),
              R=(ps[pfg], nbf), W=(GR,))
        P.act(lambda h: h.activation(out=gL[:, 0:N], in_=gL[:, 0:N], func=AF.Ln, bias=1.0, scale=1.0), R=(GR,), W=(GR,))
        P.dve(lambda h: h.tensor_tensor_scan(out=gB[:, 0:N], data0=ones4.ap[:, 0:N], data1=gL[:, 0:N], initial=Bc.ap[:, 0:1],
                                             op0=ALU.mult, op1=ALU.subtract), R=(GR, ones4, Bc), W=(GR,))
        P.dve(lambda h: h.scalar_tensor_tensor(out=gA[:, 0:N], in0=PS(pig, 4, N), scalar=bif.ap[:, 0:1], in1=gB[:, 0:N],
                                               op0=ALU.add, op1=ALU.subtract), R=(ps[pig], bif, GR), W=(GR,))
        P.dve(lambda h: h.tensor_tensor_scan(out=gG[:, 0:N], data0=ones4.ap[:, 0:N], data1=gA[:, 0:N], initial=Gc.ap[:, 0:1],
                                             op0=ALU.mult, op1=ALU.max), R=(GR, ones4, Gc), W=(GR,))
        g3 = lambda a: a[:, 0:N].rearrange("p (c t) -> p c t", t=64)
        P.act(lambda h: h.activation(out=g3(gX), in_=g3(gG)[:, :, 63:64].broadcast_to([4, nch, 64]), func=AF.Copy),
              R=(GR,), W=(GR,))
        P.dve(lambda h: h.tensor_copy(out=gE[:, 0:1], in_=Gc.ap[:, 0:1]), R=(Gc,), W=(GR,))
        if nch > 1:
            P.dve(lambda h: h.tensor_copy(out=gE[:, 1:nch], in_=g3(gG)[:, 0:nch - 1, 63]), R=(GR,), W=(GR,))
        P.dve(lambda h: h.tensor_tensor(out=gE[:, 0:nch], in0=gE[:, 0:nch], in1=g3(gG)[:, :, 63], op=ALU.subtract),
              R=(GR,), W=(GR,))
        P.act(lambda h: h.activation(out=gE[:, 0:nch], in_=gE[:, 0:nch], func=AF.Exp), R=(GR,), W=(GR,))
        P.dve(lambda h: h.tensor_tensor(out=gA[:, 0:N], in0=gA[:, 0:N], in1=gX[:, 0:N], op=ALU.subtract), R=(GR,), W=(GR,))
        P.act(lambda h: h.activation(out=gA[:, 0:N], in_=gA[:, 0:N], func=AF.Exp), R=(GR,), W=(GR,))
        P.dve(lambda h: h.tensor_tensor(out=gX[:, 0:N], in0=gX[:, 0:N], in1=gB[:, 0:N], op=ALU.add), R=(GR,), W=(GR,))
        P.act(lambda h: h.activation(out=Bc.ap, in_=gB[:, N - 1:N], func=AF.Copy), R=(GR,), W=(Bc,))
        P.act(lambda h: h.activation(out=Gc.ap, in_=gG[:, N - 1:N], func=AF.Copy), R=(GR,), W=(Gc,))
        if DBG.get('stop') == 3:
            P.release(m0)
            return
        pi = nextps()
        for b in range(nb):
            P.pe(lambda h, b=b, pi=pi: h.transpose(out=PS(pi, 128, 4, b * 4), in_=gA[0:4, b * 128:(b + 1) * 128],
                                                   identity=ident.ap[0:4, 0:4]), R=(GR, ident), W=(ps[pi],))
        P.dve(lambda h, pi=pi: h.tensor_copy(out=wk_tm.ap[:, 0:nb, :], in_=PS(pi, 128, 4 * nb).rearrange("p (b f) -> p b f", f=4)),
              R=(ps[pi],), W=(wk_tm,))
        pi = nextps()
        for hh in range(4):
            P.pe(lambda h, hh=hh, pi=pi: h.matmul(PS(pi, 128, nch, hh * 8), lhsT=sel.ap[0:4, hh * 128:(hh + 1) * 128],
                                                  rhs=gE[0:4, 0:nch], start=True, stop=True), R=(sel, GR), W=(ps[pi],))
        P.dve(lambda h, pi=pi: h.tensor_copy(out=dec_b.ap[:, :, 0:nch],
                                             in_=PS(pi, 128, 32).rearrange("p (a c) -> p a c", c=8)[:, :, 0:nch]),
              R=(ps[pi],), W=(dec_b,))
        for pr in range(2):
            pi = nextps()
            for b in range(nb):
                bs = min(128, N - b * 128)
                P.pe(lambda h, b=b, bs=bs, pi=pi, pr=pr: h.transpose(
                    out=PSB(pi)[0:bs, b * 128:(b + 1) * 128], in_=rkT.ap[:, pr, b * 128:b * 128 + bs], identity=identb.ap),
                    R=(rkT, identb), W=(ps[pi],))
            for b in range(nb):
                bs = min(128, N - b * 128)
                P.dve(lambda h, b=b, bs=bs, pi=pi, pr=pr: h.tensor_tensor(
                    out=rk_tm.ap[0:bs, b, pr * 128:(pr + 1) * 128], in0=PSB(pi)[0:bs, b * 128:(b + 1) * 128],
                    in1=kdec.ap[0:bs, pr, :], op=ALU.mult), R=(ps[pi], kdec), W=(rk_tm,))

        if DBG.get('stop') == 4:
            P.release(m0)
            return
        DBG.get('phase_hook', lambda n, p: None)('m0_heads_ml', P)
        def headnorm_out(src, hidx, gcol, gate):
            fm_norm([(src, src.ap[:, 0:N])], N, 128.0, LN_EPS, True)
            P.dve(lambda h: h.scalar_tensor_tensor(out=mixed.ap[:, hidx, 0:N], in0=src.ap[:, 0:N], scalar=gcol,
                                                   in1=gate, op0=ALU.mult, op1=ALU.mult),
                  R=(src, vecp, sigo, silg), W=(mixed,))

        def run_interleaved(gens):
            live = list(gens)
            while live:
                nxt = []
                for g_ in live:
                    try:
                        next(g_)
                        nxt.append(g_)
                    except StopIteration:
                        pass
                live = nxt

        def mlstm_head(hh, sl, PIN, PDN):
            kp_tm, MW, PT = kp_tmL[sl], MWL[sl], PTL[sl]
            pi = nextps()
            for b in range(nb):
                bs = min(128, N - b * 128)
                P.pe(lambda h: h.transpose(out=PSB(pi)[0:bs, b * 128:(b + 1) * 128], in_=kTm.ap[:, hh, b * 128:b * 128 + bs],
                                           identity=identb.ap), R=(kTm, identb), W=(ps[pi],))
            for b in range(nb):
                bs = min(128, N - b * 128)
                P.dve(lambda h: h.tensor_scalar(out=kp_tm.ap[0:bs, b, :], in0=PSB(pi)[0:bs, b * 128:(b + 1) * 128],
                                                scalar1=wk_tm.ap[0:bs, b, hh:hh + 1], scalar2=None, op0=ALU.mult),
                      R=(ps[pi], wk_tm), W=(kp_tm,))
            P.pool(lambda h: h.tensor_tensor(out=MW.ap[:, 0:nb, :], in0=maskml.ap[:, None, :].broadcast_to([128, nb, 64]),
                                             in1=wk_tm.ap[:, 0:nb, hh:hh + 1].broadcast_to([128, nb, 64]), op=ALU.mult),
                   R=(maskml, wk_tm), W=(MW,))
            yield
            psc = nextps()
            for c in range(nch):
                b, hf = c // 2, c % 2
                P.pe(lambda h: h.matmul(PS(psc, 64, 64, b * 64, hf * 64), lhsT=kTm.ap[:, hh, c * 64:(c + 1) * 64],
                                        rhs=qTm.ap[:, hh, c * 64:(c + 1) * 64], start=True, stop=True),
                     R=(kTm, qTm), W=(ps[psc],))
            P.dve(lambda h: h.tensor_tensor(out=PT.ap[:, 0:nb, :], in0=PS(psc, 128, nb * 64).rearrange("p (b t) -> p b t", t=64),
                                            in1=MW.ap[:, 0:nb, :], op=ALU.mult), R=(ps[psc], MW), W=(PT,))
            yield
            for c in range(nch):
                b, hf = c // 2, c % 2
                r0 = hf * 64
                cs = slice(c * 64, (c + 1) * 64)
                dcol = dec_b.ap[:, hh, c:c + 1]
                Cold = CaugH[hh][ccur[hh]]
                Cnew = CaugH[hh][1 - ccur[hh]]
                ccur[hh] = 1 - ccur[hh]
                Cb = CbL[sl][c % 3]
                P.act(lambda h: h.activation(out=Cb.ap, in_=Cold.ap, func=AF.Identity, scale=dcol), R=(Cold, dec_b), W=(Cb,))
                pdc = nextps()
                P.pe(lambda h: h.matmul(PS(pdc, 128, 128), lhsT=kp_tm.ap[r0:r0 + 64, b, :],
                                        rhs=mv_tm.ap[r0:r0 + 64, b, hh * 128:(hh + 1) * 128], start=True, stop=True),
                     R=(kp_tm, mv_tm), W=(ps[pdc],))
                P.pe(lambda h: h.matmul(PS(pdc, 128, 128, 128), lhsT=kp_tm.ap[r0:r0 + 64, b, :], rhs=onesb.ap[r0:r0 + 64, :],
                                        start=True, stop=True), R=(kp_tm, onesb), W=(ps[pdc],))
                P.dve(lambda h: h.scalar_tensor_tensor(out=Cnew.ap, in0=Cold.ap, scalar=dcol, in1=PS(pdc, 128, 256),
                                                       op0=ALU.mult, op1=ALU.add), R=(Cold, dec_b, ps[pdc]), W=(Cnew,))
                P.pe(lambda h: h.matmul(PS(PIN, 128, 64, cs.start), lhsT=Cb.ap[:, 0:128], rhs=qTm.ap[:, hh, cs],
                                        start=True, stop=False), R=(Cb, qTm), W=(ps[PIN],))
                P.pe(lambda h: h.matmul(PS(PIN, 128, 64, cs.start), lhsT=mv_tm.ap[r0:r0 + 64, b, hh * 128:(hh + 1) * 128],
                                        rhs=PT.ap[r0:r0 + 64, b, :], start=False, stop=True), R=(mv_tm, PT), W=(ps[PIN],))
                P.pe(lambda h: h.matmul(PS(PDN, 128, 64, cs.start), lhsT=Cb.ap[:, 128:256], rhs=qTm.ap[:, hh, cs],
                                        start=True, stop=False), R=(Cb, qTm), W=(ps[PDN],))
                P.pe(lambda h: h.matmul(PS(PDN, 128, 64, cs.start), lhsT=onesb.ap[r0:r0 + 64, :], rhs=PT.ap[r0:r0 + 64, b, :],
                                        start=False, stop=True), R=(onesb, PT), W=(ps[PDN],))
                yield
            pe_ = nextps()
            P.pe(lambda h: h.matmul(PS(pe_, 128, N), lhsT=sel.ap[0:4, hh * 128:(hh + 1) * 128], rhs=gX[0:4, 0:N],
                                    start=True, stop=True), R=(sel, GR), W=(ps[pe_],))
            P.act(lambda h: h.activation(out=T[0].ap[:, 0:N], in_=PS(pe_, 128, N), func=AF.Exp, scale=-1.0),
                  R=(ps[pe_],), W=(T[0],))
            P.act(lambda h: h.activation(out=T[1].ap[:, 0:N], in_=PS(PDN, 128, N), func=AF.Abs), R=(ps[PDN],), W=(T[1],))
            P.dve(lambda h: h.tensor_tensor(out=T[1].ap[:, 0:N], in0=T[1].ap[:, 0:N], in1=T[0].ap[:, 0:N], op=ALU.max),
                  R=(T[1], T[0]), W=(T[1],))
            P.act(lambda h: h.activation(out=T[1].ap[:, 0:N], in_=T[1].ap[:, 0:N], func=AF.Ln), R=(T[1],), W=(T[1],))
            P.act(lambda h: h.activation(out=T[1].ap[:, 0:N], in_=T[1].ap[:, 0:N], func=AF.Exp, scale=-1.0), R=(T[1],), W=(T[1],))
            hbuf = hT[sl]
            P.dve(lambda h: h.tensor_tensor(out=hbuf.ap[:, 0:N], in0=PS(PIN, 128, N), in1=T[1].ap[:, 0:N], op=ALU.mult),
                  R=(ps[PIN], T[1]), W=(hbuf,))
            yield
            headnorm_out(hbuf, hh, vecp.ap[:, 0, hh:hh + 1], sigo.ap[:, hh, 0:N])

        for h0 in (0, 2):
            run_interleaved([mlstm_head(h0, 0, 4, 5), mlstm_head(h0 + 1, 1, 6, 7)])

        DBG.get('phase_hook', lambda n, p: None)('m0_heads_ret', P)
        def ret_head(hh, PIN):
            pr, ho = hh // 2, (hh % 2) * 64
            PT = PTL[hh]
            psc = nextps()
            for c in range(nch):
                b, hf = c // 2, c % 2
                P.pe(lambda h: h.matmul(PS(psc, 64, 64, b * 64, hf * 64), lhsT=rkT.ap[ho:ho + 64, pr, c * 64:(c + 1) * 64],
                                        rhs=rqT.ap[ho:ho + 64, pr, c * 64:(c + 1) * 64], start=True, stop=True),
                     R=(rkT, rqT), W=(ps[psc],))
            P.dve(lambda h: h.tensor_tensor(out=PT.ap[:, 0:nb, :], in0=PS(psc, 128, nb * 64).rearrange("p (b t) -> p b t", t=64),
                                            in1=mret.ap[:, hh:hh + 1, :].broadcast_to([128, nb, 64]), op=ALU.mult),
                  R=(ps[psc], mret), W=(PT,))
            yield
            for c in range(nch):
                b, hf = c // 2, c % 2
                r0 = hf * 64
                cs = slice(c * 64, (c + 1) * 64)
                Sold = SH[hh][scur[hh]]
                Snew = SH[hh][1 - scur[hh]]
                scur[hh] = 1 - scur[hh]
                Sb = SbL[hh][c % 3]
                P.act(lambda h: h.activation(out=Sb.ap[ho:ho + 64, :], in_=Sold.ap[ho:ho + 64, :], func=AF.Copy),
                      R=(Sold,), W=(Sb,))
                pdc = nextps()
                P.pe(lambda h: h.matmul(PS(pdc, 64, 128, 0, ho), lhsT=rk_tm.ap[r0:r0 + 64, b, pr * 128 + ho:pr * 128 + ho + 64],
                                        rhs=rv_tm.ap[r0:r0 + 64, b, hh * 128:(hh + 1) * 128], start=True, stop=True),
                     R=(rk_tm, rv_tm), W=(ps[pdc],))
                P.dve(lambda h: h.scalar_tensor_tensor(out=Snew.ap[ho:ho + 64, :], in0=Sold.ap[ho:ho + 64, :],
                                                       scalar=float(GAM[hh] ** 64), in1=PS(pdc, 64, 128, 0, ho),
                                                       op0=ALU.mult, op1=ALU.add), R=(Sold, ps[pdc]), W=(Snew,))
                P.pe(lambda h: h.matmul(PS(PIN, 128, 64, cs.start), lhsT=Sb.ap[:, :], rhs=rqd.ap[:, pr, cs],
                                        start=True, stop=False), R=(Sb, rqd), W=(ps[PIN],))
                P.pe(lambda h: h.matmul(PS(PIN, 128, 64, cs.start), lhsT=rv_tm.ap[r0:r0 + 64, b, hh * 128:(hh + 1) * 128],
                                        rhs=PT.ap[r0:r0 + 64, b, :], start=False, stop=True), R=(rv_tm, PT), W=(ps[PIN],))
                yield
            hbuf = hT[hh % 2]
            P.act(lambda h: h.activation(out=hbuf.ap[:, 0:N], in_=PS(PIN, 128, N), func=AF.Copy), R=(ps[PIN],), W=(hbuf,))
            headnorm_out(hbuf, 4 + hh, vecp.ap[:, 0, 4 + hh:5 + hh], silg.ap[:, hh, 0:N])

        run_interleaved([ret_head(hh, 4 + hh) for hh in range(4)])

        DBG.get('phase_hook', lambda n, p: None)('m0_oproj', P)
        for g in range(2):
            slot, w = wload("OA", g)
            for j in range(4):
                m = g * 4 + j
                pi = nextps()
                for kc in range(8):
                    P.pe(lambda h, kc=kc, j=j, pi=pi, w=w: h.matmul(
                        PS(pi, 128, N), lhsT=w[:, kc, j * 128:(j + 1) * 128], rhs=mixed.ap[:, kc, 0:N],
                        start=(kc == 0), stop=(kc == 7)), R=(slot, mixed), W=(ps[pi],))
                resid_chunk(m, pi, N)
        residual_ln(N, 0, 0)
        P.release(m0)

    krT2 = P.alloc([128, 4096], BF16)
    KT_d = dscr("kt_scr", [8, 128, 4096])
    V_d = dscr("v_scr", [4096 + 128, 1024])
    kvo = [Obj() for _ in range(9)]
    kvsem = [P.new_dma_sem() for _ in range(2)]
    tsem2 = P.new_dma_sem()

    def kv_expand(ckvT, N, kpos0):
        nb = (N + 127) // 128
        reg = kvo[kpos0 // 512]
        m0 = P.mark()
        Kst = P.alloc([128, 8, 512], BF16)
        Vst = P.alloc([128, 4, 1024], BF16)
        slot, w = wload("UKV", 0)
        for hh in range(8):
            pi = nextps()
            for kc in range(2):
                P.pe(lambda h, kc=kc: h.matmul(PS(pi, 128, N), lhsT=w[:, kc, hh * 128:(hh + 1) * 128], rhs=ckvT.ap[:, kc, 0:N],
                                               start=(kc == 0), stop=(kc == 1)), R=(slot, ckvT), W=(ps[pi],))
            if hh % 2 == 0:
                P.act(lambda h: h.activation(out=Kst.ap[:, hh, 0:N], in_=PS(pi, 128, N), func=AF.Copy), R=(ps[pi],), W=(Kst,))
            else:
                P.dve(lambda h: h.tensor_copy(out=Kst.ap[:, hh, 0:N], in_=PS(pi, 128, N)), R=(ps[pi],), W=(Kst,))
        P.store(KT_d[:, :, kpos0:kpos0 + N].rearrange("h d n -> d h n"), Kst.ap[:, :, 0:N], R=(Kst,), W=(reg,))
        slot, w = wload("UKV", 1)
        for b in range(nb):
            bs = min(128, N - b * 128)
            for half in range(2):
                pi = nextps()
                for kc in range(2):
                    P.pe(lambda h, kc=kc: h.matmul(PS(pi, bs, 512), lhsT=ckvT.ap[:, kc, b * 128:b * 128 + bs],
                                                   rhs=w[:, kc, half * 512:(half + 1) * 512],
                                                   start=(kc == 0), stop=(kc == 1)), R=(slot, ckvT), W=(ps[pi],))
                if half == 0:
                    P.act(lambda h: h.activation(out=Vst.ap[0:bs, b, 0:512], in_=PS(pi, bs, 512), func=AF.Copy),
                          R=(ps[pi],), W=(Vst,))
                else:
                    P.dve(lambda h: h.tensor_copy(out=Vst.ap[0:bs, b, 512:1024], in_=PS(pi, bs, 512)), R=(ps[pi],), W=(Vst,))
        if N % 128 == 0:
            P.store(V_d[kpos0:kpos0 + N, :].rearrange("(b p) c -> p b c", p=128), Vst.ap[:, 0:nb, :], R=(Vst,), W=(reg,))
        else:
            P.store(V_d[kpos0:kpos0 + N, :], Vst.ap[0:N, 0, :], R=(Vst,), W=(reg,))
        P.release(m0)

    def tm_to_T(ckv_b, kr_b, ckvT, b, bs, kpos):
        pi = nextps()
        for kc in range(2):
            P.pe(lambda h, kc=kc: h.transpose(out=PSB(pi)[:, kc * 128:kc * 128 + bs], in_=ckv_b.ap[0:bs, kc * 128:(kc + 1) * 128],
                                              identity=identb.ap[0:bs, 0:bs]), R=(ckv_b, identb), W=(ps[pi],))
        P.pe(lambda h: h.transpose(out=PSB(pi)[:, 256:256 + bs], in_=kr_b.ap[0:bs, :], identity=identb.ap[0:bs, 0:bs]),
             R=(kr_b, identb), W=(ps[pi],))
        P.act(lambda h: h.activation(out=ckvT.ap[:, :, b * 128:b * 128 + bs],
                                     in_=PSB(pi)[:, 0:256].rearrange("p (k c) -> p k c", k=2)[:, :, 0:bs], func=AF.Copy),
              R=(ps[pi],), W=(ckvT,))
        P.dve(lambda h: h.tensor_copy(out=krT2.ap[:, kpos:kpos + bs], in_=PSB(pi)[:, 256:256 + bs]), R=(ps[pi],), W=(krT2,))

    def kv_prefill(s):
        for c in range(PAST // 512):
            m0 = P.mark()
            cin = P.alloc([128, 4, 256], F32)
            kin = P.alloc([128, 4, 64], F32)
            P.load(cin.ap, I["cckv"][s, c * 512:(c + 1) * 512, :].rearrange("(b p) f -> p b f", p=128), tsem2, W=(cin,))
            P.load(kin.ap, I["ckr"][s, c * 512:(c + 1) * 512, :].rearrange("(b p) f -> p b f", p=128), tsem2, W=(kin,))
            ckvT = P.alloc([128, 2, 512], BF16)
            for b in range(4):
                ckv_b = P.alloc([128, 256], BF16)
                kr_b = P.alloc([128, 128], BF16)
                P.act(lambda h: h.activation(out=ckv_b.ap, in_=cin.ap[:, b, :], func=AF.Copy), R=(cin,), W=(ckv_b,))
                P.dve(lambda h: h.tensor_copy(out=kr_b.ap.rearrange("p (a c) -> p a c", a=2),
                                              in_=kin.ap[:, b:b + 1, :].broadcast_to([128, 2, 64])), R=(kin,), W=(kr_b,))
                tm_to_T(ckv_b, kr_b, ckvT, b, 128, c * 512 + b * 128)
            kv_expand(ckvT, 512, c * 512)
            P.release(m0)

    def mix1(N, pos0, kpos0, is_prompt, okv, okr, s, orow0):
        nb = (N + 127) // 128
        m0 = P.mark()
        qnT = P.alloc([128, 8, 512], BF16)
        qrT = P.alloc([128, 8, 512], BF16)
        P.pool(lambda h: h.memset(qrT.ap, 0.0), W=(qrT,))
        oT = P.alloc([128, 8, 512], BF16)
        m1 = P.mark()
        cqT = P.alloc([128, 4, 512], F32)
        cqTb = P.alloc([128, 4, 512], BF16)
        ckvT = P.alloc([128, 2, 512], BF16)
        ctm = P.alloc([128, 4, 32], F32)
        stm_ = P.alloc([128, 4, 32], F32)
        P.load(cosb.ap[:, 0:N], I["cosF"][:, pos0:pos0 + N], tsem, W=(cosb,))
        P.load(sinb.ap[:, 0:N], I["sinF"][:, pos0:pos0 + N], tsem, W=(sinb,))
        if N % 128 == 0:
            P.load(ctm.ap[:, 0:nb, :], I["cosT"][pos0:pos0 + N, :].rearrange("(b p) j -> p b j", p=128), tsem2, W=(ctm,))
            P.load(stm_.ap[:, 0:nb, :], I["sinT"][pos0:pos0 + N, :].rearrange("(b p) j -> p b j", p=128), tsem2, W=(stm_,))
        else:
            P.load(ctm.ap[0:N, 0, :], I["cosT"][pos0:pos0 + N, :], tsem2, W=(ctm,))
            P.load(stm_.ap[0:N, 0, :], I["sinT"][pos0:pos0 + N, :], tsem2, W=(stm_,))
        slot, w = wload("C", 0)
        for m in range(4):
            pi = nextps()
            for kc in range(8):
                P.pe(lambda h, kc=kc: h.matmul(PS(pi, 128, N), lhsT=w[:, kc, m * 128:(m + 1) * 128], rhs=xTb.ap[:, kc, 0:N],
                                               start=(kc == 0), stop=(kc == 7)), R=(slot, xTb), W=(ps[pi],))
            P.act(lambda h: h.activation(out=cqT.ap[:, m, 0:N], in_=PS(pi, 128, N), func=AF.Copy), R=(ps[pi],), W=(cqT,))
        fm_norm([(cqT, cqT.ap[:, m, 0:N]) for m in range(4)], N, 512.0, RMS_EPS, False)
        for m in range(4):
            P.act(lambda h: h.activation(out=cqTb.ap[:, m, 0:N], in_=cqT.ap[:, m, 0:N], func=AF.Identity,
                                         scale=vecp.ap[:, 1, m:m + 1]), R=(cqT, vecp), W=(cqTb,))
        DBG.get('phase_hook', lambda n, p: None)('m1_ckv', P)
        slot, w = wload("C", 1)
        for b in range(nb):
            bs = min(128, N - b * 128)
            mb = P.mark()
            ckvo = P.alloc([128, 256], F32)
            kro = P.alloc([128, 64], F32)
            ckv_b = P.alloc([128, 256], BF16)
            kr_b = P.alloc([128, 128], BF16)
            sm = P.alloc([128, 8], F32)
            rt = P.alloc([128, 4, 32], F32)
            pi = nextps()
            for kc in range(8):
                P.pe(lambda h, kc=kc: h.matmul(PS(pi, bs, 320), lhsT=xTb.ap[:, kc, b * 128:b * 128 + bs], rhs=w[:, kc, 0:320],
                                               start=(kc == 0), stop=(kc == 7)), R=(slot, xTb), W=(ps[pi],))
            P.act(lambda h: h.activation(out=ckvo.ap[0:bs, :], in_=PS(pi, bs, 256), func=AF.Square, accum_out=sm.ap[0:bs, 0:1]),
                  R=(ps[pi],), W=(ckvo, sm))
            P.act(lambda h: h.activation(out=sm.ap[0:bs, 1:2], in_=sm.ap[0:bs, 0:1], func=AF.Ln, scale=1.0 / 256,
                                         bias=epsb[RMS_EPS].ap[0:bs, 0:1]), R=(sm, epsb[RMS_EPS]), W=(sm,))
            P.act(lambda h: h.activation(out=sm.ap[0:bs, 2:3], in_=sm.ap[0:bs, 1:2], func=AF.Exp, scale=-0.5), R=(sm,), W=(sm,))
            P.dve(lambda h: h.scalar_tensor_tensor(out=ckvo.ap[0:bs, :], in0=PS(pi, bs, 256), scalar=sm.ap[0:bs, 2:3],
                                                   in1=gkvb.ap[0:bs, :], op0=ALU.mult, op1=ALU.mult),
                  R=(ps[pi], sm, gkvb), W=(ckvo,))
            P.store(O[okv][s, orow0 + b * 128:orow0 + b * 128 + bs, :], ckvo.ap[0:bs, :], R=(ckvo,))
            P.act(lambda h: h.activation(out=ckv_b.ap[0:bs, :], in_=ckvo.ap[0:bs, :], func=AF.Copy), R=(ckvo,), W=(ckv_b,))
            x1 = PS(pi, bs, 32, 256)
            x2 = PS(pi, bs, 32, 288)
            cs_, sn_ = ctm.ap[0:bs, b, :], stm_.ap[0:bs, b, :]
            for j, (xa, tb_) in enumerate(((x1, cs_), (x2, sn_), (x2, cs_), (x1, sn_))):
                P.dve(lambda h, j=j, xa=xa, tb_=tb_: h.tensor_tensor(out=rt.ap[0:bs, j, :], in0=xa, in1=tb_, op=ALU.mult),
                      R=(ps[pi], ctm, stm_), W=(rt,))
            P.dve(lambda h: h.tensor_tensor(out=kro.ap[0:bs, 0:32], in0=rt.ap[0:bs, 0, :], in1=rt.ap[0:bs, 1, :], op=ALU.subtract),
                  R=(rt,), W=(kro,))
            P.dve(lambda h: h.tensor_tensor(out=kro.ap[0:bs, 32:64], in0=rt.ap[0:bs, 2, :], in1=rt.ap[0:bs, 3, :], op=ALU.add),
                  R=(rt,), W=(kro,))
            P.store(O[okr][s, orow0 + b * 128:orow0 + b * 128 + bs, :], kro.ap[0:bs, :], R=(kro,))
            P.act(lambda h: h.activation(out=kr_b.ap[0:bs, :].rearrange("p (a c) -> p a c", a=2),
                                         in_=kro.ap[0:bs, None, :].broadcast_to([bs, 2, 64]), func=AF.Copy), R=(kro,), W=(kr_b,))
            tm_to_T(ckv_b, kr_b, ckvT, b, bs, kpos0 + b * 128)
            P.release(mb)
        DBG.get('phase_hook', lambda n, p: None)('m1_kvexp', P)
        kv_expand(ckvT, N, kpos0)
        slot, w = wload("UQ", 0)
        for hh in range(8):
            pi = nextps()
            for kc in range(4):
                P.pe(lambda h, kc=kc: h.matmul(PS(pi, 128, N), lhsT=w[:, kc, hh * 128:(hh + 1) * 128], rhs=cqTb.ap[:, kc, 0:N],
                                               start=(kc == 0), stop=(kc == 3)), R=(slot, cqTb), W=(ps[pi],))
            if hh % 2 == 0:
                P.act(lambda h: h.activation(out=qnT.ap[:, hh, 0:N], in_=PS(pi, 128, N), func=AF.Copy), R=(ps[pi],), W=(qnT,))
            else:
                P.dve(lambda h: h.tensor_copy(out=qnT.ap[:, hh, 0:N], in_=PS(pi, 128, N)), R=(ps[pi],), W=(qnT,))
        slot, w = wload("UQ", 1)
        wsw = wswap(slot, w, 4, 512)
        for pr in range(4):
            pa = nextps()
            for kc in range(4):
                P.pe(lambda h, kc=kc: h.matmul(PS(pa, 128, N), lhsT=w[:, kc, pr * 128:(pr + 1) * 128], rhs=cqTb.ap[:, kc, 0:N],
                                               start=(kc == 0), stop=(kc == 3)), R=(slot, cqTb), W=(ps[pa],))
            pb = nextps()
            for kc in range(4):
                P.pe(lambda h, kc=kc: h.matmul(PS(pb, 128, N), lhsT=wsw[:, kc, pr * 128:(pr + 1) * 128],
                                               rhs=cqTb.ap[:, kc, 0:N], start=(kc == 0), stop=(kc == 3)),
                     R=(slot, cqTb), W=(ps[pb],))
            P.dve(lambda h: h.tensor_tensor(out=T[0].ap[:, 0:N], in0=PS(pa, 128, N), in1=cosb.ap[:, 0:N], op=ALU.mult),
                  R=(ps[pa], cosb), W=(T[0],))
            P.dve(lambda h: h.tensor_tensor(out=T[1].ap[:, 0:N], in0=PS(pb, 128, N), in1=sinb.ap[:, 0:N], op=ALU.mult),
                  R=(ps[pb], sinb), W=(T[1],))
            P.pool(lambda h: h.tensor_tensor(out=qrT.ap[0:64, 2 * pr, 0:N], in0=T[0].ap[0:64, 0:N], in1=T[1].ap[0:64, 0:N],
                                             op=ALU.add), R=(T[0], T[1]), W=(qrT,))
            P.pool(lambda h: h.tensor_tensor(out=qrT.ap[64:128, 2 * pr + 1, 0:N], in0=T[0].ap[64:128, 0:N],
                                             in1=T[1].ap[64:128, 0:N], op=ALU.add), R=(T[0], T[1]), W=(qrT,))
        P.release(m1)
        DBG.get('phase_hook', lambda n, p: None)('m1_attn', P)
        stageL = [P.alloc([128, 4096], F32) for _ in range(2)]
        PbL = [P.alloc([128, 4096], BF16) for _ in range(2)]
        KTh = P.alloc([128, 4096], BF16)
        VhL = [P.alloc([128, 32, 128], BF16) for _ in range(2)]
        PTs = [P.alloc([128, 4, 128], BF16) for _ in range(4)]
        o_tm = P.alloc([128, 4, 1024], BF16)
        sm2L = [P.alloc([128, 16], F32) for _ in range(2)]
        qrr = [0]
        nk = kpos0 + N
        nreg = (nk + 511) // 512
        kregs = tuple(kvo[i] for i in range(nreg))
        ptrr = [0]
        porr = [0]
        iters = [(hh, qb) for hh in range(8) for qb in range(nb)]

        def stage_a(it):
            hh, qb = iters[it]
            pr, ho = hh // 2, (hh % 2) * 64
            Vh = VhL[hh % 2]
            if qb == 0:
                P.load(KTh.ap[:, 0:nk], KT_d[hh, :, 0:nk], kvsem[0], R=kregs, W=(KTh,))
                nfull = nk // 128
                P.load(Vh.ap[:, 0:nfull, :], V_d[0:nfull * 128, hh * 128:(hh + 1) * 128].rearrange("(j p) c -> p j c", p=128),
                       kvsem[1], R=kregs, W=(Vh,))
                if nk % 128:
                    rem = nk % 128
                    P.load(Vh.ap[0:rem, nfull, :], V_d[nfull * 128:nk, hh * 128:(hh + 1) * 128], kvsem[1], R=kregs, W=(Vh,))
            bq = min(128, N - qb * 128)
            qs = slice(qb * 128, qb * 128 + bq)
            nvis = kpos0 + (qb + 1) * 128 if is_prompt else nk
            stage, Pb, sm2 = stageL[it % 2], PbL[it % 2], sm2L[it % 2]
            ng = (nvis + 511) // 512
            for g in range(ng):
                gs = min(512, nvis - g * 512)
                pi = nextps()
                P.pe(lambda h: h.matmul(PS(pi, bq, gs), lhsT=qnT.ap[:, hh, qs], rhs=KTh.ap[:, g * 512:g * 512 + gs],
                                        start=True, stop=False), R=(qnT, KTh), W=(ps[pi],))
                P.pe(lambda h: h.matmul(PS(pi, bq, gs), lhsT=qrT.ap[:, hh, qs],
                                        rhs=krT2.ap[:, g * 512:g * 512 + gs], start=False, stop=True),
                     R=(qrT, krT2), W=(ps[pi],))
                if g == ng - 1:
                    P.act(lambda h: h.activation(out=stage.ap[0:bq, g * 512:g * 512 + gs], in_=PS(pi, bq, gs), func=AF.Copy),
                          R=(ps[pi],), W=(stage,))
                    if is_prompt:
                        P.pool(lambda h: h.memset(stage.ap[0:64, nvis - 64:nvis], NEG), W=(stage,))
                else:
                    P.dve(lambda h: h.tensor_scalar(out=stage.ap[0:bq, g * 512:g * 512 + gs], in0=PS(pi, bq, gs),
                                                    scalar1=1.0, scalar2=None, op0=ALU.mult, op1=ALU.max,
                                                    accum_out=sm2.ap[0:bq, 4 + g:5 + g]), R=(ps[pi],), W=(stage, sm2))

        def stage_a2(it):
            hh, qb = iters[it]
            bq = min(128, N - qb * 128)
            nvis = kpos0 + (qb + 1) * 128 if is_prompt else nk
            stage, Pb, sm2 = stageL[it % 2], PbL[it % 2], sm2L[it % 2]
            ng = (nvis + 511) // 512
            g = ng - 1
            gs = min(512, nvis - g * 512)
            P.dve(lambda h: h.reduce_max(out=sm2.ap[0:bq, 4 + g:5 + g], in_=stage.ap[0:bq, g * 512:g * 512 + gs],
                                         axis=AX.X), R=(stage,), W=(sm2,))
            P.dve(lambda h: h.reduce_max(out=sm2.ap[0:bq, 0:1], in_=sm2.ap[0:bq, 4:4 + ng], axis=AX.X), R=(sm2,), W=(sm2,))
            P.dve(lambda h: h.tensor_scalar(out=sm2.ap[0:bq, 1:2], in0=sm2.ap[0:bq, 0:1], scalar1=-MLA_SCALE, scalar2=None,
                                            op0=ALU.mult), R=(sm2,), W=(sm2,))
            P.act(lambda h: h.activation(out=Pb.ap[0:bq, 0:nvis], in_=stage.ap[0:bq, 0:nvis], func=AF.Exp, scale=MLA_SCALE,
                                         bias=sm2.ap[0:bq, 1:2], accum_out=sm2.ap[0:bq, 2:3]), R=(stage, sm2), W=(Pb, sm2))
            P.dve(lambda h: h.reciprocal(out=sm2.ap[0:bq, 3:4], in_=sm2.ap[0:bq, 2:3]), R=(sm2,), W=(sm2,))

        def stage_b(it):
            hh, qb = iters[it]
            Vh = VhL[hh % 2]
            bq = min(128, N - qb * 128)
            nvis = kpos0 + (qb + 1) * 128 if is_prompt else nk
            Pb, sm2 = PbL[it % 2], sm2L[it % 2]
            nkb = (nvis + 127) // 128
            po = 4
            groups = [(k0, min(nkb, k0 + 4)) for k0 in range(0, nkb, 4)]
            slots = []

            def emit_T(gi):
                k0, k1 = groups[gi]
                pt = 5 + ptrr[0] % 3
                pts = PTs[ptrr[0] % len(PTs)]
                ptrr[0] += 1
                for kb in range(k0, k1):
                    kbs = min(128, nvis - kb * 128)
                    j = kb - k0
                    P.pe(lambda h: h.transpose(out=PSB(pt)[0:kbs, j * 128:j * 128 + bq], in_=Pb.ap[0:bq, kb * 128:kb * 128 + kbs],
                                               identity=identb.ap[0:bq, 0:bq]), R=(Pb, identb), W=(ps[pt],))
                nj = k1 - k0
                src = PSB(pt)[:, 0:nj * 128].rearrange("p (j c) -> p j c", j=nj)[:, :, 0:bq]
                if gi % 2 == 0:
                    P.dve(lambda h: h.tensor_copy(out=pts.ap[:, 0:nj, 0:bq], in_=src), R=(ps[pt],), W=(pts,))
                else:
                    P.act(lambda h: h.activation(out=pts.ap[:, 0:nj, 0:bq], in_=src, func=AF.Copy), R=(ps[pt],), W=(pts,))
                slots.append(pts)

            def emit_PV(gi):
                k0, k1 = groups[gi]
                pts = slots[gi]
                for kb in range(k0, k1):
                    kbs = min(128, nvis - kb * 128)
                    j = kb - k0
                    P.pe(lambda h: h.matmul(PS(po, bq, 128), lhsT=pts.ap[0:kbs, j, 0:bq], rhs=Vh.ap[0:kbs, kb, :],
                                            start=(kb == 0), stop=(kb == nkb - 1)), R=(pts, Vh), W=(ps[po],))

            LA = 2
            for gi in range(min(LA, len(groups))):
                emit_T(gi)
            for gi in range(len(groups)):
                if gi + LA < len(groups):
                    emit_T(gi + LA)
                emit_PV(gi)
            P.act(lambda h: h.activation(out=o_tm.ap[0:bq, qb, hh * 128:(hh + 1) * 128], in_=PS(po, bq, 128), func=AF.Identity,
                                         scale=sm2.ap[0:bq, 3:4]), R=(ps[po], sm2), W=(o_tm,))

        stage_a(0)
        stage_a2(0)
        for it in range(len(iters)):
            if it + 1 < len(iters):
                stage_a(it + 1)
            stage_b(it)
            if it + 1 < len(iters):
                stage_a2(it + 1)
        DBG.get('phase_hook', lambda n, p: None)('m1_oT', P)
        for qb in range(nb):
            bq = min(128, N - qb * 128)
            for half in range(2):
                pt = 5 + ptrr[0] % 3
                ptrr[0] += 1
                for j in range(4):
                    m = half * 4 + j
                    P.pe(lambda h, j=j, m=m: h.transpose(out=PSB(pt)[:, j * 128:j * 128 + bq],
                                                         in_=o_tm.ap[0:bq, qb, m * 128:(m + 1) * 128],
                                                         identity=identb.ap[0:bq, 0:bq]), R=(o_tm, identb), W=(ps[pt],))
                src = PSB(pt)[:, 0:512].rearrange("p (j c) -> p j c", j=4)[:, :, 0:bq]
                if half == 0:
                    P.act(lambda h, src=src: h.activation(out=oT.ap[:, 0:4, qb * 128:qb * 128 + bq], in_=src, func=AF.Copy),
                          R=(ps[pt],), W=(oT,))
                else:
                    P.dve(lambda h, src=src: h.tensor_copy(out=oT.ap[:, 4:8, qb * 128:qb * 128 + bq], in_=src),
                          R=(ps[pt],), W=(oT,))
        for g in range(2):
            slot, w = wload("OC", g)
            for j in range(4):
                m = g * 4 + j
                pi = nextps()
                for kc in range(8):
                    P.pe(lambda h, kc=kc: h.matmul(PS(pi, 128, N), lhsT=w[:, kc, j * 128:(j + 1) * 128], rhs=oT.ap[:, kc, 0:N],
                                                   start=(kc == 0), stop=(kc == 7)), R=(slot, oT), W=(ps[pi],))
                resid_chunk(m, pi, N)
        residual_ln(N, 1, 0)
        P.release(m0)

    for s in range(n_pseq):
        if "mix0" in stages:
            state_zero()
        for t in range(n_ptiles):
            N = 512
            _ph = DBG.get("phase_hook", lambda n, p: None)
            _ph("load", P)
            if (t, "ld") not in DBG.get("skip", ()):
                load_x(I["xp"][s, t * N:(t + 1) * N, :], N)
            _ph("mix0", P)
            if "mix0" in stages:
                mix0(N, t * N)
                if t == n_ptiles - 1:
                    state_store("pC", "pn", "pm", "pS", s)
            _ph("ffn0", P)
            if "ffn0" in stages:
                ffn(N, 0)
            _ph("mix1", P)
            if "mix1" in stages:
                mix1(N, t * N, t * N, True, "pckv", "pkr", s, t * N)
            _ph("ffn1", P)
            if "ffn1" in stages:
                ffn(N, 1)
            _ph("store", P)
            if (t, "st") not in DBG.get("skip", ()):
                store_y(O["yp"][s, t * N:(t + 1) * N, :], N)
    DBG.get("phase_hook", lambda n, p: None)("sample", P)
    for s in range(n_sseq):
        N = DEC_SEQ
        if DBG.get("s_load", True):
            load_x(I["xs"][s, :, :], N)
        if "mix0" in stages:
            state_load(s)
            mix0(N, SEQ)
            state_store("sC", "sn", "sm", "sS", s)
        if "ffn0" in stages:
            ffn(N, 0)
        if "mix1" in stages:
            kv_prefill(s)
            mix1(N, SEQ, PAST, False, "sckv", "skr", s, 0)
        if "ffn1" in stages:
            ffn(N, 1)
        if DBG.get("s_store", True):
            store_y(O["ys"][s, :, :], N)

    P.finish(stack)
    stack.close()
    return nc, P


def _consts():
    c = {}
    c["ident"] = np.eye(128, dtype=np.float32)
    half = 32
    inv = (10000.0 ** (-np.arange(half, dtype=np.float32) / half)).astype(np.float32)
    pos = np.concatenate([np.arange(SEQ), PAST + np.arange(DEC_SEQ)]).astype(np.float32)
    ang = (pos[:, None] * inv[None, :]).astype(np.float32)
    cos = np.cos(ang).astype(np.float32)
    sin = np.sin(ang).astype(np.float32)
    c["cosT"] = cos
    c["sinT"] = sin
    p = np.arange(128)
    j = (p % 64) % 32
    sign = np.where((p % 64) < 32, -1.0, 1.0).astype(np.float32)
    c["cosF"] = np.ascontiguousarray(cos[:, j].T)
    c["sinF"] = np.ascontiguousarray((sin[:, j] * sign[None, :]).T)
    s_idx = (p % 64)[:, None]
    t_idx = np.arange(64)[None, :]
    c["mask_ml"] = (s_idx <= t_idx).astype(np.float32)
    gam = (1.0 - 2.0 ** (-5.0 - np.arange(4, dtype=np.float64)))
    mret = np.zeros((128, 4, 64), np.float32)
    for h in range(4):
        rel = t_idx - s_idx
        mret[:, h, :] = np.where(rel >= 0, gam[h] ** np.maximum(rel, 0), 0.0) * (64.0 ** -0.5)
    c["mret"] = mret
    kdec = np.zeros((128, 2, 128), np.float32)
    for pr in range(2):
        for jj in range(128):
            h = 2 * pr + jj // 64
            kdec[:, pr, jj] = gam[h] ** (63 - (p % 64)) * (64.0 ** -0.5)
    c["kdec"] = kdec
    qdec = np.zeros((128, 2, 64), np.float32)
    for pr in range(2):
        for pp in range(128):
            h = 2 * pr + pp // 64
            qdec[pp, pr, :] = gam[h] ** (np.arange(64) + 1.0)
    c["qdec"] = qdec
    sel = np.zeros((4, 4, 128), np.float32)
    for h in range(4):
        sel[h, h, :] = 1.0
    c["sel"] = sel
    dm = np.zeros((128, 128), np.float32)
    dm[0:64, 64:128] = NEG
    c["dmask"] = dm
    return c


_CACHE = {}


def kernel(**inp):
    f = lambda a: np.ascontiguousarray(np.asarray(a, dtype=np.float32))
    key = "full"
    if key not in _CACHE:
        _CACHE[key] = build()
    nc, _ = _CACHE[key]
    cst = _consts()
    ln = np.concatenate([f(inp["ln_mix_g"]), f(inp["ln_mix_b"]), f(inp["ln_ffn_g"]), f(inp["ln_ffn_b"])], axis=0)
    shared = {
        "w_in_a": f(inp["w_in_a"]), "b_if": f(inp["b_if_a"]).reshape(8), "g_ml": f(inp["g_ml"]).reshape(512),
        "g_ret": f(inp["g_ret"]).reshape(512), "w_out_a": f(inp["w_out_a"]), "w_in_c": f(inp["w_in_c"]),
        "g_q": f(inp["g_q"]).reshape(512), "g_kv": f(inp["g_kv"]).reshape(256), "w_uq": f(inp["w_uq"]),
        "w_ukv": f(inp["w_ukv"]), "w_out_c": f(inp["w_out_c"]), "ln": ln,
        "w_gu": f(inp["w_gu"]), "w_down": f(inp["w_down"]),
    }
    shared.update(cst)
    in_maps = []
    for c in range(NCORES):
        sl = slice(2 * c, 2 * c + 2)
        m = dict(shared)
        m["xp"] = f(inp["x_prompt"][sl])
        m["xs"] = f(inp["x_sample"][sl])
        m["stC"] = f(inp["state_mlstm_C"][0, sl])
        m["stn"] = f(inp["state_mlstm_n"][0, sl])
        m["stm"] = f(inp["state_mlstm_m"][0, sl])
        m["stS"] = f(inp["state_ret_S"][0, sl])
        m["cckv"] = f(inp["cache_ckv"][0, sl])
        m["ckr"] = f(inp["cache_krope"][0, sl])
        in_maps.append(m)
    res = run_bass_kernel_spmd(nc, in_maps, core_ids=list(range(NCORES)))
    R = res.results
    cat = lambda k: np.concatenate([np.asarray(r[k], dtype=np.float32) for r in R], axis=0)
    outs = (cat("yp"), cat("ys"),
            cat("pC")[None], cat("pn")[None], cat("pm")[None], cat("pS")[None], cat("pckv")[None], cat("pkr")[None],
            cat("sC")[None], cat("sn")[None], cat("sm")[None], cat("sS")[None], cat("sckv")[None], cat("skr")[None])
    return outs
```
